# Optimizing a Trainium2 kernel written in Bass

```python
import math
import jax, jax.numpy as jnp
from jax import lax
import numpy as np

D_MODEL = 2048
BATCH = 1
SEQ = 8192
DEPTH = 4

GRID_W = 64
CTX_LEN = 256
EPS = 1e-6
ROPE_BASE = 10000.0

D_A = 512
N_BLK_A = 8
BLK_A = D_A // N_BLK_A
CONV_A = 4
LRU_C = 8.0

N_HEADS_B = 4
HEAD_DIM_B = 128
D_B = N_HEADS_B * HEAD_DIM_B
CONV_B = 4
CHUNK_B = 64

N_Q_HEADS_C = 8
N_KV_HEADS_C = 2
HEAD_DIM_C = 128
D_Q_C = N_Q_HEADS_C * HEAD_DIM_C
D_KV_C = N_KV_HEADS_C * HEAD_DIM_C
Q_BLOCK = 128

D_MIX = D_A + D_B + D_Q_C
IN_SIZES = (D_A, D_A, 3 * D_B, D_B, 2 * N_HEADS_B, 2 * N_HEADS_B, D_Q_C, D_KV_C, D_KV_C, D_Q_C)
D_IN = sum(IN_SIZES)

kernel_name = 'hybrid_rglru_gdn_gqa_prefix_trunk'


def rms_norm(x, g):
    xf = x.astype(jnp.float32)
    y = xf * lax.rsqrt(jnp.mean(xf * xf, axis=-1, keepdims=True) + EPS)
    return (y * g.astype(jnp.float32)).astype(x.dtype)


def l2_normalize(x):
    return x * lax.rsqrt(jnp.sum(x * x, axis=-1, keepdims=True) + EPS)


def flip_time(z, axis, rev):
    return jnp.flip(z, axis=axis) if rev else z


def dwconv_centred(x, w):
    k = w.shape[0]
    return lax.conv_general_dilated(
        x, w.astype(x.dtype)[:, None, :], window_strides=(1,),
        padding=[(k // 2, k - 1 - k // 2)],
        dimension_numbers=('NWC', 'WIO', 'NWC'), feature_group_count=x.shape[-1])


def split_cols(z):
    idx = np.cumsum(IN_SIZES)[:-1].tolist()
    return jnp.split(z, idx, axis=-1)


def axial_rope(row, col):
    n_freq = HEAD_DIM_C // 4
    inv_freq = ROPE_BASE ** (-jnp.arange(n_freq, dtype=jnp.float32) / n_freq)
    ang = jnp.concatenate([row.astype(jnp.float32)[:, None] * inv_freq,
                           col.astype(jnp.float32)[:, None] * inv_freq], axis=-1)
    return jnp.cos(ang), jnp.sin(ang)


def apply_rope(x, cos, sin):
    xf = x.astype(jnp.float32)
    half = xf.shape[-1] // 2
    x1, x2 = xf[..., :half], xf[..., half:]
    cs, sn = cos[None, :, None, :], sin[None, :, None, :]
    return jnp.concatenate([x1 * cs - x2 * sn, x2 * cs + x1 * sn], axis=-1).astype(x.dtype)


def _linrec_combine(left, right):
    a_l, b_l = left
    a_r, b_r = right
    return a_l * a_r, a_r * b_l + b_r


def rglru_scan(u, h0, w_r, b_r, w_i, b_i, lam):
    bsz, t_len, _ = u.shape
    ub = u.reshape(bsz, t_len, N_BLK_A, BLK_A)
    gate_r = jax.nn.sigmoid(jnp.einsum('btni,nij->btnj', ub, w_r.astype(jnp.float32)).reshape(bsz, t_len, D_A) + b_r)
    gate_i = jax.nn.sigmoid(jnp.einsum('btni,nij->btnj', ub, w_i.astype(jnp.float32)).reshape(bsz, t_len, D_A) + b_i)
    log_a = -LRU_C * gate_r * jax.nn.softplus(-lam.astype(jnp.float32))
    a = jnp.exp(log_a)
    b = jnp.sqrt(-jnp.expm1(2.0 * log_a)) * (gate_i * u)
    a_cum, h = lax.associative_scan(_linrec_combine, (a, b), axis=1)
    h = h + a_cum * h0[:, None, :]
    return h, h[:, -1]


def rglru_branch(xa_lat, xa_ctx, conv_w, conv_b, w_r, b_r, w_i, b_i, lam):
    u_lat = (dwconv_centred(xa_lat, conv_w) + conv_b).astype(jnp.float32)
    u_ctx = (dwconv_centred(xa_ctx, conv_w) + conv_b).astype(jnp.float32)
    h0 = jnp.zeros((u_ctx.shape[0], D_A), jnp.float32)
    ys_lat, ys_ctx = [], []
    for d in range(2):
        rev = d == 1
        y_c, h_c = rglru_scan(flip_time(u_ctx, 1, rev), h0, w_r[d], b_r[d], w_i[d], b_i[d], lam[d])
        y_l, _ = rglru_scan(flip_time(u_lat, 1, rev), h_c, w_r[d], b_r[d], w_i[d], b_i[d], lam[d])
        ys_ctx.append(flip_time(y_c, 1, rev))
        ys_lat.append(flip_time(y_l, 1, rev))
    return ys_lat[0] + ys_lat[1], ys_ctx[0] + ys_ctx[1]


def gdn_chunked(q, k, v, g, beta, s0):
    bsz, nh, t_len, dk = q.shape
    dv = v.shape[-1]
    cs = CHUNK_B
    n = t_len // cs
    q = (q * dk ** -0.5).reshape(bsz, nh, n, cs, dk)
    k = k.reshape(bsz, nh, n, cs, dk)
    v = v.reshape(bsz, nh, n, cs, dv)
    beta = beta.reshape(bsz, nh, n, cs)
    g = jnp.cumsum(g.reshape(bsz, nh, n, cs), axis=-1)
    causal = jnp.tril(jnp.ones((cs, cs), bool))
    decay = jnp.exp(jnp.where(causal, g[..., :, None] - g[..., None, :], -jnp.inf))
    kb = k * beta[..., None]
    strict = jnp.tril(jnp.einsum('bhnid,bhnjd->bhnij', kb, k) * decay, -1)
    eye = jnp.eye(cs, dtype=jnp.float32)
    t_mat = lax.linalg.triangular_solve(eye + strict, jnp.broadcast_to(eye, strict.shape),
                                        left_side=True, lower=True, unit_diagonal=True)
    u = t_mat @ (v * beta[..., None])
    w = t_mat @ (kb * jnp.exp(g)[..., None])
    qk = jnp.einsum('bhnid,bhnjd->bhnij', q, k) * decay
    g_last = g[..., -1]
    k_tail = k * jnp.exp(g_last[..., None] - g)[..., None]
    q_head = q * jnp.exp(g)[..., None]

    def step(s, inp):
        q_i, k_i, u_i, w_i, qk_i, gl_i = inp
        v_new = u_i - w_i @ s
        o = q_i @ s + qk_i @ v_new
        s = s * jnp.exp(gl_i)[..., None, None] + jnp.swapaxes(k_i, -1, -2) @ v_new
        return s, o

    xs = tuple(jnp.moveaxis(z, 2, 0) for z in (q_head, k_tail, u, w, qk, g_last))
    s_last, o = lax.scan(step, s0, xs)
    o = jnp.moveaxis(o, 0, 2).reshape(bsz, nh, t_len, dv)
    return o, s_last


def _gdn_prep(qkv, conv_w):
    bsz, t_len, _ = qkv.shape
    u = jax.nn.silu(dwconv_centred(qkv, conv_w).astype(jnp.float32))
    q, k, v = jnp.split(u, 3, axis=-1)
    heads = lambda z: z.reshape(bsz, t_len, N_HEADS_B, HEAD_DIM_B).transpose(0, 2, 1, 3)
    return l2_normalize(heads(q)), l2_normalize(heads(k)), heads(v)


def _gdn_gates(beta_raw, alpha_raw, a_log, dt_bias, d):
    sl = slice(d * N_HEADS_B, (d + 1) * N_HEADS_B)
    beta = jax.nn.sigmoid(beta_raw[..., sl].astype(jnp.float32)).transpose(0, 2, 1)
    g = -jnp.exp(a_log[d].astype(jnp.float32)) * jax.nn.softplus(alpha_raw[..., sl].astype(jnp.float32) + dt_bias[d])
    return beta, g.transpose(0, 2, 1)


def gdn_branch(qkv_lat, qkv_ctx, beta_lat, beta_ctx, alpha_lat, alpha_ctx, conv_w, a_log, dt_bias, onorm):
    ql, kl, vl = _gdn_prep(qkv_lat, conv_w)
    qx, kx, vx = _gdn_prep(qkv_ctx, conv_w)
    bsz = ql.shape[0]
    s0 = jnp.zeros((bsz, N_HEADS_B, HEAD_DIM_B, HEAD_DIM_B), jnp.float32)
    os_lat, os_ctx = [], []
    for d in range(2):
        rev = d == 1
        b_l, g_l = _gdn_gates(beta_lat, alpha_lat, a_log, dt_bias, d)
        b_x, g_x = _gdn_gates(beta_ctx, alpha_ctx, a_log, dt_bias, d)
        fl = lambda z: flip_time(z, 2, rev)
        o_x, s_x = gdn_chunked(fl(qx), fl(kx), fl(vx), fl(g_x), fl(b_x), s0)
        o_l, _ = gdn_chunked(fl(ql), fl(kl), fl(vl), fl(g_l), fl(b_l), s_x)
        os_ctx.append(fl(o_x))
        os_lat.append(fl(o_l))

    def finish(o):
        o = rms_norm(o.transpose(0, 2, 1, 3), onorm)
        return o.reshape(o.shape[0], o.shape[1], D_B)

    return finish(os_lat[0] + os_lat[1]), finish(os_ctx[0] + os_ctx[1])


def _attend(q, k, v):
    s = jnp.einsum('bqkgd,bskd->bkgqs', q, k).astype(jnp.float32) * (HEAD_DIM_C ** -0.5)
    p = jax.nn.softmax(s, axis=-1).astype(v.dtype)
    return jnp.einsum('bkgqs,bskd->bqkgd', p, v)


def gqa_branch(q_lat, k_lat, v_lat, q_ctx, k_ctx, v_ctx, qn, kn, cos, sin, need_ctx_out):
    bsz, t_len, _ = q_lat.shape
    grp = N_Q_HEADS_C // N_KV_HEADS_C

    def heads(q, k, v):
        n_tok = q.shape[1]
        q = rms_norm(q.reshape(bsz, n_tok, N_KV_HEADS_C, grp, HEAD_DIM_C), qn)
        k = rms_norm(k.reshape(bsz, n_tok, N_KV_HEADS_C, HEAD_DIM_C), kn)
        return q, k, v.reshape(bsz, n_tok, N_KV_HEADS_C, HEAD_DIM_C)

    ql, kl, vl = heads(q_lat, k_lat, v_lat)
    qx, kx, vx = heads(q_ctx, k_ctx, v_ctx)
    ql = apply_rope(ql.reshape(bsz, t_len, N_Q_HEADS_C, HEAD_DIM_C), cos, sin).reshape(ql.shape)
    kl = apply_rope(kl, cos, sin)
    keys = jnp.concatenate([kl, kx], axis=1)
    vals = jnp.concatenate([vl, vx], axis=1)
    n_blk = t_len // Q_BLOCK
    qb = ql.reshape(bsz, n_blk, Q_BLOCK, N_KV_HEADS_C, grp, HEAD_DIM_C).transpose(1, 0, 2, 3, 4, 5)
    ob = lax.map(lambda qi: _attend(qi, keys, vals), qb)
    o_lat = ob.transpose(1, 0, 2, 3, 4, 5).reshape(bsz, t_len, D_Q_C)
    o_ctx = None
    if need_ctx_out:
        o_ctx = _attend(qx, kx, vx).reshape(bsz, qx.shape[1], D_Q_C)
    return o_lat, o_ctx


def hybrid_layer(h_lat, h_ctx, c, c_ctx, norm_g, w_mod, b_mod, w_in, conv_a_w, conv_a_b, w_ra, b_ra,
                 w_ia, b_ia, lam_a, conv_b_w, a_log_b, dt_bias_b, onorm_b, qn_c, kn_c, w_out, cos, sin, last):
    shift, scale, gate = jnp.split(jax.nn.silu(c) @ w_mod + b_mod, 3, axis=-1)
    shift_x, scale_x, gate_x = jnp.split(jax.nn.silu(c_ctx) @ w_mod + b_mod, 3, axis=-1)
    z_lat = (rms_norm(h_lat, norm_g) * (1.0 + scale[:, None]) + shift[:, None]) @ w_in
    z_ctx = (rms_norm(h_ctx, norm_g) * (1.0 + scale_x) + shift_x) @ w_in
    xa_l, ga_l, qkvb_l, gb_l, beta_l, alpha_l, qc_l, kc_l, vc_l, gc_l = split_cols(z_lat)
    xa_x, ga_x, qkvb_x, gb_x, beta_x, alpha_x, qc_x, kc_x, vc_x, gc_x = split_cols(z_ctx)

    ya_l, ya_x = rglru_branch(xa_l, xa_x, conv_a_w, conv_a_b, w_ra, b_ra, w_ia, b_ia, lam_a)
    yb_l, yb_x = gdn_branch(qkvb_l, qkvb_x, beta_l, beta_x, alpha_l, alpha_x, conv_b_w, a_log_b, dt_bias_b, onorm_b)
    yc_l, yc_x = gqa_branch(qc_l, kc_l, vc_l, qc_x, kc_x, vc_x, qn_c, kn_c, cos, sin, not last)

    mix_lat = jnp.concatenate([ya_l * jax.nn.silu(ga_l), yb_l * jax.nn.silu(gb_l),
                               yc_l * jax.nn.silu(gc_l)], axis=-1) @ w_out
    h_lat = h_lat + (gate[:, None] * mix_lat).astype(h_lat.dtype)
    if not last:
        mix_ctx = jnp.concatenate([ya_x * jax.nn.silu(ga_x), yb_x * jax.nn.silu(gb_x),
                                   yc_x * jax.nn.silu(gc_x)], axis=-1) @ w_out
        h_ctx = h_ctx + (gate_x * mix_ctx).astype(h_ctx.dtype)
    return h_lat, h_ctx


def setup_inputs(seed: int = 0) -> dict:
    key = jax.random.key(seed)
    ks = jax.random.split(key, 24)
    f32 = jnp.float32
    L, D = DEPTH, D_MODEL

    def nrm(k, shape, s):
        return jax.random.normal(k, shape, f32) * s

    def gain(k, shape):
        return 1.0 + 0.02 * jax.random.normal(k, shape, f32)

    u = jax.random.uniform(ks[14], (L, 2, D_A), f32, 0.9, 0.999)
    s = u ** (1.0 / LRU_C)
    lam_a = jnp.log(s) - jnp.log1p(-s)
    a_log_b = jnp.log(jax.random.uniform(ks[16], (L, 2, N_HEADS_B), f32, 1.0, 16.0))
    dt = jnp.exp(jax.random.uniform(ks[17], (L, 2, N_HEADS_B), f32, math.log(1e-3), math.log(1e-1)))
    dt_bias_b = dt + jnp.log(-jnp.expm1(-dt))
    return {
        'x': nrm(ks[0], (BATCH, SEQ, D), 1.0),
        'c': nrm(ks[1], (BATCH, D), 1.0),
        'ctx': nrm(ks[2], (BATCH, CTX_LEN, D), 1.0),
        'c_ctx': nrm(ks[3], (D,), 1.0),
        'norm_g': gain(ks[4], (L, D)),
        'w_mod': nrm(ks[5], (L, D, 3 * D), 0.5 * D ** -0.5),
        'b_mod': nrm(ks[6], (L, 3 * D), 0.02),
        'w_in': nrm(ks[7], (L, D, D_IN), D ** -0.5),
        'conv_a_w': nrm(ks[8], (L, CONV_A, D_A), CONV_A ** -0.5),
        'conv_a_b': nrm(ks[9], (L, D_A), 0.02),
        'w_ra': nrm(ks[10], (L, 2, N_BLK_A, BLK_A, BLK_A), BLK_A ** -0.5),
        'b_ra': nrm(ks[11], (L, 2, D_A), 0.02),
        'w_ia': nrm(ks[12], (L, 2, N_BLK_A, BLK_A, BLK_A), BLK_A ** -0.5),
        'b_ia': nrm(ks[13], (L, 2, D_A), 0.02),
        'lam_a': lam_a,
        'conv_b_w': nrm(ks[15], (L, CONV_B, 3 * D_B), CONV_B ** -0.5),
        'a_log_b': a_log_b,
        'dt_bias_b': dt_bias_b,
        'onorm_b': gain(ks[18], (L, HEAD_DIM_B)),
        'qn_c': gain(ks[19], (L, HEAD_DIM_C)),
        'kn_c': gain(ks[20], (L, HEAD_DIM_C)),
        'w_out': nrm(ks[21], (L, D_MIX, D), D_MIX ** -0.5),
        'final_g': gain(ks[22], (D,)),
    }


def reference(x, c, ctx, c_ctx, norm_g, w_mod, b_mod, w_in, conv_a_w, conv_a_b, w_ra, b_ra, w_ia, b_ia,
              lam_a, conv_b_w, a_log_b, dt_bias_b, onorm_b, qn_c, kn_c, w_out, final_g):
    n_tok = x.shape[1]
    rows = n_tok // GRID_W
    row = jnp.repeat(jnp.arange(rows, dtype=jnp.int32), GRID_W)
    col = jnp.tile(jnp.arange(GRID_W, dtype=jnp.int32), rows)
    cos, sin = axial_rope(row, col)
    h_lat, h_ctx = x, ctx
    for l in range(DEPTH):
        h_lat, h_ctx = hybrid_layer(
            h_lat, h_ctx, c, c_ctx, norm_g[l], w_mod[l], b_mod[l], w_in[l], conv_a_w[l], conv_a_b[l],
            w_ra[l], b_ra[l], w_ia[l], b_ia[l], lam_a[l], conv_b_w[l], a_log_b[l], dt_bias_b[l],
            onorm_b[l], qn_c[l], kn_c[l], w_out[l], cos, sin, l == DEPTH - 1)
    return rms_norm(h_lat, final_g)
```

```python
import numpy as np
from contextlib import ExitStack
import concourse.bass as bass
import concourse.mybir as mybir
from concourse.bass_utils import run_bass_kernel_spmd

F32 = mybir.dt.float32
BF16 = mybir.dt.bfloat16
F32R = mybir.dt.float32r
I32 = mybir.dt.int32
ALU = mybir.AluOpType
AF = mybir.ActivationFunctionType
AX = mybir.AxisListType

D = 2048
T = 8192
LC = 256
NT = T + LC
DEPTH = 4
NCORE = 8
EPS = 1e-6
D_A, D_B, D_QC, D_KVC = 512, 512, 1024, 256
IN_SIZES = (D_A, D_A, 3 * D_B, D_B, 8, 8, D_QC, D_KVC, D_KVC, D_QC)
IN_OFF = np.concatenate([[0], np.cumsum(IN_SIZES)]).tolist()
D_IN = IN_OFF[-1]

ENGS = ("pe", "act", "dve", "pool", "sp")
_UQ = [0]


def _uq(n):
    _UQ[0] += 1
    return "%s_%d" % (n, _UQ[0])

NDMA = 24


class Prog:
    def __init__(self, nc, stack):
        self.nc = nc
        self.streams = {e: [] for e in ENGS}
        self.count = {e: 0 for e in ENGS}
        self.sem = {e: stack.enter_context(nc.semaphore("s_" + e)) for e in ENGS}
        self.dsem = [stack.enter_context(nc.semaphore("d_%d" % i)) for i in range(NDMA)]
        self.ndma = 0
        self.ccsem = stack.enter_context(nc.semaphore("s_cc"))
        self.ncc = 0
        self.needed = {e: set() for e in ENGS}
        self.seen = {e: {} for e in ENGS}
        self.lastw = {}
        self.readers = {}

    def _wait(self, eng, ev, war=False):
        kind, who, val = ev
        if kind == "eng":
            if who == eng and (eng in ("pe", "sp") or war):
                return
            key, sem = ("e", who), self.sem[who]
        elif kind == "cc":
            key, sem = ("c", 0), self.ccsem
        else:
            key, sem = ("d", who), self.dsem[who]
        if self.seen[eng].get(key, 0) >= val:
            return
        self.seen[eng][key] = val
        if kind == "eng":
            self.needed[who].add(val)
            self.streams[eng].append(("waite", who, val))
        else:
            self.streams[eng].append(("wait", sem, val))

    def _deps(self, eng, reads, writes):
        best = {}

        def add(ev, war):
            kind, who, val = ev
            if kind == "eng" and who == eng and (eng in ("pe", "sp") or war):
                return
            k = (kind, who)
            if best.get(k, 0) < val:
                best[k] = val

        for k in reads:
            ev = self.lastw.get(k)
            if ev is not None:
                add(ev, False)
        for k in writes:
            ev = self.lastw.get(k)
            if ev is not None:
                add(ev, False)
            for ev in self.readers.get(k, ()):
                add(ev, True)
        for (kind, who), val in best.items():
            self._wait(eng, (kind, who, val), war=False)

    def _record(self, ev, reads, writes):
        for k in reads:
            self.readers.setdefault(k, []).append(ev)
        for k in writes:
            self.lastw[k] = ev
            self.readers[k] = []

    def op(self, eng, fn, reads=(), writes=()):
        ex = [k for k in reads if isinstance(k, str) and k.startswith("PS")]
        if ex:
            reads = [k for k in reads if k not in ex]
            writes = list(writes) + ex
        self._deps(eng, reads, writes)
        self.count[eng] += 1
        ev = ("eng", eng, self.count[eng])
        self.streams[eng].append(("inste", fn, eng, self.count[eng]))
        self._record(ev, reads, writes)

    def dma(self, out, in_, reads=(), writes=(), q="sp", **kw):
        i = self.ndma
        self.ndma += 1
        s = i % NDMA
        tgt = 16 * (i // NDMA + 1)
        if i >= NDMA:
            self._wait(q, ("dma", s, tgt - 16))
        self._deps(q, reads, writes)
        self.streams[q].append(("inst", lambda e: e.dma_start(out=out, in_=in_, **kw), (self.dsem[s], 16)))
        self._record(("dma", s, tgt), reads, writes)

    def allgather(self, out, in_, reads=(), writes=()):
        if self.ncc > 0:
            self._wait("pool", ("cc", 0, self.ncc))
        self._deps("pool", reads, writes)
        self.ncc += 1
        self.streams["pool"].append(("inst", lambda e: e.collective_compute("AllGather", ALU.bypass, replica_groups=[list(range(NCORE))],
                                                                            ins=[in_], outs=[out]), (self.ccsem, 1)))
        self._record(("cc", 0, self.ncc), reads, writes)

    def barrier(self):
        for e in ENGS:
            for f in ENGS:
                if f != e and self.count[f] > 0:
                    self._wait(e, ("eng", f, self.count[f]))
            for s in range(NDMA):
                n = (self.ndma - 1 - s) // NDMA + 1 if self.ndma > s else 0
                if n > 0:
                    self._wait(e, ("dma", s, 16 * n))
            if self.ncc > 0:
                self._wait(e, ("cc", 0, self.ncc))
        self.lastw.clear()
        self.readers.clear()

    def finish(self):
        for s in range(NDMA):
            n = (self.ndma - 1 - s) // NDMA + 1 if self.ndma > s else 0
            if n > 0:
                self._wait("sp", ("dma", s, 16 * n))
        for f in ENGS:
            if f != "sp" and self.count[f] > 0:
                self._wait("sp", ("eng", f, self.count[f]))

    def emit(self):
        nc = self.nc
        rank = {e: {v: i + 1 for i, v in enumerate(sorted(self.needed[e]))} for e in ENGS}
        self.maxsem = {e: len(rank[e]) for e in ENGS}
        with nc.Block() as block:
            for ename, deco in (("sp", block.sync), ("act", block.scalar), ("dve", block.vector),
                                ("pool", block.gpsimd), ("pe", block.tensor)):
                items = self.streams[ename]

                def body(e, items=items):
                    for it in items:
                        if it[0] == "wait":
                            e.wait_ge(it[1], it[2])
                        elif it[0] == "waite":
                            e.wait_ge(self.sem[it[1]], rank[it[1]][it[2]])
                        elif it[0] == "inste":
                            ins = it[1](e)
                            if it[3] in rank[it[2]]:
                                ins.then_inc(self.sem[it[2]], 1)
                        else:
                            ins = it[1](e)
                            ins.then_inc(it[2][0], it[2][1])

                deco(body)


def _ident(P, ident_f32, ident_bf, tmp_key="ident"):
    P.op("pool", lambda e: e.memset(ident_f32[:], 1.0), writes=[tmp_key])
    P.op("pool", lambda e: e.affine_select(out=ident_f32[:], in_=ident_f32[:], pattern=[[-1, 128]],
                                           compare_op=ALU.is_equal, fill=0.0, base=0, channel_multiplier=1),
         reads=[tmp_key], writes=[tmp_key])
    if ident_bf is not None:
        P.op("pool", lambda e: e.tensor_copy(out=ident_bf[:], in_=ident_f32[:]), reads=[tmp_key], writes=[tmp_key + "b"])


def build_mod():
    nc = bass.Bass("TRN2", target_bir_lowering=False)
    cc = nc.dram_tensor("cc", [2, D], F32, kind="ExternalInput").ap()
    wm = nc.dram_tensor("wm", [DEPTH, D, 768], F32, kind="ExternalInput").ap()
    bm = nc.dram_tensor("bm", [DEPTH, 768], F32, kind="ExternalInput").ap()
    mo = nc.dram_tensor("mo", [DEPTH, 2, 768], F32, kind="ExternalOutput").ap()
    with ExitStack() as st:
        P = Prog(nc, st)
        sb = lambda n, s, d: st.enter_context(nc.sbuf_tensor(_uq(n), s, d))
        craw = sb("craw", [128, 16, 2], F32)
        sc = sb("sc", [128, 16, 2], F32)
        wb = [sb("wb%d" % i, [128, 768], F32) for i in range(4)]
        bt = sb("bt", [2, DEPTH, 768], F32)
        ot = sb("ot", [2, DEPTH, 768], F32)
        ps = [st.enter_context(nc.psum_tensor("ps%d" % i, [128, 512], F32)) for i in range(2)]
        for r in range(2):
            P.dma(craw[:, :, r], cc[r, :].rearrange("(k p) -> p k", p=128), writes=["craw%d" % r],
                  allow_slow_non_contiguous=True)
            P.dma(bt[r:r + 1, :, :], bm[None, :, :], writes=["bt%d" % r])
        P.op("act", lambda e: e.activation(out=sc[:], in_=craw[:], func=AF.Silu), reads=["craw0", "craw1"], writes=["sc"])
        n = 0
        for l in range(DEPTH):
            for kc in range(16):
                w = wb[n % 4]
                P.dma(w[:], wm[l, kc * 128:(kc + 1) * 128, :], writes=["wb%d" % (n % 4)])
                for hf in range(2):
                    P.op("pe", lambda e, w=w, hf=hf, kc=kc: e.matmul(out=ps[hf][0:2, 0:384], lhsT=sc[:, kc, :],
                                                                   rhs=w[:, hf * 384:(hf + 1) * 384],
                                                                   start=(kc == 0), stop=(kc == 15)),
                         reads=["sc", "wb%d" % (n % 4)], writes=["PSm%d" % hf])
                n += 1
            for hf in range(2):
                P.op("dve", lambda e, l=l, hf=hf: e.tensor_tensor(out=ot[:, l, hf * 384:(hf + 1) * 384], in0=ps[hf][0:2, 0:384],
                                                                 in1=bt[:, l, hf * 384:(hf + 1) * 384], op=ALU.add),
                     reads=["PSm%d" % hf, "bt0", "bt1"], writes=["ot"])
        P.dma(mo.rearrange("l r c -> r l c"), ot[:], reads=["ot"], writes=["mo"])
        P.finish()
        P.emit()
    return nc


NTA = 1056
A_TILES = [(0, 32)] + [(32 + 128 * k, 128) for k in range(8)]


def build_A(l):
    first, last = (l == 0), (l == DEPTH)
    nc = bass.Bass("TRN2", target_bir_lowering=False)
    din = lambda n, s, d: nc.dram_tensor(n, s, d, kind="ExternalInput").ap()
    dout = lambda n, s, d: nc.dram_tensor(n, s, d, kind="ExternalOutput").ap()
    h_in = din("h", [NTA, D], F32)
    if not first:
        yaT = din("yaT", [512, NTA], BF16)
        obT = din("obT", [512, NTA], F32)
        gbT = din("gbT", [512, NTA], BF16)
        ycT = din("ycT", [1024, NTA], BF16)
        wout = din("wout", [D, D], F32)
        gvec = din("gvec", [2, D], F32)
        onorm = din("onorm", [128, 1], F32)
    if not last:
        ng = din("ng", [1, D], F32)
        ss = din("ss", [4, D], F32)
        hout = dout("hout", [NTA, D], F32)
        xnT = dout("xnT", [D, NTA], BF16)
    else:
        fg = din("fg", [1, D], F32)
        yfin = dout("yfin", [1024, D], F32)
    with ExitStack() as st:
        P = Prog(nc, st)
        sb = lambda n, s, d: st.enter_context(nc.sbuf_tensor(_uq(n), s, d))
        ident = sb("ident", [128, 128], F32)
        identb = sb("identb", [128, 128], BF16)
        ones = sb("ones", [128, 128], F32)
        psf = [st.enter_context(nc.psum_tensor("PSf%d" % i, [128, 512], F32)) for i in range(6)]
        pst = [st.enter_context(nc.psum_tensor("PSt%d" % i, [128, 1024], BF16)) for i in range(2)]
        _ident(P, ident, identb)
        P.op("pool", lambda e: e.memset(ones[:], 1.0), writes=["ones"])
        if not last:
            gv = [sb("gv%d" % r, [128, D], F32) for r in range(2)]
            sh = [sb("sh%d" % r, [128, D], F32) for r in range(2)]
            ngb = sb("ngb", [128, D], F32)
            P.dma(ngb[:], ng[0:1, :].partition_broadcast(128), writes=["ngb"])
            for r in range(2):
                P.dma(gv[r][:], ss[2 * r:2 * r + 1, :].partition_broadcast(128), writes=["gv%d" % r])
                P.dma(sh[r][:], ss[2 * r + 1:2 * r + 2, :].partition_broadcast(128), writes=["sh%d" % r])
                P.op("dve", lambda e, r=r: e.scalar_tensor_tensor(out=gv[r][:], in0=gv[r][:], scalar=1.0, in1=ngb[:],
                                                                  op0=ALU.add, op1=ALU.mult),
                     reads=["gv%d" % r, "ngb"], writes=["gv%d" % r])
        else:
            fgb = sb("fgb", [128, D], F32)
            P.dma(fgb[:], fg[0:1, :].partition_broadcast(128), writes=["fgb"])
        if not first:
            gt = [sb("gt%d" % r, [128, D], F32) for r in range(2)]
            for r in range(2):
                P.dma(gt[r][:], gvec[r:r + 1, :].partition_broadcast(128), writes=["gt%d" % r])
            on = sb("on", [128, 1], F32)
            P.dma(on[:], onorm[:, :], writes=["on"])
            mixT = sb("mixT", [128, 16, NTA], BF16)
            for k in range(4):
                P.dma(mixT[:, k, :], yaT[k * 128:(k + 1) * 128, :], writes=[("mix", k)])
            for k in range(8):
                P.dma(mixT[:, 8 + k, :], ycT[k * 128:(k + 1) * 128, :], writes=[("mix", 8 + k)])
            wbf = sb("wbf", [128, 16, D], BF16)
            st2 = ExitStack()
            sb2 = lambda n, s, d: st2.enter_context(nc.sbuf_tensor(_uq(n), s, d))
            obs = sb2("obs", [128, 4, NTA], F32)
            gbs = sb2("gbs", [128, 4, NTA], BF16)
            sq = sb2("sq", [128, 512], F32)
            rs = sb2("rs", [128, 512], F32)
            tt_ = sb2("tt", [128, 512], F32)
            for hb in range(4):
                P.dma(obs[:, hb, :], obT[hb * 128:(hb + 1) * 128, :], writes=[("obs", hb)])
                P.dma(gbs[:, hb, :], gbT[hb * 128:(hb + 1) * 128, :], writes=[("gbs", hb)])
            for hb in range(4):
                for (c0, cn) in ((0, 512), (512, 512), (1024, 32)):
                    ob = obs[:, hb, c0:c0 + cn]
                    P.op("dve", lambda e, ob=ob, cn=cn: e.tensor_tensor(out=sq[:, :cn], in0=ob, in1=ob, op=ALU.mult),
                         reads=[("obs", hb)], writes=["sq"])
                    P.op("pe", lambda e, cn=cn: e.matmul(out=psf[5][:, :cn], lhsT=ones[:], rhs=sq[:, :cn], start=True, stop=True),
                         reads=["sq", "ones"], writes=["PSf5"])
                    P.op("act", lambda e, cn=cn: e.activation(out=rs[:, :cn], in_=psf[5][:, :cn], func=AF.Sqrt, scale=1.0 / 128, bias=EPS),
                         reads=["PSf5"], writes=["rs"])
                    P.op("dve", lambda e, cn=cn: e.reciprocal(out=rs[:, :cn], in_=rs[:, :cn]), reads=["rs"], writes=["rs"])
                    P.op("dve", lambda e, ob=ob, cn=cn: e.scalar_tensor_tensor(out=tt_[:, :cn], in0=ob, scalar=on[:, 0:1], in1=rs[:, :cn],
                                                                              op0=ALU.mult, op1=ALU.mult),
                         reads=[("obs", hb), "rs", "on"], writes=["tt"])
                    P.op("dve", lambda e, hb=hb, c0=c0, cn=cn: e.tensor_tensor(out=mixT[:, 4 + hb, c0:c0 + cn], in0=tt_[:, :cn],
                                                                              in1=gbs[:, hb, c0:c0 + cn], op=ALU.mult),
                         reads=["tt", ("gbs", hb)], writes=[("mix", 4 + hb)])
            wst = [sb2("wst%d" % i, [128, D], F32) for i in range(2)]
            for kc in range(16):
                P.dma(wst[kc % 2][:], wout[kc * 128:(kc + 1) * 128, :], writes=["wst%d" % (kc % 2)])
                eng = "pool" if kc % 2 == 0 else "act"
                if eng == "pool":
                    P.op("pool", lambda e, kc=kc: e.tensor_copy(out=wbf[:, kc, :], in_=wst[kc % 2][:]),
                         reads=["wst%d" % (kc % 2)], writes=[("wbf", kc)])
                else:
                    P.op("act", lambda e, kc=kc: e.copy(out=wbf[:, kc, :], in_=wst[kc % 2][:]),
                         reads=["wst%d" % (kc % 2)], writes=[("wbf", kc)])
        if not first:
            P.barrier()
            st2.close()
        ht = [sb("ht%d" % i, [128, D], F32) for i in range(2)]
        tmp = sb("tmp", [128, 512], F32)
        ssum = sb("ssum", [128, 2], F32)
        xnf = sb("xnf", [128, D], F32)
        xnb = sb("xnb", [128, D], BF16)
        xT = [sb("xT%d" % i, [128, 16, 128], BF16) for i in range(2)]
        for ti, (r0, np_) in enumerate(A_TILES):
            r = 0 if ti == 0 else 1
            h = ht[ti % 2]
            hk = "ht%d" % (ti % 2)
            P.dma(h[:np_, :], h_in[r0:r0 + np_, :], writes=[hk])
            if not first:
                for ct in range(4):
                    pk = "PSf%d" % (ct % 4)
                    for kc in range(16):
                        P.op("pe", lambda e, ct=ct, kc=kc, r0=r0, np_=np_: e.matmul(
                            out=psf[ct % 4][:np_, :], lhsT=mixT[:, kc, r0:r0 + np_], rhs=wbf[:, kc, ct * 512:(ct + 1) * 512],
                            start=(kc == 0), stop=(kc == 15)),
                            reads=[("mix", kc), ("wbf", kc)], writes=[pk])
                    P.op("dve", lambda e, ct=ct, np_=np_, r=r: e.tensor_tensor(out=tmp[:np_, :], in0=psf[ct % 4][:np_, :],
                                                                              in1=gt[r][:np_, ct * 512:(ct + 1) * 512], op=ALU.mult),
                         reads=[pk, "gt%d" % r], writes=["tmp"])
                    P.op("dve", lambda e, ct=ct, np_=np_, h=h: e.tensor_tensor(out=h[:np_, ct * 512:(ct + 1) * 512],
                                                                              in0=h[:np_, ct * 512:(ct + 1) * 512], in1=tmp[:np_, :], op=ALU.add),
                         reads=["tmp", hk], writes=[hk])
            if not last:
                P.dma(hout[r0:r0 + np_, :], h[:np_, :], reads=[hk], writes=[("hout", ti)])
            elif ti == 0:
                continue
            P.op("act", lambda e, h=h, np_=np_: e.activation(out=xnf[:np_, :], in_=h[:np_, :], func=AF.Square,
                                                             accum_out=ssum[:np_, 0:1]),
                 reads=[hk], writes=["xnf", "ssum"])
            P.op("act", lambda e, np_=np_: e.activation(out=ssum[:np_, 1:2], in_=ssum[:np_, 0:1], func=AF.Sqrt, scale=1.0 / D, bias=EPS),
                 reads=["ssum"], writes=["ssum1"])
            P.op("dve", lambda e, np_=np_: e.reciprocal(out=ssum[:np_, 1:2], in_=ssum[:np_, 1:2]), reads=["ssum1"], writes=["ssum1"])
            if last:
                P.op("dve", lambda e, h=h, np_=np_: e.scalar_tensor_tensor(out=xnf[:np_, :], in0=h[:np_, :], scalar=ssum[:np_, 1:2],
                                                                          in1=fgb[:np_, :], op0=ALU.mult, op1=ALU.mult),
                     reads=[hk, "ssum1", "fgb"], writes=["xnf"])
                P.dma(yfin[r0 - 32:r0 - 32 + np_, :], xnf[:np_, :], reads=["xnf"], writes=[("yfin", ti)])
                continue
            P.op("dve", lambda e, h=h, np_=np_, r=r: e.scalar_tensor_tensor(out=xnf[:np_, :], in0=h[:np_, :], scalar=ssum[:np_, 1:2],
                                                                           in1=gv[r][:np_, :], op0=ALU.mult, op1=ALU.mult),
                 reads=[hk, "ssum1", "gv%d" % r], writes=["xnf"])
            P.op("pool", lambda e, np_=np_, r=r: e.tensor_tensor(out=xnb[:np_, :], in0=xnf[:np_, :], in1=sh[r][:np_, :], op=ALU.add),
                 reads=["xnf", "sh%d" % r], writes=["xnb"])
            x_t = xT[ti % 2]
            xk = "xT%d" % (ti % 2)
            for half in range(2):
                pk = "PSt%d" % half
                for k8 in range(8):
                    kc = half * 8 + k8
                    P.op("pe", lambda e, half=half, k8=k8, kc=kc, np_=np_: e.transpose(
                        out=pst[half][:, k8 * 128:k8 * 128 + np_], in_=xnb[:np_, kc * 128:(kc + 1) * 128], identity=identb[:np_, :np_]),
                        reads=["xnb", "identb"], writes=[pk])
                src = pst[half][:, :].rearrange("p (k t) -> p k t", t=128)[:, :, :np_]
                if half == 0:
                    P.op("act", lambda e, src=src, x_t=x_t, np_=np_: e.copy(out=x_t[:, 0:8, :np_], in_=src), reads=[pk], writes=[xk])
                else:
                    P.op("dve", lambda e, src=src, x_t=x_t, np_=np_: e.tensor_copy(out=x_t[:, 8:16, :np_], in_=src), reads=[pk], writes=[xk])
            P.dma(xnT[:, r0:r0 + np_].rearrange("(k p) t -> p k t", p=128), x_t[:, :, :np_], reads=[xk], writes=[("xnT", ti)])
        P.finish()
        P.emit()
    return nc


NZ = 1028
Z_TILES = [(0, 256)] + [(256 + 512 * k, 512) for k in range(16)]


def zcols(j):
    o = IN_OFF
    hb, hf, kv = j // 2, j % 2, j // 4
    r = lambda a, n: list(range(a, a + n))
    cols = (r(o[0] + 64 * j, 64) + r(o[1] + 64 * j, 64) + r(o[2] + 128 * hb, 128) + r(o[2] + 512 + 128 * hb, 128)
            + r(o[2] + 1024 + 128 * hb + 64 * hf, 64) + r(o[3] + 128 * hb + 64 * hf, 64)
            + r(o[6] + 128 * j, 128) + r(o[7] + 128 * kv, 128) + r(o[9] + 128 * j, 128)
            + r(o[8] + 128 * kv, 128) + [o[4] + hb, o[4] + 4 + hb, o[5] + hb, o[5] + 4 + hb])
    assert len(cols) == NZ
    return cols


def build_Z():
    nc = bass.Bass("TRN2", target_bir_lowering=False)
    xnT = nc.dram_tensor("xnT", [D, NT], BF16, kind="ExternalInput").ap()
    wz = nc.dram_tensor("wz", [D, NZ], F32, kind="ExternalInput").ap()
    zT = nc.dram_tensor("zT", [896, NT], BF16, kind="ExternalOutput").ap()
    zv = nc.dram_tensor("zv", [NT, 128], BF16, kind="ExternalOutput").ap()
    zba = nc.dram_tensor("zba", [NT, 4], F32, kind="ExternalOutput").ap()
    with ExitStack() as st:
        P = Prog(nc, st)
        emit_Z(nc, st, P, xnT, wz, zT, zv, zba)
        P.finish()
        P.emit()
    return nc


def emit_Z(nc, st0, P, xnT, wz, zT, zv, zba, PSB=None):
    with ExitStack() as st:
        sb = lambda n, s, d: st.enter_context(nc.sbuf_tensor(_uq(n), s, d))
        wbf = sb("z_wbf", [128, 16, NZ], BF16)
        wst = [sb("z_wst%d" % i, [128, NZ], F32) for i in range(3)]
        xt = [sb("z_xt%d" % i, [128, 16, 512], BF16) for i in range(2)]
        ob = [sb("z_ob%d" % i, [128, 512], BF16) for i in range(4)]
        ov = [sb("z_ov%d" % i, [128, 128], BF16) for i in range(2)]
        oba = [sb("z_oba%d" % i, [128, 4], F32) for i in range(2)]
        psn = "PSz%d" if PSB is None else "PS%d"
        ps = [st.enter_context(nc.psum_tensor("PSz%d" % i, [128, 512], F32)) for i in range(6)] if PSB is None else PSB
        for kc in range(16):
            P.dma(wst[kc % 3][:], wz[kc * 128:(kc + 1) * 128, :], writes=["z_wst%d" % (kc % 3)])
            if kc % 2 == 0:
                P.op("pool", lambda e, kc=kc: e.tensor_copy(out=wbf[:, kc, :], in_=wst[kc % 3][:]),
                     reads=["z_wst%d" % (kc % 3)], writes=[("z_wbf", kc)])
            else:
                P.op("act", lambda e, kc=kc: e.copy(out=wbf[:, kc, :], in_=wst[kc % 3][:]),
                     reads=["z_wst%d" % (kc % 3)], writes=[("z_wbf", kc)])
        n_ev = 0
        n_tm = 0
        for ti, (t0, tn) in enumerate(Z_TILES):
            x = xt[ti % 2]
            xk = "z_xt%d" % (ti % 2)
            P.dma(x[:, :, :tn], xnT[:, t0:t0 + tn].rearrange("(k p) t -> p k t", p=128), writes=[xk])
            for g in range(7):
                pk = psn % (g % 4)
                for kc in range(16):
                    P.op("pe", lambda e, g=g, kc=kc, x=x, tn=tn: e.matmul(out=ps[g % 4][:, :tn], lhsT=wbf[:, kc, g * 128:(g + 1) * 128],
                                                                       rhs=x[:, kc, :tn], start=(kc == 0), stop=(kc == 15)),
                         reads=[("z_wbf", kc), xk], writes=[pk])
                o = ob[n_ev % 4]
                ok = "z_ob%d" % (n_ev % 4)
                if n_ev % 2 == 0:
                    P.op("act", lambda e, g=g, o=o, tn=tn: e.copy(out=o[:, :tn], in_=ps[g % 4][:, :tn]), reads=[pk], writes=[ok])
                else:
                    P.op("dve", lambda e, g=g, o=o, tn=tn: e.tensor_copy(out=o[:, :tn], in_=ps[g % 4][:, :tn]), reads=[pk], writes=[ok])
                P.dma(zT[g * 128:(g + 1) * 128, t0:t0 + tn], o[:, :tn], reads=[ok], writes=[("zT", g, ti)])
                n_ev += 1
            for sub in range(tn // 128):
                pk = psn % (4 + n_tm % 2)
                p_ = ps[4 + n_tm % 2]
                for kc in range(16):
                    P.op("pe", lambda e, kc=kc, x=x, sub=sub, p_=p_: e.matmul(out=p_[:, :132], lhsT=x[:, kc, sub * 128:(sub + 1) * 128],
                                                                           rhs=wbf[:, kc, 896:1028], start=(kc == 0), stop=(kc == 15)),
                         reads=[("z_wbf", kc), xk], writes=[pk])
                o_v, o_b = ov[n_tm % 2], oba[n_tm % 2]
                P.op("act", lambda e, o_v=o_v, p_=p_: e.copy(out=o_v[:], in_=p_[:, 0:128]), reads=[pk], writes=["z_ov%d" % (n_tm % 2)])
                P.op("dve", lambda e, o_b=o_b, p_=p_: e.tensor_copy(out=o_b[:], in_=p_[:, 128:132]), reads=[pk], writes=["z_oba%d" % (n_tm % 2)])
                r0 = t0 + sub * 128
                P.dma(zv[r0:r0 + 128, :], o_v[:], reads=["z_ov%d" % (n_tm % 2)], writes=[("zv", r0)])
                P.dma(zba[r0:r0 + 128, :], o_b[:], reads=["z_oba%d" % (n_tm % 2)], writes=[("zba", r0)])
                n_tm += 1
        P.barrier()


SEGS = [(0, LC), (LC, NT)]
PIECES = [(0, 256, 0)] + [(256 + 2048 * k, 2048, 1) for k in range(4)]
LN1E4_32 = float(np.log(10000.0) / 32.0)
TWO_PI = float(2 * np.pi)


def conv_piece(P, zrow, xin, xk, outs, ok, taps, t0, n, seg, npart, bias=None):
    s0, s1 = SEGS[seg]
    a, b = max(s0, t0 - 2), min(s1, t0 + n + 1)
    P.op("pool", lambda e: e.memset(xin[:npart, :n + 3], 0.0), writes=[xk])
    P.dma(xin[:npart, a - (t0 - 2):b - (t0 - 2)], zrow[:, a:b], reads=[], writes=[xk])
    for k in range(4):
        if k == 0:
            if bias is None:
                P.op("dve", lambda e: e.tensor_scalar(out=outs[:npart, :n], in0=xin[:npart, 0:n], scalar1=taps[:, 0:1], scalar2=None, op0=ALU.mult),
                     reads=[xk], writes=[ok])
            else:
                P.op("dve", lambda e: e.tensor_scalar(out=outs[:npart, :n], in0=xin[:npart, 0:n], scalar1=taps[:, 0:1], scalar2=bias, op0=ALU.mult, op1=ALU.add),
                     reads=[xk], writes=[ok])
        else:
            P.op("dve", lambda e, k=k: e.scalar_tensor_tensor(out=outs[:npart, :n], in0=xin[:npart, k:k + n], scalar=taps[:, k:k + 1],
                                                              in1=outs[:npart, :n], op0=ALU.mult, op1=ALU.add),
                 reads=[xk, ok], writes=[ok])


def emit_MA(nc, P, C, zT, pa, wri, yaT):
    with ExitStack() as st:
        sb = lambda n, s, d: st.enter_context(nc.sbuf_tensor(_uq(n), s, d))
        PS = C["PS"]
        prm = sb("a_prm", [64, 12], F32)
        wf = sb("a_wf", [64, 4, 64], F32)
        wb_ = sb("a_wb", [64, 4, 64], BF16)
        cneg = sb("a_cneg", [64, 2], F32)
        xin = sb("a_xin", [64, 2051], BF16)
        u = sb("a_u", [64, NT], F32)
        ub = sb("a_ub", [64, NT], BF16)
        ga = sb("a_ga", [64, NT], BF16)
        y = [sb("a_y%d" % d, [64, NT], F32) for d in range(2)]
        yo = sb("a_yo", [64, NT], BF16)
        tr, ti_, ta, tb = (sb("a_t%d" % i, [64, 512], F32) for i in range(4))
        P.dma(prm[:], pa[:, :], writes=["a_prm"])
        P.dma(wf[:], wri[:, :, :], writes=["a_wf"])
        P.dma(ga[:], zT[64:128, :], writes=["a_ga"])
        P.op("act", lambda e: e.copy(out=wb_[:], in_=wf[:]), reads=["a_wf"], writes=["a_wb"])
        P.op("act", lambda e: e.activation(out=cneg[:], in_=prm[:, 9:11], func=AF.Exp, scale=-1.0), reads=["a_prm"], writes=["a_cneg"])
        P.op("act", lambda e: e.activation(out=cneg[:], in_=cneg[:], func=AF.Ln, bias=1.0), reads=["a_cneg"], writes=["a_cneg"])
        P.op("dve", lambda e: e.tensor_scalar(out=cneg[:], in0=cneg[:], scalar1=-8.0, scalar2=None, op0=ALU.mult), reads=["a_cneg"], writes=["a_cneg"])
        P.op("act", lambda e: e.activation(out=ga[:], in_=ga[:], func=AF.Silu), reads=["a_ga"], writes=["a_ga"])
        for (t0, n, seg) in PIECES:
            conv_piece(P, zT[0:64, :], xin, "a_xin", u[:, t0:t0 + n], ("a_u", t0), prm, t0, n, seg, 64, bias=prm[:, 4:5])
            P.op("act", lambda e, t0=t0, n=n: e.copy(out=ub[:, t0:t0 + n], in_=u[:, t0:t0 + n]), reads=[("a_u", t0)], writes=[("a_ub", t0)])
        P.op("pool", lambda e: e.tensor_copy(out=tr[:, 0:1], in_=tr[:, 0:1]), reads=[("a_u", p[0]) for p in PIECES] + [("a_ub", p[0]) for p in PIECES], writes=["a_uall", "a_tr"])
        tiles = [(0, 256)] + [(256 + 512 * k, 512) for k in range(16)]
        for d in range(2):
            order = tiles if d == 0 else [tiles[0]] + tiles[:0:-1]
            prev = None
            for (t0, n) in order:
                P.op("pe", lambda e, t0=t0, n=n, d=d: e.matmul(out=PS[0][0:64, :n], lhsT=wb_[:, d, :], rhs=ub[:, t0:t0 + n], start=True, stop=True),
                     reads=["a_wb", "a_uall"], writes=["PS0"])
                P.op("pe", lambda e, t0=t0, n=n, d=d: e.matmul(out=PS[1][0:64, :n], lhsT=wb_[:, 2 + d, :], rhs=ub[:, t0:t0 + n], start=True, stop=True),
                     reads=["a_wb", "a_uall"], writes=["PS1"])
                P.op("act", lambda e, n=n, d=d: e.activation(out=tr[:, :n], in_=PS[0][0:64, :n], func=AF.Sigmoid, bias=prm[:, 5 + d:6 + d]),
                     reads=["PS0", "a_prm"], writes=["a_tr"])
                P.op("act", lambda e, n=n, d=d: e.activation(out=ti_[:, :n], in_=PS[1][0:64, :n], func=AF.Sigmoid, bias=prm[:, 7 + d:8 + d]),
                     reads=["PS1", "a_prm"], writes=["a_ti"])
                P.op("act", lambda e, n=n, d=d: e.activation(out=ta[:, :n], in_=tr[:, :n], func=AF.Exp, scale=cneg[:, d:d + 1]),
                     reads=["a_tr", "a_cneg"], writes=["a_ta"])
                P.op("dve", lambda e, n=n: e.tensor_tensor(out=tb[:, :n], in0=ta[:, :n], in1=ta[:, :n], op=ALU.mult), reads=["a_ta"], writes=["a_tb"])
                P.op("dve", lambda e, n=n: e.tensor_scalar(out=tb[:, :n], in0=tb[:, :n], scalar1=-1.0, scalar2=1.0, op0=ALU.mult, op1=ALU.add),
                     reads=["a_tb"], writes=["a_tb"])
                P.op("act", lambda e, n=n: e.activation(out=tb[:, :n], in_=tb[:, :n], func=AF.Sqrt), reads=["a_tb"], writes=["a_tb"])
                P.op("dve", lambda e, n=n, t0=t0: e.tensor_tensor(out=ti_[:, :n], in0=ti_[:, :n], in1=u[:, t0:t0 + n], op=ALU.mult),
                     reads=["a_ti", "a_uall"], writes=["a_ti"])
                P.op("dve", lambda e, n=n: e.tensor_tensor(out=tb[:, :n], in0=tb[:, :n], in1=ti_[:, :n], op=ALU.mult), reads=["a_tb", "a_ti"], writes=["a_tb"])
                yd = y[d]
                if d == 0:
                    init = 0.0 if prev is None else yd[:, prev[0] + prev[1] - 1:prev[0] + prev[1]]
                    P.op("dve", lambda e, n=n, t0=t0, yd=yd, init=init: e.tensor_tensor_scan(out=yd[:, t0:t0 + n], data0=ta[:, :n], data1=tb[:, :n],
                                                                                           initial=init, op0=ALU.mult, op1=ALU.add),
                         reads=["a_ta", "a_tb", ("a_y", d)], writes=[("a_y", d)])
                else:
                    init = 0.0 if prev is None else yd[:, prev[0]:prev[0] + 1]
                    P.op("dve", lambda e, n=n, t0=t0, yd=yd, init=init: e.tensor_tensor_scan(out=yd[:, t0:t0 + n][:, ::-1], data0=ta[:, :n][:, ::-1],
                                                                                           data1=tb[:, :n][:, ::-1], initial=init, op0=ALU.mult, op1=ALU.add),
                         reads=["a_ta", "a_tb", ("a_y", d)], writes=[("a_y", d)])
                prev = (t0, n)
        for (t0, n, seg) in PIECES:
            P.op("pool", lambda e, t0=t0, n=n: e.tensor_tensor(out=y[0][:, t0:t0 + n], in0=y[0][:, t0:t0 + n], in1=y[1][:, t0:t0 + n], op=ALU.add),
                 reads=[("a_y", 0), ("a_y", 1)], writes=[("a_y", 0)])
            P.op("dve", lambda e, t0=t0, n=n: e.tensor_tensor(out=yo[:, t0:t0 + n], in0=y[0][:, t0:t0 + n], in1=ga[:, t0:t0 + n], op=ALU.mult),
                 reads=[("a_y", 0), "a_ga"], writes=["a_out"])
        P.dma(yaT[:, :], yo[:, :], reads=["a_out"], writes=["yaT"])
        P.barrier()


def emit_MC(nc, P, C, zT, zv, pc, ycT):
    with ExitStack() as st:
        sb = lambda n, s, d: st.enter_context(nc.sbuf_tensor(_uq(n), s, d))
        PS, identf, onesb = C["PS"], C["identf"], C["onesb"]
        prm = sb("c_prm", [128, 2], F32)
        P.dma(prm[:], pc[:, :], writes=["c_prm"])
        cosT = sb("c_cos", [128, T], BF16)
        sinT = sb("c_sin", [128, T], BF16)
        rm = sb("c_rm", [128, 128], BF16)
        with ExitStack() as st2:
            sb2 = lambda n, s, d: st2.enter_context(nc.sbuf_tensor(_uq(n), s, d))
            CW = 2048
            pos = sb2("c_pos", [128, CW], F32)
            ang = sb2("c_ang", [128, CW], F32)
            ki = sb2("c_ki", [128, CW], I32)
            rr = sb2("c_rr", [128, CW], F32)
            tt = sb2("c_tt", [128, CW], F32)
            fi = sb2("c_fi", [128, 1], F32)
            for q in range(4):
                P.op("pool", lambda e, q=q: e.iota(out=fi[32 * q:32 * q + 32, :], pattern=[[0, 1]], base=0, channel_multiplier=1,
                                                   allow_small_or_imprecise_dtypes=True), writes=[("c_fi", q)])
            P.op("act", lambda e: e.activation(out=fi[:], in_=fi[:], func=AF.Exp, scale=-LN1E4_32), reads=[("c_fi", q) for q in range(4)], writes=["c_fi"])
            PI = float(np.pi)
            for c in range(T // CW):
                for q in range(4):
                    pat = [[1, CW // 64], [0, 64]] if q % 2 == 0 else [[0, CW // 64], [1, 64]]
                    base = (CW // 64) * c if q % 2 == 0 else 0
                    P.op("pool", lambda e, q=q, pat=pat, base=base: e.iota(out=pos[32 * q:32 * q + 32, :], pattern=pat, base=base, channel_multiplier=0,
                                                                           allow_small_or_imprecise_dtypes=True), writes=[("c_pos", q)])
                for which, (dst, shift) in enumerate(((sinT, 0.0), (cosT, PI / 2))):
                    P.op("dve", lambda e, shift=shift: e.tensor_scalar(out=ang[:], in0=pos[:], scalar1=fi[:, 0:1], scalar2=shift, op0=ALU.mult, op1=ALU.add),
                         reads=[("c_pos", q) for q in range(4)] + ["c_fi"], writes=["c_ang"])
                    P.op("dve", lambda e: e.tensor_scalar(out=ki[:], in0=ang[:], scalar1=1.0 / TWO_PI, scalar2=None, op0=ALU.mult), reads=["c_ang"], writes=["c_ki"])
                    P.op("dve", lambda e: e.tensor_copy(out=rr[:], in_=ki[:]), reads=["c_ki"], writes=["c_rr"])
                    P.op("dve", lambda e: e.scalar_tensor_tensor(out=rr[:], in0=rr[:], scalar=-TWO_PI, in1=ang[:], op0=ALU.mult, op1=ALU.add),
                         reads=["c_rr", "c_ang"], writes=["c_rr"])
                    P.op("dve", lambda e: e.tensor_scalar(out=tt[:], in0=rr[:], scalar1=PI, scalar2=TWO_PI, op0=ALU.is_gt, op1=ALU.mult), reads=["c_rr"], writes=["c_tt"])
                    P.op("dve", lambda e: e.tensor_tensor(out=rr[:], in0=rr[:], in1=tt[:], op=ALU.subtract), reads=["c_rr", "c_tt"], writes=["c_rr"])
                    P.op("dve", lambda e: e.tensor_scalar(out=tt[:], in0=rr[:], scalar1=-PI, scalar2=TWO_PI, op0=ALU.is_lt, op1=ALU.mult), reads=["c_rr"], writes=["c_tt"])
                    P.op("dve", lambda e: e.tensor_tensor(out=rr[:], in0=rr[:], in1=tt[:], op=ALU.add), reads=["c_rr", "c_tt"], writes=["c_rr"])
                    P.op("dve", lambda e: e.tensor_scalar(out=rr[:], in0=rr[:], scalar1=PI, scalar2=-PI, op0=ALU.min, op1=ALU.max), reads=["c_rr"], writes=["c_rr"])
                    P.op("act", lambda e, dst=dst, c=c: e.activation(out=dst[:, c * CW:(c + 1) * CW], in_=rr[:], func=AF.Sin), reads=["c_rr"], writes=[("c_tab", which, c)])
            P.barrier()
        P.op("dve", lambda e: e.tensor_scalar(out=rm[:, 0:64], in0=identf[:, 64:128], scalar1=-1.0, scalar2=None, op0=ALU.mult), reads=["ident"], writes=["c_rm"])
        P.op("dve", lambda e: e.tensor_copy(out=rm[:, 64:128], in_=identf[:, 0:64]), reads=["ident"], writes=["c_rm"])
        qT = sb("c_qT", [128, NT], BF16)
        kT = sb("c_kT", [128, NT], BF16)
        sg = sb("c_sg", [128, NT], BF16)
        V = sb("c_V", [128, 66, 128], BF16)
        zin = [sb("c_zin%d" % i, [128, 512], BF16) for i in range(2)]
        sq = sb("c_sq", [128, 512], BF16)
        rs = sb("c_rs", [128, 512], F32)
        xn = sb("c_xn", [128, 512], BF16)
        t1 = sb("c_t1", [128, 512], F32)
        t2 = sb("c_t2", [128, 512], F32)
        P.dma(sg[:], zT[768:896, :], writes=["c_sg"])
        P.op("act", lambda e: e.activation(out=sg[:], in_=sg[:], func=AF.Silu), reads=["c_sg"], writes=["c_sg"])
        for half in range(2):
            P.dma(V[:, 33 * half:33 * half + 33, :], zv[4224 * half:4224 * (half + 1), :].rearrange("(k p) d -> p k d", p=128), writes=[("c_V", half)])
        tiles = [(0, 256)] + [(256 + 512 * k, 512) for k in range(16)]
        n_in = 0
        for which, (row0, dst, pcol) in enumerate(((512, qT, 0), (640, kT, 1))):
            for (t0, n) in tiles:
                zi = zin[n_in % 2]
                zk = "c_zin%d" % (n_in % 2)
                n_in += 1
                P.dma(zi[:, :n], zT[row0:row0 + 128, t0:t0 + n], writes=[zk])
                P.op("pool", lambda e, zi=zi, n=n: e.tensor_tensor(out=sq[:, :n], in0=zi[:, :n], in1=zi[:, :n], op=ALU.mult), reads=[zk], writes=["c_sq"])
                P.op("pe", lambda e, n=n: e.matmul(out=PS[7][:, :n], lhsT=onesb[:], rhs=sq[:, :n], start=True, stop=True), reads=["c_sq", "onesb"], writes=["PS7"])
                P.op("act", lambda e, n=n: e.activation(out=rs[:, :n], in_=PS[7][:, :n], func=AF.Sqrt, scale=1.0 / 128, bias=C["epsb"][:, 0:1]),
                     reads=["PS7", "epsb"], writes=["c_rs"])
                P.op("dve", lambda e, n=n: e.reciprocal(out=rs[:, :n], in_=rs[:, :n]), reads=["c_rs"], writes=["c_rs"])
                if t0 < LC:
                    P.op("dve", lambda e, zi=zi, n=n, t0=t0, dst=dst, pcol=pcol: e.scalar_tensor_tensor(
                        out=dst[:, t0:t0 + n], in0=zi[:, :n], scalar=prm[:, pcol:pcol + 1], in1=rs[:, :n], op0=ALU.mult, op1=ALU.mult),
                        reads=[zk, "c_rs", "c_prm"], writes=[("c_qk", which, t0)])
                    continue
                P.op("dve", lambda e, zi=zi, n=n, pcol=pcol: e.scalar_tensor_tensor(out=xn[:, :n], in0=zi[:, :n], scalar=prm[:, pcol:pcol + 1],
                                                                                  in1=rs[:, :n], op0=ALU.mult, op1=ALU.mult),
                     reads=[zk, "c_rs", "c_prm"], writes=["c_xn"])
                P.op("pe", lambda e, n=n: e.matmul(out=PS[6][:, :n], lhsT=rm[:], rhs=xn[:, :n], start=True, stop=True), reads=["c_xn", "c_rm"], writes=["PS6"])
                P.op("pool", lambda e, n=n, t0=t0: e.tensor_tensor(out=t1[:, :n], in0=xn[:, :n], in1=cosT[:, t0 - LC:t0 - LC + n], op=ALU.mult),
                     reads=["c_xn"], writes=["c_t1"])
                P.op("dve", lambda e, n=n, t0=t0: e.tensor_tensor(out=t2[:, :n], in0=PS[6][:, :n], in1=sinT[:, t0 - LC:t0 - LC + n], op=ALU.mult),
                     reads=["PS6"], writes=["c_t2"])
                P.op("dve", lambda e, n=n, t0=t0, dst=dst: e.tensor_tensor(out=dst[:, t0:t0 + n], in0=t1[:, :n], in1=t2[:, :n], op=ALU.add),
                     reads=["c_t1", "c_t2"], writes=[("c_qk", which, t0)])
        P.op("pool", lambda e: e.tensor_copy(out=t1[:, 0:1], in_=t1[:, 0:1]),
             reads=[("c_qk", w, t[0]) for w in range(2) for t in tiles] + [("c_V", 0), ("c_V", 1)], writes=["c_qkv", "c_t1"])
        pt = [sb("c_pt%d" % i, [128, 512], BF16) for i in range(3)]
        rd = sb("c_rd", [128, 512], F32)
        of = sb("c_of", [128, 512], F32)
        yo = [sb("c_yo%d" % i, [128, 512], BF16) for i in range(2)]
        scl = float(128 ** -0.5)
        for qi, (q0, qn_) in enumerate(tiles):
            kts = list(range(2)) if qi == 0 else list(range(66))
            nk = len(kts)
            ob, db = 3 + qi % 2, 5
            def smm(i, kts=kts, q0=q0, qn_=qn_):
                kt = kts[i]
                P.op("pe", lambda e, kt=kt, i=i: e.matmul(out=PS[i % 3][:, :qn_], lhsT=kT[:, kt * 128:(kt + 1) * 128], rhs=qT[:, q0:q0 + qn_], start=True, stop=True),
                     reads=["c_qkv"], writes=["PS%d" % (i % 3)])
            smm(0)
            if nk > 1:
                smm(1)
            for i in range(nk):
                kt = kts[i]
                p_ = pt[i % 3]
                P.op("act", lambda e, i=i, p_=p_, qn_=qn_: e.activation(out=p_[:, :qn_], in_=PS[i % 3][:, :qn_], func=AF.Exp, scale=scl),
                     reads=["PS%d" % (i % 3)], writes=["c_pt%d" % (i % 3)])
                if i + 2 < nk:
                    smm(i + 2)
                P.op("pe", lambda e, kt=kt, i=i, p_=p_, qn_=qn_, ob=ob, nk=nk: e.matmul(out=PS[ob][:, :qn_], lhsT=V[:, kt, :], rhs=p_[:, :qn_], start=(i == 0), stop=(i == nk - 1)),
                     reads=["c_pt%d" % (i % 3), "c_qkv"], writes=["PS%d" % ob])
                P.op("pe", lambda e, i=i, p_=p_, qn_=qn_, db=db, nk=nk: e.matmul(out=PS[db][:, :qn_], lhsT=onesb[:], rhs=p_[:, :qn_], start=(i == 0), stop=(i == nk - 1)),
                     reads=["c_pt%d" % (i % 3), "onesb"], writes=["PS%d" % db])
            P.op("dve", lambda e, qn_=qn_, db=db: e.reciprocal(out=rd[:, :qn_], in_=PS[db][:, :qn_]), reads=["PS%d" % db], writes=["c_rd"])
            P.op("dve", lambda e, qn_=qn_, ob=ob: e.tensor_tensor(out=of[:, :qn_], in0=PS[ob][:, :qn_], in1=rd[:, :qn_], op=ALU.mult), reads=["PS%d" % ob, "c_rd"], writes=["c_of"])
            y_ = yo[qi % 2]
            P.op("pool", lambda e, y_=y_, qn_=qn_, q0=q0: e.tensor_tensor(out=y_[:, :qn_], in0=of[:, :qn_], in1=sg[:, q0:q0 + qn_], op=ALU.mult),
                 reads=["c_of", "c_sg"], writes=["c_yo%d" % (qi % 2)])
            P.dma(ycT[:, q0:q0 + qn_], y_[:, :qn_], reads=["c_yo%d" % (qi % 2)], writes=[("ycT", qi)])
        P.barrier()


def build_M(parts=("A", "B", "C")):
    nc = bass.Bass("TRN2", target_bir_lowering=False)
    din = lambda n, s, d: nc.dram_tensor(n, s, d, kind="ExternalInput").ap()
    dout = lambda n, s, d: nc.dram_tensor(n, s, d, kind="ExternalOutput").ap()
    zT = din("zT", [896, NT], BF16)
    zv = din("zv", [NT, 128], BF16)
    zba = din("zba", [NT, 4], F32)
    pa = din("pa", [64, 12], F32)
    wri = din("wri", [64, 4, 64], F32)
    pb = din("pb", [128, 16], F32)
    pc = din("pc", [128, 2], F32)
    yaT = dout("yaT", [64, NT], BF16)
    obT = dout("obT", [64, NT], F32)
    gbT = dout("gbT", [64, NT], BF16)
    ycT = dout("ycT", [128, NT], BF16)
    with ExitStack() as st:
        P = Prog(nc, st)
        C = mixer_consts(nc, st, P)
        if "A" in parts:
            emit_MA(nc, P, C, zT, pa, wri, yaT)
        if "B" in parts:
            emit_MB(nc, P, C, zT, zba, pb, obT, gbT)
        if "C" in parts:
            emit_MC(nc, P, C, zT, zv, pc, ycT)
        P.finish()
        P.emit()
    return nc


def mixer_consts(nc, st, P):
    sb = lambda n, s, d: st.enter_context(nc.sbuf_tensor(_uq(n), s, d))
    C = {}
    C["PS"] = [st.enter_context(nc.psum_tensor("PS%d" % i, [128, 512], F32)) for i in range(8)]
    C["identf"] = sb("identf", [128, 128], F32)
    C["identb"] = sb("identb", [128, 128], BF16)
    C["onesf"] = sb("onesf", [128, 128], F32)
    C["onesb"] = sb("onesb", [128, 128], BF16)
    C["epsb"] = sb("epsb", [128, 1], F32)
    _ident(P, C["identf"], C["identb"])
    P.op("pool", lambda e: e.memset(C["onesf"][:], 1.0), writes=["onesf"])
    P.op("pool", lambda e: e.memset(C["onesb"][:], 1.0), writes=["onesb"])
    P.op("pool", lambda e: e.memset(C["epsb"][:], EPS), writes=["epsb"])
    return C


def m_params(d, l, j):
    hb, hf = j // 2, j % 2
    sl = slice(64 * j, 64 * j + 64)
    pa = np.zeros((64, 12), np.float32)
    pa[:, 0:4] = d["conv_a_w"][l][:, sl].T
    pa[:, 4] = d["conv_a_b"][l][sl]
    pa[:, 5:7] = d["b_ra"][l][:, sl].T
    pa[:, 7:9] = d["b_ia"][l][:, sl].T
    pa[:, 9:11] = d["lam_a"][l][:, sl].T
    wri = np.stack([d["w_ra"][l][0, j], d["w_ra"][l][1, j], d["w_ia"][l][0, j], d["w_ia"][l][1, j]], axis=1)
    cw = d["conv_b_w"][l]
    pb = np.zeros((128, 16), np.float32)
    pb[:, 0:4] = cw[:, 128 * hb:128 * hb + 128].T
    pb[:, 4:8] = cw[:, 512 + 128 * hb:512 + 128 * hb + 128].T
    pb[0:64, 8:12] = cw[:, 1024 + 128 * hb + 64 * hf:1024 + 128 * hb + 64 * hf + 64].T
    pb[:, 12:14] = d["a_log_b"][l][:, hb][None, :]
    pb[:, 14:16] = d["dt_bias_b"][l][:, hb][None, :]
    pc = np.stack([d["qn_c"][l], d["kn_c"][l]], axis=1)
    return {"pa": pa, "wri": np.ascontiguousarray(wri.astype(np.float32)), "pb": pb, "pc": np.ascontiguousarray(pc.astype(np.float32))}


def emit_MB(nc, P, C, zT, zba, pb, obT, gbT, ssb=None):
    NCH = NT // 64
    WINS = [(0, 4)] + [(4 + 8 * k, 8) for k in range(16)]
    with ExitStack() as st:
        sb = lambda n, s, d: st.enter_context(nc.sbuf_tensor(_uq(n), s, d))
        PS, identf, identb, onesf, onesb, epsb = C["PS"], C["identf"], C["identb"], C["onesf"], C["onesb"], C["epsb"]
        prm = sb("b_prm", [128, 16], F32)
        P.dma(prm[:], pb[:, :], writes=["b_prm"])
        qT = sb("b_qT", [128, NT], BF16)
        kT = sb("b_kT", [128, NT], BF16)
        vT = sb("b_vT", [64, NT], BF16)
        oT = sb("b_oT", [64, NT], F32)
        with ExitStack() as st2:
            sb2 = lambda n, s, d: st2.enter_context(nc.sbuf_tensor(_uq(n), s, d))
            gbt = sb2("b_gb", [64, NT], BF16)
            P.dma(gbt[:], zT[448:512, :], writes=["b_gb"])
            P.op("act", lambda e: e.activation(out=gbt[:], in_=gbt[:], func=AF.Silu), reads=["b_gb"], writes=["b_gb"])
            P.dma(gbT[:, :], gbt[:], reads=["b_gb"], writes=["gbT"])
            xin = sb2("b_xin", [128, 2051], BF16)
            cv = sb2("b_cv", [128, 2048], F32)
            sl = sb2("b_sl", [128, 2048], F32)
            sq = sb2("b_sq", [128, 512], BF16)
            rs = sb2("b_rs", [128, 512], F32)
            for which, (row0, npart, tc, dst) in enumerate(((128, 128, 0, qT), (256, 128, 4, kT), (384, 64, 8, vT))):
                for (t0, n, seg) in PIECES:
                    conv_piece(P, zT[row0:row0 + npart, :], xin, "b_xin", cv, "b_cv", prm[:npart, tc:tc + 4], t0, n, seg, npart)
                    if which == 2:
                        P.op("act", lambda e, t0=t0, n=n: e.activation(out=vT[:, t0:t0 + n], in_=cv[0:64, :n], func=AF.Silu),
                             reads=["b_cv"], writes=[("b_qkv", which, t0)])
                        continue
                    P.op("act", lambda e, n=n: e.activation(out=sl[:, :n], in_=cv[:, :n], func=AF.Silu), reads=["b_cv"], writes=["b_sl"])
                    for s0 in range(0, n, 512):
                        sn = min(512, n - s0)
                        P.op("pool", lambda e, s0=s0, sn=sn: e.tensor_tensor(out=sq[:, :sn], in0=sl[:, s0:s0 + sn], in1=sl[:, s0:s0 + sn], op=ALU.mult),
                             reads=["b_sl"], writes=["b_sq"])
                        P.op("pe", lambda e, sn=sn: e.matmul(out=PS[7][:, :sn], lhsT=onesb[:], rhs=sq[:, :sn], start=True, stop=True),
                             reads=["b_sq", "onesb"], writes=["PS7"])
                        P.op("act", lambda e, sn=sn: e.activation(out=rs[:, :sn], in_=PS[7][:, :sn], func=AF.Sqrt, bias=epsb[:, 0:1]),
                             reads=["PS7", "epsb"], writes=["b_rs"])
                        P.op("dve", lambda e, sn=sn: e.reciprocal(out=rs[:, :sn], in_=rs[:, :sn]), reads=["b_rs"], writes=["b_rs"])
                        cst = float(128 ** -0.5) if which == 0 else 1.0
                        P.op("dve", lambda e, s0=s0, sn=sn, t0=t0, dst=dst, cst=cst: e.scalar_tensor_tensor(
                            out=dst[:, t0 + s0:t0 + s0 + sn], in0=sl[:, s0:s0 + sn], scalar=cst, in1=rs[:, :sn], op0=ALU.mult, op1=ALU.mult),
                            reads=["b_sl", "b_rs"], writes=[("b_qkv", which, t0)])
            P.barrier()
        gin = sb("b_gin", [64, NCH, 4], F32)
        for q in range(4):
            P.dma(gin[:, 33 * q:33 * q + 33, :], zba[2112 * q:2112 * (q + 1), :].rearrange("(c p) f -> p c f", p=64), writes=[("b_gin", q)])
        P.op("pool", lambda e: e.tensor_copy(out=gin[:, 0, 0:1], in_=gin[:, 0, 0:1]), reads=[("b_gin", q) for q in range(4)], writes=["b_gin"])
        na = sb("b_na", [128, 2], F32)
        P.op("act", lambda e: e.activation(out=na[:], in_=prm[:, 12:14], func=AF.Exp), reads=["b_prm"], writes=["b_na"])
        P.op("dve", lambda e: e.tensor_scalar(out=na[:], in0=na[:], scalar1=-1.0, scalar2=None, op0=ALU.mult), reads=["b_na"], writes=["b_na"])
        beta = sb("b_beta", [64, 2, NCH], F32)
        nbeta = sb("b_nbeta", [64, 2, NCH], F32)
        gg = sb("b_gg", [64, 2, NCH], F32)
        G = sb("b_G", [64, 2, NCH], F32)
        nG = sb("b_nG", [64, 2, NCH], F32)
        bexpG = sb("b_bexpG", [64, 2, NCH], F32)
        etail = sb("b_etail", [64, 2, NCH], F32)
        EGT = sb("b_EGT", [128, 2, NCH], F32)
        mc = sb("b_mc", [64, 2, 64], F32)
        nms = sb("b_nms", [64, 2, 64], F32)
        nmt = sb("b_nmt", [64, 2, 64], F32)
        for d in range(2):
            sgn = 1 if d == 0 else -1
            P.op("pool", lambda e, d=d: e.memset(mc[:, d, :], 1.0), writes=[("b_mc", d)])
            P.op("pool", lambda e, d=d, sgn=sgn: e.affine_select(out=mc[:, d, :], in_=mc[:, d, :], pattern=[[sgn, 64]], compare_op=ALU.is_ge, fill=0.0,
                                                               base=0, channel_multiplier=-sgn), reads=[("b_mc", d)], writes=[("b_mc", d)])
            P.op("pool", lambda e, d=d: e.memset(nms[:, d, :], 0.0), writes=[("b_nms", d)])
            P.op("pool", lambda e, d=d, sgn=sgn: e.affine_select(out=nms[:, d, :], in_=nms[:, d, :], pattern=[[-sgn, 64]], compare_op=ALU.is_gt, fill=-1.0e5,
                                                               base=0, channel_multiplier=sgn), reads=[("b_nms", d)], writes=[("b_nms", d)])
            P.op("pool", lambda e, d=d: e.memset(nmt[:, d, :], 0.0), writes=[("b_nmt", d)])
            P.op("pool", lambda e, d=d, sgn=sgn: e.affine_select(out=nmt[:, d, :], in_=nmt[:, d, :], pattern=[[sgn, 64]], compare_op=ALU.is_ge, fill=-1.0e5,
                                                               base=0, channel_multiplier=-sgn), reads=[("b_nmt", d)], writes=[("b_nmt", d)])
            P.op("act", lambda e, d=d: e.activation(out=beta[:, d, :], in_=gin[:, :, d], func=AF.Sigmoid), reads=["b_gin"], writes=[("b_beta", d)])
            P.op("dve", lambda e, d=d: e.tensor_scalar(out=nbeta[:, d, :], in0=beta[:, d, :], scalar1=-1.0, scalar2=None, op0=ALU.mult),
                 reads=[("b_beta", d)], writes=[("b_nbeta", d)])
            P.op("act", lambda e, d=d: e.activation(out=gg[:, d, :], in_=gin[:, :, 2 + d], func=AF.Exp, bias=prm[0:64, 14 + d:15 + d]),
                 reads=["b_gin", "b_prm"], writes=[("b_gg", d)])
            P.op("act", lambda e, d=d: e.activation(out=gg[:, d, :], in_=gg[:, d, :], func=AF.Ln, bias=1.0), reads=[("b_gg", d)], writes=[("b_gg", d)])
            P.op("dve", lambda e, d=d: e.tensor_scalar(out=gg[:, d, :], in0=gg[:, d, :], scalar1=na[0:64, d:d + 1], scalar2=None, op0=ALU.mult),
                 reads=[("b_gg", d), "b_na"], writes=[("b_gg", d)])
            P.op("pe", lambda e, d=d: e.matmul(out=PS[7][0:64, 0:NCH], lhsT=mc[:, d, :], rhs=gg[:, d, :], start=True, stop=True),
                 reads=[("b_mc", d), ("b_gg", d)], writes=["PS7"])
            P.op("pe", lambda e, d=d: e.matmul(out=PS[6][:, 0:NCH], lhsT=onesf[0:64, :], rhs=gg[:, d, :], start=True, stop=True),
                 reads=["onesf", ("b_gg", d)], writes=["PS6"])
            P.op("act", lambda e, d=d: e.copy(out=G[:, d, :], in_=PS[7][0:64, 0:NCH]), reads=["PS7"], writes=[("b_G", d)])
            P.op("dve", lambda e, d=d: e.tensor_scalar(out=nG[:, d, :], in0=G[:, d, :], scalar1=-1.0, scalar2=None, op0=ALU.mult),
                 reads=[("b_G", d)], writes=[("b_nG", d)])
            P.op("act", lambda e, d=d: e.activation(out=EGT[:, d, :], in_=PS[6][:, 0:NCH], func=AF.Exp), reads=["PS6"], writes=[("b_EGT", d)])
            P.op("dve", lambda e, d=d: e.tensor_tensor(out=etail[:, d, :], in0=PS[6][0:64, 0:NCH], in1=G[:, d, :], op=ALU.subtract),
                 reads=["PS6", ("b_G", d)], writes=[("b_etail", d)])
            P.op("act", lambda e, d=d: e.activation(out=etail[:, d, :], in_=etail[:, d, :], func=AF.Exp), reads=[("b_etail", d)], writes=[("b_etail", d)])
            P.op("act", lambda e, d=d: e.activation(out=bexpG[:, d, :], in_=G[:, d, :], func=AF.Exp), reads=[("b_G", d)], writes=[("b_bexpG", d)])
            P.op("dve", lambda e, d=d: e.tensor_tensor(out=bexpG[:, d, :], in0=bexpG[:, d, :], in1=beta[:, d, :], op=ALU.mult),
                 reads=[("b_bexpG", d), ("b_beta", d)], writes=[("b_bexpG", d)])
        f3 = lambda n, p, w: sb(n, [p, 8, w], F32)
        dG, X, Dm, XT, DT, Nm = (f3("b_w%d" % i, 64, 64) for i in range(6))
        EGB = f3("b_EGB", 128, 64)
        Aa = [f3("b_A%d" % i, 64, 64) for i in range(2)]
        At = [f3("b_At%d" % i, 64, 64) for i in range(2)]
        Rr = [f3("b_R%d" % i, 64, 64) for i in range(2)]
        ktok = f3("b_ktok", 64, 128)
        kbg = f3("b_kbg", 64, 128)
        vb = f3("b_vb", 64, 64)
        u_ = f3("b_u", 64, 64)
        wT = f3("b_wT", 128, 64)
        qkT = f3("b_qkT", 64, 64)
        qhT = f3("b_qhT", 128, 64)
        ktail = f3("b_ktail", 64, 128)
        S = sb("b_S", [128, 64], F32)
        vnew = sb("b_vnew", [64, 64], F32)
        id64 = identf[0:64, 0:64]
        bcl = lambda ap, n, w: ap.unsqueeze(2).to_broadcast([64, n, w])
        bcm = lambda ap, n: ap.unsqueeze(1).to_broadcast([64, n, 64])
        fl = lambda t, n: t[:, :n, :].rearrange("p c w -> p (c w)")
        rr = lambda ap: ap.bitcast(F32R)
        S_r = sb("b_Sr", [128, 64], F32)

        def precompute(d, c0, n):
            cs = slice(c0, c0 + n)
            W = n * 64
            tk = lambda c: slice((c0 + c) * 64, (c0 + c + 1) * 64)
            P.op("pool", lambda e: e.tensor_tensor(out=dG[:, :n, :], in0=bcm(id64, n), in1=bcl(G[:, d, cs], n, 64), op=ALU.mult),
                 reads=["ident", ("b_G", d)], writes=["b_dG"])
            for c in range(n):
                P.op("pe", lambda e, c=c: e.matmul(out=PS[0][:, c * 64:(c + 1) * 64], lhsT=onesf[0:64, :], rhs=dG[:, c, :], start=True, stop=True),
                     reads=["onesf", "b_dG"], writes=["PS0"])
            gbv = PS[0][0:64, 0:W].rearrange("p (c w) -> p c w", w=64)
            P.op("dve", lambda e: e.scalar_tensor_tensor(out=X[:, :n, :], in0=gbv, scalar=-1.0, in1=bcm(nms[:, d, :], n), op0=ALU.mult, op1=ALU.add),
                 reads=["PS0", ("b_nms", d)], writes=["b_X"])
            P.op("pool", lambda e: e.tensor_tensor(out=X[:, :n, :], in0=X[:, :n, :], in1=bcl(G[:, d, cs], n, 64), op=ALU.add),
                 reads=["b_X", ("b_G", d)], writes=["b_X"])
            P.op("act", lambda e: e.activation(out=Dm[:, :n, :], in_=X[:, :n, :], func=AF.Exp), reads=["b_X"], writes=["b_D"])
            P.op("dve", lambda e: e.tensor_tensor(out=XT[:, :n, :], in0=gbv, in1=bcm(nmt[:, d, :], n), op=ALU.add),
                 reads=["PS0", ("b_nmt", d)], writes=["b_XT"])
            P.op("pool", lambda e: e.tensor_tensor(out=XT[:, :n, :], in0=XT[:, :n, :], in1=bcl(nG[:, d, cs], n, 64), op=ALU.add),
                 reads=["b_XT", ("b_nG", d)], writes=["b_XT"])
            P.op("act", lambda e: e.activation(out=DT[:, :n, :], in_=XT[:, :n, :], func=AF.Exp), reads=["b_XT"], writes=["b_DT"])
            P.op("act", lambda e: e.activation(out=fl(EGB, n), in_=PS[0][:, 0:W], func=AF.Exp), reads=["PS0"], writes=["b_EGB"])
            for c in range(n):
                P.op("pe", lambda e, c=c: e.matmul(out=PS[1][0:64, c * 64:(c + 1) * 64], lhsT=kT[:, tk(c)], rhs=kT[:, tk(c)], start=True, stop=True),
                     reads=["b_qkvall"], writes=["PS1"])
            kkv = PS[1][0:64, 0:W].rearrange("p (c w) -> p c w", w=64)
            P.op("dve", lambda e: e.tensor_tensor(out=Nm[:, :n, :], in0=kkv, in1=bcl(nbeta[:, d, cs], n, 64), op=ALU.mult),
                 reads=["PS1", ("b_nbeta", d)], writes=["b_N"])
            P.op("pool", lambda e: e.tensor_tensor(out=rr(Aa[0][:, :n, :]), in0=Nm[:, :n, :], in1=Dm[:, :n, :], op=ALU.mult),
                 reads=["b_N", "b_D"], writes=["b_A0"])
            for c in range(n):
                P.op("pe", lambda e, c=c: e.transpose(out=PS[2][0:64, c * 64:(c + 1) * 64], in_=Aa[0][:, c, :], identity=id64),
                     reads=["b_A0", "ident"], writes=["PS2"])
            ntv = PS[2][0:64, 0:W].rearrange("p (c w) -> p c w", w=64)
            P.op("act", lambda e: e.copy(out=rr(At[0][:, :n, :]), in_=ntv), reads=["PS2"], writes=["b_At0"])
            P.op("dve", lambda e: e.tensor_tensor(out=rr(Rr[0][:, :n, :]), in0=ntv, in1=bcm(id64, n), op=ALU.add), reads=["PS2", "ident"], writes=["b_R0"])
            cur = 0
            for k in range(1, 6):
                nx = 1 - cur
                for c in range(n):
                    P.op("pe", lambda e, c=c, cur=cur: e.matmul(out=PS[3][0:64, c * 64:(c + 1) * 64], lhsT=rr(At[cur][:, c, :]), rhs=rr(Aa[cur][:, c, :]), start=True, stop=True),
                         reads=["b_A%d" % cur, "b_At%d" % cur], writes=["PS3"])
                if k < 5:
                    for c in range(n):
                        P.op("pe", lambda e, c=c, cur=cur: e.matmul(out=PS[4][0:64, c * 64:(c + 1) * 64], lhsT=rr(Aa[cur][:, c, :]), rhs=rr(At[cur][:, c, :]), start=True, stop=True),
                             reads=["b_A%d" % cur, "b_At%d" % cur], writes=["PS4"])
                P.op("act", lambda e, nx=nx: e.copy(out=rr(fl(Aa[nx], n)), in_=PS[3][0:64, 0:W]), reads=["PS3"], writes=["b_A%d" % nx])
                if k < 5:
                    P.op("dve", lambda e, nx=nx: e.tensor_copy(out=rr(fl(At[nx], n)), in_=PS[4][0:64, 0:W]), reads=["PS4"], writes=["b_At%d" % nx])
                for c in range(n):
                    P.op("pe", lambda e, c=c, cur=cur, nx=nx: e.matmul(out=PS[5][0:64, c * 64:(c + 1) * 64], lhsT=rr(Aa[nx][:, c, :]), rhs=rr(Rr[cur][:, c, :]), start=True, stop=True),
                         reads=["b_A%d" % nx, "b_R%d" % cur], writes=["PS5"])
                P.op("dve", lambda e, cur=cur, nx=nx: e.tensor_tensor(out=rr(fl(Rr[nx], n)), in0=PS[5][0:64, 0:W], in1=fl(Rr[cur], n), op=ALU.add),
                     reads=["PS5", "b_R%d" % cur], writes=["b_R%d" % nx])
                cur = nx
            R = Rr[cur]
            rk = "b_R%d" % cur
            for hf in range((n + 3) // 4):
                for c in range(4 * hf, min(n, 4 * hf + 4)):
                    P.op("pe", lambda e, c=c, hf=hf: e.matmul(out=PS[6 + hf][0:64, (c - 4 * hf) * 128:(c - 4 * hf + 1) * 128], lhsT=kT[:, tk(c)], rhs=identb[:],
                                                            start=True, stop=True), reads=["b_qkvall", "identb"], writes=["PS%d" % (6 + hf)])
                m = min(n, 4 * hf + 4) - 4 * hf
                P.op("act", lambda e, hf=hf, m=m: e.copy(out=ktok[:, 4 * hf:4 * hf + m, :].rearrange("p c w -> p (c w)"), in_=PS[6 + hf][0:64, 0:m * 128]),
                     reads=["PS%d" % (6 + hf)], writes=[("b_ktok", hf)])
            P.op("pool", lambda e: e.tensor_tensor(out=rr(kbg[:, :n, :]), in0=ktok[:, :n, :], in1=bcl(bexpG[:, d, cs], n, 128), op=ALU.mult),
                 reads=[("b_ktok", 0), ("b_ktok", 1), ("b_bexpG", d)], writes=["b_kbg"])
            P.op("pool", lambda e: e.tensor_tensor(out=rr(ktail[:, :n, :]), in0=ktok[:, :n, :], in1=bcl(etail[:, d, cs], n, 128), op=ALU.mult),
                 reads=[("b_ktok", 0), ("b_ktok", 1), ("b_etail", d)], writes=["b_ktail"])
            for c in range(n):
                P.op("pe", lambda e, c=c: e.matmul(out=PS[1][0:64, c * 64:(c + 1) * 64], lhsT=vT[:, tk(c)], rhs=identb[0:64, 0:64], start=True, stop=True),
                     reads=["b_qkvall", "identb"], writes=["PS1"])
            P.op("dve", lambda e: e.tensor_tensor(out=rr(vb[:, :n, :]), in0=kkv, in1=bcl(beta[:, d, cs], n, 64), op=ALU.mult),
                 reads=["PS1", ("b_beta", d)], writes=["b_vb"])
            for c in range(n):
                P.op("pe", lambda e, c=c, R=R: e.matmul(out=PS[2][0:64, c * 64:(c + 1) * 64], lhsT=rr(R[:, c, :]), rhs=rr(vb[:, c, :]), start=True, stop=True),
                     reads=[rk, "b_vb"], writes=["PS2"])
            P.op("act", lambda e: e.copy(out=fl(u_, n), in_=PS[2][0:64, 0:W]), reads=["PS2"], writes=["b_u"])
            for c in range(n):
                P.op("pe", lambda e, c=c, R=R: e.matmul(out=PS[3][:, c * 64:(c + 1) * 64], lhsT=rr(kbg[:, c, :]), rhs=rr(R[:, c, :]), start=True, stop=True),
                     reads=[rk, "b_kbg"], writes=["PS3"])
            P.op("dve", lambda e: e.tensor_copy(out=rr(fl(wT, n)), in_=PS[3][:, 0:W]), reads=["PS3"], writes=["b_wT"])
            for c in range(n):
                P.op("pe", lambda e, c=c: e.matmul(out=PS[4][0:64, c * 64:(c + 1) * 64], lhsT=kT[:, tk(c)], rhs=qT[:, tk(c)], start=True, stop=True),
                     reads=["b_qkvall"], writes=["PS4"])
            P.op("dve", lambda e: e.tensor_tensor(out=rr(fl(qkT, n)), in0=PS[4][0:64, 0:W], in1=fl(DT, n), op=ALU.mult), reads=["PS4", "b_DT"], writes=["b_qkT"])
            P.op("pool", lambda e: e.tensor_tensor(out=rr(fl(qhT, n)), in0=qT[:, c0 * 64:c0 * 64 + W], in1=fl(EGB, n), op=ALU.mult),
                 reads=["b_qkvall", "b_EGB"], writes=["b_qhT"])

        def sequential(d, c0, n, first_dir):
            order = list(range(n)) if d == 0 else list(range(n - 1, -1, -1))
            W = n * 64
            for ci in order:
                c = c0 + ci
                P.op("pe", lambda e, ci=ci: e.matmul(out=PS[0][0:64, 0:64], lhsT=rr(wT[:, ci, :]), rhs=rr(S_r[:]), start=True, stop=True),
                     reads=["b_wT", "b_Sr"], writes=["PS0"])
                P.op("pe", lambda e, ci=ci: e.matmul(out=PS[5][0:64, ci * 64:(ci + 1) * 64], lhsT=rr(S_r[:]), rhs=rr(qhT[:, ci, :]), start=True, stop=False),
                     reads=["b_Sr", "b_qhT"], writes=["PS5"])
                P.op("dve", lambda e, ci=ci: e.tensor_tensor(out=rr(vnew[:]), in0=u_[:, ci, :], in1=PS[0][0:64, 0:64], op=ALU.subtract),
                     reads=["b_u", "PS0"], writes=["b_vnew"])
                P.op("pe", lambda e, ci=ci: e.matmul(out=PS[5][0:64, ci * 64:(ci + 1) * 64], lhsT=rr(vnew[:]), rhs=rr(qkT[:, ci, :]), start=False, stop=True),
                     reads=["b_vnew", "b_qkT"], writes=["PS5"])
                P.op("pe", lambda e, ci=ci: e.matmul(out=PS[1][:, 0:64], lhsT=rr(ktail[:, ci, :]), rhs=rr(vnew[:]), start=True, stop=True),
                     reads=["b_vnew", "b_ktail"], writes=["PS1"])
                P.op("dve", lambda e, c=c: e.scalar_tensor_tensor(out=S[:], in0=S[:], scalar=EGT[:, d, c:c + 1], in1=PS[1][:, 0:64], op0=ALU.mult, op1=ALU.add),
                     reads=["b_S", ("b_EGT", d), "PS1"], writes=["b_S"])
                P.op("act", lambda e: e.copy(out=rr(S_r[:]), in_=S[:]), reads=["b_S"], writes=["b_Sr"])
            osl = oT[:, c0 * 64:c0 * 64 + W]
            if first_dir:
                P.op("act", lambda e: e.copy(out=osl, in_=PS[5][0:64, 0:W]), reads=["PS5"], writes=[("b_oT", c0)])
            else:
                P.op("dve", lambda e: e.tensor_tensor(out=osl, in0=osl, in1=PS[5][0:64, 0:W], op=ALU.add), reads=["PS5", ("b_oT", c0)], writes=[("b_oT", c0)])

        dmy = sb("b_dmy", [64, 2], F32)
        P.op("pool", lambda e: e.memset(dmy[:], 0.0),
             reads=[("b_qkv", w, p[0]) for w in range(3) for p in PIECES], writes=["b_qkvall"])
        for d in range(2):
            P.op("pool", lambda e: e.memset(S[:], 0.0), writes=["b_S"])
            P.op("pool", lambda e: e.tensor_copy(out=rr(S_r[:]), in_=S[:]), reads=["b_S"], writes=["b_Sr"])
            wins = WINS if d == 0 else [WINS[0]] + WINS[:0:-1]
            for (c0, n) in wins:
                precompute(d, c0, n)
                sequential(d, c0, n, d == 0)
        P.dma(obT[:, :], oT[:, :], reads=[("b_oT", w[0]) for w in WINS], writes=["obT"])
        if ssb is not None:
            ssr = sb("b_ssr", [1, NT], F32)
            sq2 = [sb("b_sq2%d" % i, [64, 512], F32) for i in range(2)]
            for ti, (t0, n) in enumerate(Z_TILES):
                q_ = sq2[ti % 2]
                P.op("pool", lambda e, q_=q_, t0=t0, n=n: e.tensor_tensor(out=q_[:, :n], in0=oT[:, t0:t0 + n], in1=oT[:, t0:t0 + n], op=ALU.mult),
                     reads=[("b_oT", w[0]) for w in WINS], writes=["b_sq2%d" % (ti % 2)])
                P.op("pe", lambda e, q_=q_, n=n: e.matmul(out=PS[7][0:1, :n], lhsT=onesf[0:64, 0:1], rhs=q_[:, :n], start=True, stop=True),
                     reads=["b_sq2%d" % (ti % 2), "onesf"], writes=["PS7"])
                P.op("act", lambda e, t0=t0, n=n: e.copy(out=ssr[0:1, t0:t0 + n], in_=PS[7][0:1, :n]), reads=["PS7"], writes=["b_ssr"])
            P.dma(ssb[0:1, :], ssr[:], reads=["b_ssr"], writes=["ssb"])
        P.barrier()


_PROGS = {}


def _prog(name, fn):
    if name not in _PROGS:
        _PROGS[name] = fn()
    return _PROGS[name]


def _run(nc, maps):
    return run_bass_kernel_spmd(nc, maps, core_ids=list(range(NCORE))).results


def _tok_slice(full, i):
    return np.ascontiguousarray(np.concatenate([full[:, 32 * i:32 * i + 32], full[:, LC + 1024 * i:LC + 1024 * (i + 1)]], axis=1))


def mix_row_feature(j, r):
    if r < 64:
        return 64 * j + r
    if r < 128:
        return D_A + 64 * j + (r - 64)
    return D_A + D_B + 128 * j + (r - 128)


MIX_PERM = [mix_row_feature(j, r) for j in range(NCORE) for r in range(256)]


def emit_mod(nc, P, C, cc, wmT, bmT, modv):
    with ExitStack() as st:
        sb = lambda n, s, d: st.enter_context(nc.sbuf_tensor(_uq(n), s, d))
        PS = C["PS"]
        craw = sb("m_craw", [128, 16, 2], F32)
        sc = sb("m_sc", [128, 16, 2], F32)
        wb = [sb("m_wb%d" % i, [128, 768], F32) for i in range(4)]
        bt = sb("m_bt", [128, DEPTH, 6], F32)
        for r in range(2):
            P.dma(craw[:, :, r], cc[r, :].rearrange("(k p) -> p k", p=128), writes=["m_craw%d" % r], allow_slow_non_contiguous=True)
        P.dma(bt[:], bmT[:, :, :], writes=["m_bt"])
        P.op("act", lambda e: e.activation(out=sc[:], in_=craw[:], func=AF.Silu), reads=["m_craw0", "m_craw1"], writes=["m_sc"])
        n = 0
        for l in range(DEPTH):
            for kc in range(16):
                w = wb[n % 4]
                P.dma(w[:], wmT[l, kc * 128:(kc + 1) * 128, :], writes=["m_wb%d" % (n % 4)])
                for q in range(6):
                    P.op("pe", lambda e, w=w, q=q, kc=kc: e.matmul(out=PS[q][:, 0:2], lhsT=w[:, q * 128:(q + 1) * 128], rhs=sc[:, kc, :],
                                                                 start=(kc == 0), stop=(kc == 15)),
                         reads=["m_sc", "m_wb%d" % (n % 4)], writes=["PS%d" % q])
                n += 1
            for q in range(6):
                P.op("dve", lambda e, l=l, q=q: e.tensor_scalar(out=modv[:, l, q, :], in0=PS[q][:, 0:2], scalar1=bt[:, l, q:q + 1], scalar2=None, op0=ALU.add),
                     reads=["PS%d" % q, "m_bt"], writes=["modv"])
        P.barrier()


def emit_Ap(nc, P, C, l, hT, mix_all, wo, modv, gvv, ss_loc, ss_all, xn_loc, xn_all, yT):
    PS, onesb, onesf, epsb = C["PS"], C["onesb"], C["onesf"], C["epsb"]
    first, last = (l == 0), (l == DEPTH)
    with ExitStack() as st:
        sb = lambda n, s, d: st.enter_context(nc.sbuf_tensor(_uq(n), s, d))
        if not first:
            wst = sb("ap_wst", [128, 16, 256], F32)
            wbf = sb("ap_wbf", [128, 16, 256], BF16)
            P.dma(wst[:], wo.rearrange("(k p) c -> p k c", p=128), writes=["ap_wst"])
            P.op("act", lambda e: e.copy(out=wbf[:], in_=wst[:]), reads=["ap_wst"], writes=["ap_wbf"])
            mt = [sb("ap_mt%d" % i, [128, 16, 512], BF16) for i in range(2)]
        ht = [sb("ap_ht%d" % i, [128, 512], F32) for i in range(4)]
        sq = [sb("ap_sq%d" % i, [128, 512], BF16) for i in range(2)]
        ssr = sb("ap_ssr", [1, NT], F32)
        for ti, (t0, n) in enumerate(Z_TILES):
            r = 1 if t0 < LC else 0
            if not first:
                m_ = mt[ti % 2]
                mk = "ap_mt%d" % (ti % 2)
                P.dma(m_[:, :, :n], mix_all[:, t0:t0 + n].rearrange("(k p) t -> p k t", p=128), writes=[mk])
            for fc in range(2):
                hi = (2 * ti + fc) % 4
                h, hk = ht[hi], "ap_ht%d" % hi
                P.dma(h[:, :n], hT[fc, :, t0:t0 + n], reads=[("hT", fc, ti)], writes=[hk])
                if not first:
                    for kc in range(16):
                        P.op("pe", lambda e, kc=kc, fc=fc, m_=m_, n=n: e.matmul(out=PS[fc][:, :n], lhsT=wbf[:, kc, fc * 128:(fc + 1) * 128], rhs=m_[:, kc, :n],
                                                                             start=(kc == 0), stop=(kc == 15)),
                             reads=["ap_wbf", mk], writes=["PS%d" % fc])
                    P.op("dve", lambda e, fc=fc, h=h, n=n, r=r: e.scalar_tensor_tensor(out=h[:, :n], in0=PS[fc][:, :n], scalar=modv[:, l - 1, 4 + fc, r:r + 1],
                                                                                     in1=h[:, :n], op0=ALU.mult, op1=ALU.add),
                         reads=["PS%d" % fc, hk, "modv"], writes=[hk])
                    P.dma(hT[fc, :, t0:t0 + n], h[:, :n], reads=[hk], writes=[("hT", fc, ti)])
                s_ = sq[fc]
                P.op("act", lambda e, h=h, s_=s_, n=n: e.activation(out=s_[:, :n], in_=h[:, :n], func=AF.Square), reads=[hk], writes=["ap_sq%d" % fc])
                P.op("pe", lambda e, s_=s_, n=n, fc=fc: e.matmul(out=PS[2][0:1, :n], lhsT=onesb[:, 0:1], rhs=s_[:, :n], start=(fc == 0), stop=(fc == 1)),
                     reads=["ap_sq%d" % fc, "onesb"], writes=["PS2"])
            P.op("act", lambda e, t0=t0, n=n: e.copy(out=ssr[0:1, t0:t0 + n], in_=PS[2][0:1, :n]), reads=["PS2"], writes=["ap_ssr"])
        P.dma(ss_loc[0:1, :], ssr[:], reads=["ap_ssr"], writes=["ss_loc"])
        P.allgather(ss_all[:, :], ss_loc[:, :], reads=["ss_loc"], writes=["ss_all"])
        P.barrier()
    with ExitStack() as st:
        sb = lambda n, s, d: st.enter_context(nc.sbuf_tensor(_uq(n), s, d))
        sst = [sb("ap_sst%d" % i, [8, 512], F32) for i in range(2)]
        rstd = [sb("ap_rstd%d" % i, [128, 512], F32) for i in range(2)]
        ht = [sb("ap_h2%d" % i, [128, 512], F32) for i in range(4)]
        tf = [sb("ap_tf%d" % i, [128, 512], F32) for i in range(2)]
        xo = [sb("ap_xo%d" % i, [128, 512], BF16) for i in range(2)]
        for ti, (t0, n) in enumerate(Z_TILES):
            if last and t0 < LC:
                continue
            r = 1 if t0 < LC else 0
            s_, sk = sst[ti % 2], "ap_sst%d" % (ti % 2)
            rs, rk = rstd[ti % 2], "ap_rstd%d" % (ti % 2)
            P.dma(s_[:, :n], ss_all[:, t0:t0 + n], writes=[sk])
            P.op("pe", lambda e, s_=s_, n=n: e.matmul(out=PS[3][:, :n], lhsT=onesf[0:8, :], rhs=s_[:, :n], start=True, stop=True), reads=[sk, "onesf"], writes=["PS3"])
            P.op("act", lambda e, rs=rs, n=n: e.activation(out=rs[:, :n], in_=PS[3][:, :n], func=AF.Sqrt, scale=1.0 / D, bias=epsb[:, 0:1]),
                 reads=["PS3", "epsb"], writes=[rk])
            P.op("dve", lambda e, rs=rs, n=n: e.reciprocal(out=rs[:, :n], in_=rs[:, :n]), reads=[rk], writes=[rk])
            for fc in range(2):
                hi = (2 * ti + fc) % 4
                h, hk = ht[hi], "ap_h2%d" % hi
                P.dma(h[:, :n], hT[fc, :, t0:t0 + n], writes=[hk])
                t_, tk_ = tf[fc], "ap_tf%d" % fc
                gsc = gvv[:, l, fc, r:r + 1]
                P.op("dve", lambda e, h=h, t_=t_, rs=rs, n=n, gsc=gsc: e.scalar_tensor_tensor(out=t_[:, :n], in0=h[:, :n], scalar=gsc, in1=rs[:, :n],
                                                                                           op0=ALU.mult, op1=ALU.mult),
                     reads=[hk, rk, "gvv"], writes=[tk_])
                if last:
                    P.dma(yT[fc * 128:(fc + 1) * 128, t0 - LC:t0 - LC + n], t_[:, :n], reads=[tk_], writes=[("yT", fc, ti)])
                    continue
                x_, xk = xo[fc], "ap_xo%d" % fc
                P.op("act", lambda e, t_=t_, x_=x_, n=n, fc=fc, r=r: e.activation(out=x_[:, :n], in_=t_[:, :n], func=AF.Identity, bias=modv[:, l, fc, r:r + 1]),
                     reads=[tk_, "modv"], writes=[xk])
                P.dma(xn_loc[fc * 128:(fc + 1) * 128, t0:t0 + n], x_[:, :n], reads=[xk], writes=[("xn_loc", fc, ti)])
        if not last:
            P.allgather(xn_all[:, :], xn_loc[:, :], reads=[("xn_loc", fc, ti) for fc in range(2) for ti in range(len(Z_TILES))], writes=["xn_all"])
        P.barrier()


def emit_MBfin(nc, P, C, ob_scr, gb_scr, ssb_all, selB, onB, l, yb_out):
    PS, onesf, epsb = C["PS"], C["onesf"], C["epsb"]
    with ExitStack() as st:
        sb = lambda n, s, d: st.enter_context(nc.sbuf_tensor(_uq(n), s, d))
        sel = sb("f_sel", [8, 64], F32)
        on = sb("f_on", [64, DEPTH], F32)
        P.dma(sel[:], selB[:, :], writes=["f_sel"])
        P.dma(on[:], onB[:, :], writes=["f_on"])
        sst = [sb("f_sst%d" % i, [8, 512], F32) for i in range(2)]
        ot = [sb("f_ot%d" % i, [64, 512], F32) for i in range(2)]
        gt = [sb("f_gt%d" % i, [64, 512], BF16) for i in range(2)]
        rs = [sb("f_rs%d" % i, [64, 512], F32) for i in range(2)]
        yo = [sb("f_yo%d" % i, [64, 512], BF16) for i in range(2)]
        for ti, (t0, n) in enumerate(Z_TILES):
            b = ti % 2
            P.dma(sst[b][:, :n], ssb_all[:, t0:t0 + n], writes=["f_sst%d" % b])
            P.dma(ot[b][:, :n], ob_scr[:, t0:t0 + n], writes=["f_ot%d" % b])
            P.dma(gt[b][:, :n], gb_scr[:, t0:t0 + n], writes=["f_gt%d" % b])
            P.op("pe", lambda e, b=b, n=n: e.matmul(out=PS[4][0:64, :n], lhsT=sel[:], rhs=sst[b][:, :n], start=True, stop=True), reads=["f_sst%d" % b, "f_sel"], writes=["PS4"])
            P.op("act", lambda e, b=b, n=n: e.activation(out=rs[b][:, :n], in_=PS[4][0:64, :n], func=AF.Sqrt, scale=1.0 / 128, bias=epsb[0:64, 0:1]),
                 reads=["PS4", "epsb"], writes=["f_rs%d" % b])
            P.op("dve", lambda e, b=b, n=n: e.reciprocal(out=rs[b][:, :n], in_=rs[b][:, :n]), reads=["f_rs%d" % b], writes=["f_rs%d" % b])
            P.op("dve", lambda e, b=b, n=n: e.scalar_tensor_tensor(out=ot[b][:, :n], in0=ot[b][:, :n], scalar=on[:, l:l + 1], in1=rs[b][:, :n], op0=ALU.mult, op1=ALU.mult),
                 reads=["f_ot%d" % b, "f_rs%d" % b, "f_on"], writes=["f_ot%d" % b])
            P.op("pool", lambda e, b=b, n=n: e.tensor_tensor(out=yo[b][:, :n], in0=ot[b][:, :n], in1=gt[b][:, :n], op=ALU.mult),
                 reads=["f_ot%d" % b, "f_gt%d" % b], writes=["f_yo%d" % b])
            P.dma(yb_out[:, t0:t0 + n], yo[b][:, :n], reads=["f_yo%d" % b], writes=[("yb", ti)])
        P.barrier()


def build_fused(upto=99, nlayers=DEPTH):
    nc = bass.Bass("TRN2", target_bir_lowering=False)
    din = lambda n, s, d: nc.dram_tensor(n, s, d, kind="ExternalInput").ap()
    scr = lambda n, s, d: nc.dram_tensor(n, s, d).ap()
    hT0 = din("hT0", [2, 128, NT], F32)
    cc = din("cc", [2, D], F32)
    wmT = din("wmT", [DEPTH, D, 768], F32)
    bmT = din("bmT", [128, DEPTH, 6], F32)
    ngT = din("ngT", [128, DEPTH + 1, 2], F32)
    selB = din("selB", [8, 64], F32)
    onB = din("onB", [64, DEPTH], F32)
    wz = [din("wz%d" % l, [D, NZ], F32) for l in range(DEPTH)]
    wo = [din("wo%d" % l, [D, 256], F32) for l in range(DEPTH)]
    pa = [din("pa%d" % l, [64, 12], F32) for l in range(DEPTH)]
    wri = [din("wri%d" % l, [64, 4, 64], F32) for l in range(DEPTH)]
    pb = [din("pb%d" % l, [128, 16], F32) for l in range(DEPTH)]
    pc = [din("pc%d" % l, [128, 2], F32) for l in range(DEPTH)]
    yT = nc.dram_tensor("yT", [256, T], F32, kind="ExternalOutput").ap()
    hT = scr("hT", [2, 128, NT], F32)
    ss_loc, ss_all = scr("ss_loc", [1, NT], F32), scr("ss_all", [8, NT], F32)
    ssb_loc, ssb_all = scr("ssb_loc", [1, NT], F32), scr("ssb_all", [8, NT], F32)
    xn_loc, xn_all = scr("xn_loc", [256, NT], BF16), scr("xn_all", [D, NT], BF16)
    mix_loc, mix_all = scr("mix_loc", [256, NT], BF16), scr("mix_all", [D, NT], BF16)
    zT, zv, zba = scr("zT", [896, NT], BF16), scr("zv", [NT, 128], BF16), scr("zba", [NT, 4], F32)
    ob_scr, gb_scr = scr("ob_scr", [64, NT], F32), scr("gb_scr", [64, NT], BF16)
    dbg = upto < 99 or nlayers < DEPTH
    if dbg:
        d_xn = nc.dram_tensor("d_xn", [D, NT], BF16, kind="ExternalOutput").ap()
        d_mix = nc.dram_tensor("d_mix", [D, NT], BF16, kind="ExternalOutput").ap()
        d_h = nc.dram_tensor("d_h", [2, 128, NT], F32, kind="ExternalOutput").ap()
    with ExitStack() as st:
        P = Prog(nc, st)
        C = mixer_consts(nc, st, P)
        sb = lambda n, s, d: st.enter_context(nc.sbuf_tensor(_uq(n), s, d))
        modv = sb("modv", [128, DEPTH, 6, 2], F32)
        gvv = sb("gvv", [128, DEPTH + 1, 2, 2], F32)
        ngs = sb("ngs", [128, DEPTH + 1, 2], F32)
        P.dma(ngs[:], ngT[:, :, :], writes=["ngs"])
        for fc in range(2):
            P.dma(hT[fc, :, :], hT0[fc, :, :], writes=[("hT", fc, ti) for ti in range(len(Z_TILES))])
        emit_mod(nc, P, C, cc, wmT, bmT, modv)
        for l in range(DEPTH + 1):
            for fc in range(2):
                for r in range(2):
                    if l < DEPTH:
                        P.op("dve", lambda e, l=l, fc=fc, r=r: e.tensor_scalar(out=gvv[:, l, fc, r:r + 1], in0=modv[:, l, 2 + fc, r:r + 1], scalar1=1.0,
                                                                             scalar2=ngs[:, l, fc:fc + 1], op0=ALU.add, op1=ALU.mult),
                             reads=["modv", "ngs"], writes=["gvv"])
                    else:
                        P.op("dve", lambda e, l=l, fc=fc, r=r: e.tensor_copy(out=gvv[:, l, fc, r:r + 1], in_=ngs[:, l, fc:fc + 1]), reads=["ngs"], writes=["gvv"])
        P.barrier()
        for l in range(DEPTH + 1):
            if upto < 1:
                break
            emit_Ap(nc, P, C, l, hT, mix_all, wo[l - 1] if l > 0 else None, modv, gvv, ss_loc, ss_all, xn_loc, xn_all, yT)
            if l == DEPTH or l >= nlayers or upto < 2:
                break
            emit_Z(nc, st, P, xn_all, wz[l], zT, zv, zba, PSB=C["PS"])
            if upto < 3:
                break
            emit_MB(nc, P, C, zT, zba, pb[l], ob_scr, gb_scr, ssb=ssb_loc)
            P.allgather(ssb_all[:, :], ssb_loc[:, :], writes=["ssb_all"])
            if upto < 4:
                break
            emit_MA(nc, P, C, zT, pa[l], wri[l], mix_loc[0:64, :])
            if upto < 5:
                break
            emit_MC(nc, P, C, zT, zv, pc[l], mix_loc[128:256, :])
            if upto < 6:
                break
            emit_MBfin(nc, P, C, ob_scr, gb_scr, ssb_all, selB, onB, l, mix_loc[64:128, :])
            P.allgather(mix_all[:, :], mix_loc[:, :], writes=["mix_all"])
            P.barrier()
        if dbg:
            P.barrier()
            P.dma(d_xn[:, :], xn_all[:, :])
            P.dma(d_mix[:, :], mix_all[:, :])
            P.dma(d_h[:, :, :], hT[:, :, :])
        P.finish()
        P.emit()
    return nc


def fused_inputs(i, x, c, ctx, c_ctx, norm_g, w_mod, b_mod, w_in, d, onorm_b, w_out, final_g):
    fs = slice(256 * i, 256 * i + 256)
    hT0 = np.concatenate([ctx[0][:, fs].T, x[0][:, fs].T], axis=1).reshape(2, 128, NT)
    mcols = [p * 2048 + 256 * i + fc * 128 + f for p in range(3) for fc in range(2) for f in range(128)]
    m = {
        "hT0": np.ascontiguousarray(hT0),
        "cc": np.ascontiguousarray(np.stack([c[0], c_ctx])),
        "wmT": np.ascontiguousarray(w_mod[:, :, mcols]),
        "bmT": np.ascontiguousarray(b_mod[:, mcols].reshape(DEPTH, 6, 128).transpose(2, 0, 1)),
        "ngT": np.ascontiguousarray(np.concatenate([norm_g[:, fs], final_g[None, fs]], axis=0).reshape(DEPTH + 1, 2, 128).transpose(2, 0, 1)),
        "selB": np.ascontiguousarray(np.tile((np.arange(8) // 2 == i // 2).astype(np.float32)[:, None], (1, 64))),
        "onB": np.ascontiguousarray(onorm_b[:, 64 * (i % 2):64 * (i % 2) + 64].T),
    }
    for l in range(DEPTH):
        m["wz%d" % l] = np.ascontiguousarray(w_in[l][:, zcols(i)])
        m["wo%d" % l] = np.ascontiguousarray(w_out[l][MIX_PERM][:, fs])
        for k, v in m_params(d, l, i).items():
            m["%s%d" % (k, l)] = v
    return m


def kernel(x, c, ctx, c_ctx, norm_g, w_mod, b_mod, w_in, conv_a_w, conv_a_b, w_ra, b_ra, w_ia, b_ia,
           lam_a, conv_b_w, a_log_b, dt_bias_b, onorm_b, qn_c, kn_c, w_out, final_g):
    f32 = lambda a: np.ascontiguousarray(np.asarray(a, dtype=np.float32))
    d = {k: f32(v) for k, v in dict(conv_a_w=conv_a_w, conv_a_b=conv_a_b, w_ra=w_ra, b_ra=b_ra, w_ia=w_ia, b_ia=b_ia, lam_a=lam_a,
                                    conv_b_w=conv_b_w, a_log_b=a_log_b, dt_bias_b=dt_bias_b, qn_c=qn_c, kn_c=kn_c).items()}
    x, c, ctx, c_ctx, norm_g, w_mod, b_mod, w_in, w_out, onorm_b, final_g = map(f32, (x, c, ctx, c_ctx, norm_g, w_mod, b_mod, w_in, w_out, onorm_b, final_g))
    maps = [fused_inputs(i, x, c, ctx, c_ctx, norm_g, w_mod, b_mod, w_in, d, onorm_b, w_out, final_g) for i in range(NCORE)]
    res = _run(_prog("fused", build_fused), maps)
    yT = np.concatenate([np.asarray(r["yT"]) for r in res], axis=0)
    return np.ascontiguousarray(yT.T)[None].astype(np.float32)
```

```python
import numpy as np
from contextlib import ExitStack
import concourse.bass as bass
import concourse.mybir as mybir
from concourse.bass_utils import run_bass_kernel_spmd

F32 = mybir.dt.float32
BF16 = mybir.dt.bfloat16
F32R = mybir.dt.float32r
I32 = mybir.dt.int32
ALU = mybir.AluOpType
AF = mybir.ActivationFunctionType
AX = mybir.AxisListType

D = 2048
T = 8192
LC = 256
NT = T + LC
DEPTH = 4
NCORE = 8
EPS = 1e-6
D_A, D_B, D_QC, D_KVC = 512, 512, 1024, 256
IN_SIZES = (D_A, D_A, 3 * D_B, D_B, 8, 8, D_QC, D_KVC, D_KVC, D_QC)
IN_OFF = np.concatenate([[0], np.cumsum(IN_SIZES)]).tolist()
D_IN = IN_OFF[-1]

ENGS = ("pe", "act", "dve", "pool", "sp")
_UQ = [0]


def _uq(n):
    _UQ[0] += 1
    return "%s_%d" % (n, _UQ[0])

NDMA = 24
DEN_ACC = False


class Prog:
    def __init__(self, nc, stack):
        self.nc = nc
        self.streams = {e: [] for e in ENGS}
        self.count = {e: 0 for e in ENGS}
        self.sem = {e: stack.enter_context(nc.semaphore("s_" + e)) for e in ENGS}
        self.dsem = [stack.enter_context(nc.semaphore("d_%d" % i)) for i in range(NDMA)]
        self.ndma = 0
        self.ccsem = stack.enter_context(nc.semaphore("s_cc"))
        self.ncc = 0
        self.needed = {e: set() for e in ENGS}
        self.seen = {e: {} for e in ENGS}
        self.lastw = {}
        self.readers = {}

    def _wait(self, eng, ev, war=False):
        kind, who, val = ev
        if kind == "eng":
            if who == eng and (eng in ("pe", "sp") or war):
                return
            key, sem = ("e", who), self.sem[who]
        elif kind == "cc":
            key, sem = ("c", 0), self.ccsem
        else:
            key, sem = ("d", who), self.dsem[who]
        if self.seen[eng].get(key, 0) >= val:
            return
        self.seen[eng][key] = val
        if kind == "eng":
            self.needed[who].add(val)
            self.streams[eng].append(("waite", who, val))
        else:
            self.streams[eng].append(("wait", sem, val))

    def _deps(self, eng, reads, writes):
        best = {}

        def add(ev, war):
            kind, who, val = ev
            if kind == "eng" and who == eng and (eng in ("pe", "sp") or war):
                return
            k = (kind, who)
            if best.get(k, 0) < val:
                best[k] = val

        for k in reads:
            ev = self.lastw.get(k)
            if ev is not None:
                add(ev, False)
        for k in writes:
            ev = self.lastw.get(k)
            if ev is not None:
                add(ev, False)
            for ev in self.readers.get(k, ()):
                add(ev, True)
        for (kind, who), val in best.items():
            self._wait(eng, (kind, who, val), war=False)

    def _record(self, ev, reads, writes):
        for k in reads:
            self.readers.setdefault(k, []).append(ev)
        for k in writes:
            self.lastw[k] = ev
            self.readers[k] = []

    def op(self, eng, fn, reads=(), writes=()):
        ex = [k for k in reads if isinstance(k, str) and k.startswith("PS")]
        if ex:
            reads = [k for k in reads if k not in ex]
            writes = list(writes) + ex
        self._deps(eng, reads, writes)
        self.count[eng] += 1
        ev = ("eng", eng, self.count[eng])
        self.streams[eng].append(("inste", fn, eng, self.count[eng]))
        self._record(ev, reads, writes)

    def dma(self, out, in_, reads=(), writes=(), q="sp", **kw):
        i = self.ndma
        self.ndma += 1
        s = i % NDMA
        tgt = 16 * (i // NDMA + 1)
        if i >= NDMA:
            self._wait(q, ("dma", s, tgt - 16))
        self._deps(q, reads, writes)
        self.streams[q].append(("inst", lambda e: e.dma_start(out=out, in_=in_, **kw), (self.dsem[s], 16)))
        self._record(("dma", s, tgt), reads, writes)

    def allgather(self, out, in_, reads=(), writes=()):
        if self.ncc > 0:
            self._wait("pool", ("cc", 0, self.ncc))
        self._deps("pool", reads, writes)
        self.ncc += 1
        self.streams["pool"].append(("inst", lambda e: e.collective_compute("AllGather", ALU.bypass, replica_groups=[list(range(NCORE))],
                                                                            ins=[in_], outs=[out]), (self.ccsem, 1)))
        self._record(("cc", 0, self.ncc), reads, writes)

    def barrier(self):
        for e in ENGS:
            for f in ENGS:
                if f != e and self.count[f] > 0:
                    self._wait(e, ("eng", f, self.count[f]))
            for s in range(NDMA):
                n = (self.ndma - 1 - s) // NDMA + 1 if self.ndma > s else 0
                if n > 0:
                    self._wait(e, ("dma", s, 16 * n))
            if self.ncc > 0:
                self._wait(e, ("cc", 0, self.ncc))
        self.lastw.clear()
        self.readers.clear()

    def finish(self):
        for s in range(NDMA):
            n = (self.ndma - 1 - s) // NDMA + 1 if self.ndma > s else 0
            if n > 0:
                self._wait("sp", ("dma", s, 16 * n))
        for f in ENGS:
            if f != "sp" and self.count[f] > 0:
                self._wait("sp", ("eng", f, self.count[f]))

    def emit(self):
        nc = self.nc
        rank = {e: {v: i + 1 for i, v in enumerate(sorted(self.needed[e]))} for e in ENGS}
        self.maxsem = {e: len(rank[e]) for e in ENGS}
        with nc.Block() as block:
            for ename, deco in (("sp", block.sync), ("act", block.scalar), ("dve", block.vector),
                                ("pool", block.gpsimd), ("pe", block.tensor)):
                items = self.streams[ename]

                def body(e, items=items):
                    for it in items:
                        if it[0] == "wait":
                            e.wait_ge(it[1], it[2])
                        elif it[0] == "waite":
                            e.wait_ge(self.sem[it[1]], rank[it[1]][it[2]])
                        elif it[0] == "inste":
                            ins = it[1](e)
                            if it[3] in rank[it[2]]:
                                ins.then_inc(self.sem[it[2]], 1)
                        else:
                            ins = it[1](e)
                            ins.then_inc(it[2][0], it[2][1])

                deco(body)


def _ident(P, ident_f32, ident_bf, tmp_key="ident"):
    P.op("pool", lambda e: e.memset(ident_f32[:], 1.0), writes=[tmp_key])
    P.op("pool", lambda e: e.affine_select(out=ident_f32[:], in_=ident_f32[:], pattern=[[-1, 128]],
                                           compare_op=ALU.is_equal, fill=0.0, base=0, channel_multiplier=1),
         reads=[tmp_key], writes=[tmp_key])
    if ident_bf is not None:
        P.op("pool", lambda e: e.tensor_copy(out=ident_bf[:], in_=ident_f32[:]), reads=[tmp_key], writes=[tmp_key + "b"])


def build_mod():
    nc = bass.Bass("TRN2", target_bir_lowering=False)
    cc = nc.dram_tensor("cc", [2, D], F32, kind="ExternalInput").ap()
    wm = nc.dram_tensor("wm", [DEPTH, D, 768], F32, kind="ExternalInput").ap()
    bm = nc.dram_tensor("bm", [DEPTH, 768], F32, kind="ExternalInput").ap()
    mo = nc.dram_tensor("mo", [DEPTH, 2, 768], F32, kind="ExternalOutput").ap()
    with ExitStack() as st:
        P = Prog(nc, st)
        sb = lambda n, s, d: st.enter_context(nc.sbuf_tensor(_uq(n), s, d))
        craw = sb("craw", [128, 16, 2], F32)
        sc = sb("sc", [128, 16, 2], F32)
        wb = [sb("wb%d" % i, [128, 768], F32) for i in range(4)]
        bt = sb("bt", [2, DEPTH, 768], F32)
        ot = sb("ot", [2, DEPTH, 768], F32)
        ps = [st.enter_context(nc.psum_tensor("ps%d" % i, [128, 512], F32)) for i in range(2)]
        for r in range(2):
            P.dma(craw[:, :, r], cc[r, :].rearrange("(k p) -> p k", p=128), writes=["craw%d" % r],
                  allow_slow_non_contiguous=True)
            P.dma(bt[r:r + 1, :, :], bm[None, :, :], writes=["bt%d" % r])
        P.op("act", lambda e: e.activation(out=sc[:], in_=craw[:], func=AF.Silu), reads=["craw0", "craw1"], writes=["sc"])
        n = 0
        for l in range(DEPTH):
            for kc in range(16):
                w = wb[n % 4]
                P.dma(w[:], wm[l, kc * 128:(kc + 1) * 128, :], writes=["wb%d" % (n % 4)])
                for hf in range(2):
                    P.op("pe", lambda e, w=w, hf=hf, kc=kc: e.matmul(out=ps[hf][0:2, 0:384], lhsT=sc[:, kc, :],
                                                                   rhs=w[:, hf * 384:(hf + 1) * 384],
                                                                   start=(kc == 0), stop=(kc == 15)),
                         reads=["sc", "wb%d" % (n % 4)], writes=["PSm%d" % hf])
                n += 1
            for hf in range(2):
                P.op("dve", lambda e, l=l, hf=hf: e.tensor_tensor(out=ot[:, l, hf * 384:(hf + 1) * 384], in0=ps[hf][0:2, 0:384],
                                                                 in1=bt[:, l, hf * 384:(hf + 1) * 384], op=ALU.add),
                     reads=["PSm%d" % hf, "bt0", "bt1"], writes=["ot"])
        P.dma(mo.rearrange("l r c -> r l c"), ot[:], reads=["ot"], writes=["mo"])
        P.finish()
        P.emit()
    return nc


NTA = 1056
A_TILES = [(0, 32)] + [(32 + 128 * k, 128) for k in range(8)]


def build_A(l):
    first, last = (l == 0), (l == DEPTH)
    nc = bass.Bass("TRN2", target_bir_lowering=False)
    din = lambda n, s, d: nc.dram_tensor(n, s, d, kind="ExternalInput").ap()
    dout = lambda n, s, d: nc.dram_tensor(n, s, d, kind="ExternalOutput").ap()
    h_in = din("h", [NTA, D], F32)
    if not first:
        yaT = din("yaT", [512, NTA], BF16)
        obT = din("obT", [512, NTA], F32)
        gbT = din("gbT", [512, NTA], BF16)
        ycT = din("ycT", [1024, NTA], BF16)
        wout = din("wout", [D, D], F32)
        gvec = din("gvec", [2, D], F32)
        onorm = din("onorm", [128, 1], F32)
    if not last:
        ng = din("ng", [1, D], F32)
        ss = din("ss", [4, D], F32)
        hout = dout("hout", [NTA, D], F32)
        xnT = dout("xnT", [D, NTA], BF16)
    else:
        fg = din("fg", [1, D], F32)
        yfin = dout("yfin", [1024, D], F32)
    with ExitStack() as st:
        P = Prog(nc, st)
        sb = lambda n, s, d: st.enter_context(nc.sbuf_tensor(_uq(n), s, d))
        ident = sb("ident", [128, 128], F32)
        identb = sb("identb", [128, 128], BF16)
        ones = sb("ones", [128, 128], F32)
        psf = [st.enter_context(nc.psum_tensor("PSf%d" % i, [128, 512], F32)) for i in range(6)]
        pst = [st.enter_context(nc.psum_tensor("PSt%d" % i, [128, 1024], BF16)) for i in range(2)]
        _ident(P, ident, identb)
        P.op("pool", lambda e: e.memset(ones[:], 1.0), writes=["ones"])
        if not last:
            gv = [sb("gv%d" % r, [128, D], F32) for r in range(2)]
            sh = [sb("sh%d" % r, [128, D], F32) for r in range(2)]
            ngb = sb("ngb", [128, D], F32)
            P.dma(ngb[:], ng[0:1, :].partition_broadcast(128), writes=["ngb"])
            for r in range(2):
                P.dma(gv[r][:], ss[2 * r:2 * r + 1, :].partition_broadcast(128), writes=["gv%d" % r])
                P.dma(sh[r][:], ss[2 * r + 1:2 * r + 2, :].partition_broadcast(128), writes=["sh%d" % r])
                P.op("dve", lambda e, r=r: e.scalar_tensor_tensor(out=gv[r][:], in0=gv[r][:], scalar=1.0, in1=ngb[:],
                                                                  op0=ALU.add, op1=ALU.mult),
                     reads=["gv%d" % r, "ngb"], writes=["gv%d" % r])
        else:
            fgb = sb("fgb", [128, D], F32)
            P.dma(fgb[:], fg[0:1, :].partition_broadcast(128), writes=["fgb"])
        if not first:
            gt = [sb("gt%d" % r, [128, D], F32) for r in range(2)]
            for r in range(2):
                P.dma(gt[r][:], gvec[r:r + 1, :].partition_broadcast(128), writes=["gt%d" % r])
            on = sb("on", [128, 1], F32)
            P.dma(on[:], onorm[:, :], writes=["on"])
            mixT = sb("mixT", [128, 16, NTA], BF16)
            for k in range(4):
                P.dma(mixT[:, k, :], yaT[k * 128:(k + 1) * 128, :], writes=[("mix", k)])
            for k in range(8):
                P.dma(mixT[:, 8 + k, :], ycT[k * 128:(k + 1) * 128, :], writes=[("mix", 8 + k)])
            wbf = sb("wbf", [128, 16, D], BF16)
            st2 = ExitStack()
            sb2 = lambda n, s, d: st2.enter_context(nc.sbuf_tensor(_uq(n), s, d))
            obs = sb2("obs", [128, 4, NTA], F32)
            gbs = sb2("gbs", [128, 4, NTA], BF16)
            sq = sb2("sq", [128, 512], F32)
            rs = sb2("rs", [128, 512], F32)
            tt_ = sb2("tt", [128, 512], F32)
            for hb in range(4):
                P.dma(obs[:, hb, :], obT[hb * 128:(hb + 1) * 128, :], writes=[("obs", hb)])
                P.dma(gbs[:, hb, :], gbT[hb * 128:(hb + 1) * 128, :], writes=[("gbs", hb)])
            for hb in range(4):
                for (c0, cn) in ((0, 512), (512, 512), (1024, 32)):
                    ob = obs[:, hb, c0:c0 + cn]
                    P.op("dve", lambda e, ob=ob, cn=cn: e.tensor_tensor(out=sq[:, :cn], in0=ob, in1=ob, op=ALU.mult),
                         reads=[("obs", hb)], writes=["sq"])
                    P.op("pe", lambda e, cn=cn: e.matmul(out=psf[5][:, :cn], lhsT=ones[:], rhs=sq[:, :cn], start=True, stop=True),
                         reads=["sq", "ones"], writes=["PSf5"])
                    P.op("act", lambda e, cn=cn: e.activation(out=rs[:, :cn], in_=psf[5][:, :cn], func=AF.Sqrt, scale=1.0 / 128, bias=EPS),
                         reads=["PSf5"], writes=["rs"])
                    P.op("dve", lambda e, cn=cn: e.reciprocal(out=rs[:, :cn], in_=rs[:, :cn]), reads=["rs"], writes=["rs"])
                    P.op("dve", lambda e, ob=ob, cn=cn: e.scalar_tensor_tensor(out=tt_[:, :cn], in0=ob, scalar=on[:, 0:1], in1=rs[:, :cn],
                                                                              op0=ALU.mult, op1=ALU.mult),
                         reads=[("obs", hb), "rs", "on"], writes=["tt"])
                    P.op("dve", lambda e, hb=hb, c0=c0, cn=cn: e.tensor_tensor(out=mixT[:, 4 + hb, c0:c0 + cn], in0=tt_[:, :cn],
                                                                              in1=gbs[:, hb, c0:c0 + cn], op=ALU.mult),
                         reads=["tt", ("gbs", hb)], writes=[("mix", 4 + hb)])
            wst = [sb2("wst%d" % i, [128, D], F32) for i in range(2)]
            for kc in range(16):
                P.dma(wst[kc % 2][:], wout[kc * 128:(kc + 1) * 128, :], writes=["wst%d" % (kc % 2)])
                eng = "pool" if kc % 2 == 0 else "act"
                if eng == "pool":
                    P.op("pool", lambda e, kc=kc: e.tensor_copy(out=wbf[:, kc, :], in_=wst[kc % 2][:]),
                         reads=["wst%d" % (kc % 2)], writes=[("wbf", kc)])
                else:
                    P.op("act", lambda e, kc=kc: e.copy(out=wbf[:, kc, :], in_=wst[kc % 2][:]),
                         reads=["wst%d" % (kc % 2)], writes=[("wbf", kc)])
        if not first:
            P.barrier()
            st2.close()
        ht = [sb("ht%d" % i, [128, D], F32) for i in range(2)]
        tmp = sb("tmp", [128, 512], F32)
        ssum = sb("ssum", [128, 2], F32)
        xnf = sb("xnf", [128, D], F32)
        xnb = sb("xnb", [128, D], BF16)
        xT = [sb("xT%d" % i, [128, 16, 128], BF16) for i in range(2)]
        for ti, (r0, np_) in enumerate(A_TILES):
            r = 0 if ti == 0 else 1
            h = ht[ti % 2]
            hk = "ht%d" % (ti % 2)
            P.dma(h[:np_, :], h_in[r0:r0 + np_, :], writes=[hk])
            if not first:
                for ct in range(4):
                    pk = "PSf%d" % (ct % 4)
                    for kc in range(16):
                        P.op("pe", lambda e, ct=ct, kc=kc, r0=r0, np_=np_: e.matmul(
                            out=psf[ct % 4][:np_, :], lhsT=mixT[:, kc, r0:r0 + np_], rhs=wbf[:, kc, ct * 512:(ct + 1) * 512],
                            start=(kc == 0), stop=(kc == 15)),
                            reads=[("mix", kc), ("wbf", kc)], writes=[pk])
                    P.op("dve", lambda e, ct=ct, np_=np_, r=r: e.tensor_tensor(out=tmp[:np_, :], in0=psf[ct % 4][:np_, :],
                                                                              in1=gt[r][:np_, ct * 512:(ct + 1) * 512], op=ALU.mult),
                         reads=[pk, "gt%d" % r], writes=["tmp"])
                    P.op("dve", lambda e, ct=ct, np_=np_, h=h: e.tensor_tensor(out=h[:np_, ct * 512:(ct + 1) * 512],
                                                                              in0=h[:np_, ct * 512:(ct + 1) * 512], in1=tmp[:np_, :], op=ALU.add),
                         reads=["tmp", hk], writes=[hk])
            if not last:
                P.dma(hout[r0:r0 + np_, :], h[:np_, :], reads=[hk], writes=[("hout", ti)])
            elif ti == 0:
                continue
            P.op("act", lambda e, h=h, np_=np_: e.activation(out=xnf[:np_, :], in_=h[:np_, :], func=AF.Square,
                                                             accum_out=ssum[:np_, 0:1]),
                 reads=[hk], writes=["xnf", "ssum"])
            P.op("act", lambda e, np_=np_: e.activation(out=ssum[:np_, 1:2], in_=ssum[:np_, 0:1], func=AF.Sqrt, scale=1.0 / D, bias=EPS),
                 reads=["ssum"], writes=["ssum1"])
            P.op("dve", lambda e, np_=np_: e.reciprocal(out=ssum[:np_, 1:2], in_=ssum[:np_, 1:2]), reads=["ssum1"], writes=["ssum1"])
            if last:
                P.op("dve", lambda e, h=h, np_=np_: e.scalar_tensor_tensor(out=xnf[:np_, :], in0=h[:np_, :], scalar=ssum[:np_, 1:2],
                                                                          in1=fgb[:np_, :], op0=ALU.mult, op1=ALU.mult),
                     reads=[hk, "ssum1", "fgb"], writes=["xnf"])
                P.dma(yfin[r0 - 32:r0 - 32 + np_, :], xnf[:np_, :], reads=["xnf"], writes=[("yfin", ti)])
                continue
            P.op("dve", lambda e, h=h, np_=np_, r=r: e.scalar_tensor_tensor(out=xnf[:np_, :], in0=h[:np_, :], scalar=ssum[:np_, 1:2],
                                                                           in1=gv[r][:np_, :], op0=ALU.mult, op1=ALU.mult),
                 reads=[hk, "ssum1", "gv%d" % r], writes=["xnf"])
            P.op("pool", lambda e, np_=np_, r=r: e.tensor_tensor(out=xnb[:np_, :], in0=xnf[:np_, :], in1=sh[r][:np_, :], op=ALU.add),
                 reads=["xnf", "sh%d" % r], writes=["xnb"])
            x_t = xT[ti % 2]
            xk = "xT%d" % (ti % 2)
            for half in range(2):
                pk = "PSt%d" % half
                for k8 in range(8):
                    kc = half * 8 + k8
                    P.op("pe", lambda e, half=half, k8=k8, kc=kc, np_=np_: e.transpose(
                        out=pst[half][:, k8 * 128:k8 * 128 + np_], in_=xnb[:np_, kc * 128:(kc + 1) * 128], identity=identb[:np_, :np_]),
                        reads=["xnb", "identb"], writes=[pk])
                src = pst[half][:, :].rearrange("p (k t) -> p k t", t=128)[:, :, :np_]
                if half == 0:
                    P.op("act", lambda e, src=src, x_t=x_t, np_=np_: e.copy(out=x_t[:, 0:8, :np_], in_=src), reads=[pk], writes=[xk])
                else:
                    P.op("dve", lambda e, src=src, x_t=x_t, np_=np_: e.tensor_copy(out=x_t[:, 8:16, :np_], in_=src), reads=[pk], writes=[xk])
            P.dma(xnT[:, r0:r0 + np_].rearrange("(k p) t -> p k t", p=128), x_t[:, :, :np_], reads=[xk], writes=[("xnT", ti)])
        P.finish()
        P.emit()
    return nc


NZ = 1028
Z_TILES = [(0, 256)] + [(256 + 512 * k, 512) for k in range(16)]


def zcols(j):
    o = IN_OFF
    hb, hf, kv = j // 2, j % 2, j // 4
    r = lambda a, n: list(range(a, a + n))
    cols = (r(o[0] + 64 * j, 64) + r(o[1] + 64 * j, 64) + r(o[2] + 128 * hb, 128) + r(o[2] + 512 + 128 * hb, 128)
            + r(o[2] + 1024 + 128 * hb + 64 * hf, 64) + r(o[3] + 128 * hb + 64 * hf, 64)
            + r(o[6] + 128 * j, 128) + r(o[7] + 128 * kv, 128) + r(o[9] + 128 * j, 128)
            + r(o[8] + 128 * kv, 128) + [o[4] + hb, o[4] + 4 + hb, o[5] + hb, o[5] + 4 + hb])
    assert len(cols) == NZ
    return cols


def build_Z():
    nc = bass.Bass("TRN2", target_bir_lowering=False)
    xnT = nc.dram_tensor("xnT", [D, NT], BF16, kind="ExternalInput").ap()
    wz = nc.dram_tensor("wz", [D, NZ], F32, kind="ExternalInput").ap()
    zT = nc.dram_tensor("zT", [896, NT], BF16, kind="ExternalOutput").ap()
    zv = nc.dram_tensor("zv", [NT, 128], BF16, kind="ExternalOutput").ap()
    zba = nc.dram_tensor("zba", [NT, 4], F32, kind="ExternalOutput").ap()
    with ExitStack() as st:
        P = Prog(nc, st)
        emit_Z(nc, st, P, xnT, wz, zT, zv, zba)
        P.finish()
        P.emit()
    return nc


def emit_Z(nc, st0, P, xnT, wz, zT, zv, zba, PSB=None):
    with ExitStack() as st:
        sb = lambda n, s, d: st.enter_context(nc.sbuf_tensor(_uq(n), s, d))
        wbf = sb("z_wbf", [128, 16, NZ], BF16)
        wst = [sb("z_wst%d" % i, [128, NZ], F32) for i in range(3)]
        xt = [sb("z_xt%d" % i, [128, 16, 512], BF16) for i in range(2)]
        ob = [sb("z_ob%d" % i, [128, 512], BF16) for i in range(4)]
        ov = [sb("z_ov%d" % i, [128, 128], BF16) for i in range(2)]
        oba = [sb("z_oba%d" % i, [128, 4], F32) for i in range(2)]
        psn = "PSz%d" if PSB is None else "PS%d"
        ps = [st.enter_context(nc.psum_tensor("PSz%d" % i, [128, 512], F32)) for i in range(6)] if PSB is None else PSB
        for kc in range(16):
            P.dma(wst[kc % 3][:], wz[kc * 128:(kc + 1) * 128, :], writes=["z_wst%d" % (kc % 3)])
            if kc % 2 == 0:
                P.op("pool", lambda e, kc=kc: e.tensor_copy(out=wbf[:, kc, :], in_=wst[kc % 3][:]),
                     reads=["z_wst%d" % (kc % 3)], writes=[("z_wbf", kc)])
            else:
                P.op("act", lambda e, kc=kc: e.copy(out=wbf[:, kc, :], in_=wst[kc % 3][:]),
                     reads=["z_wst%d" % (kc % 3)], writes=[("z_wbf", kc)])
        n_ev = 0
        n_tm = 0
        for ti, (t0, tn) in enumerate(Z_TILES):
            x = xt[ti % 2]
            xk = "z_xt%d" % (ti % 2)
            xsrc = colsel(xnT, t0, tn) if isinstance(xnT, tuple) else xnT[:, t0:t0 + tn]
            P.dma(x[:, :, :tn], xsrc.rearrange("(k p) t -> p k t", p=128), writes=[xk])
            for g in range(7):
                pk = psn % (g % 4)
                for kc in range(16):
                    P.op("pe", lambda e, g=g, kc=kc, x=x, tn=tn: e.matmul(out=ps[g % 4][:, :tn], lhsT=wbf[:, kc, g * 128:(g + 1) * 128],
                                                                       rhs=x[:, kc, :tn], start=(kc == 0), stop=(kc == 15)),
                         reads=[("z_wbf", kc), xk], writes=[pk])
                o = ob[n_ev % 4]
                ok = "z_ob%d" % (n_ev % 4)
                if n_ev % 2 == 0:
                    P.op("act", lambda e, g=g, o=o, tn=tn: e.copy(out=o[:, :tn], in_=ps[g % 4][:, :tn]), reads=[pk], writes=[ok])
                else:
                    P.op("dve", lambda e, g=g, o=o, tn=tn: e.tensor_copy(out=o[:, :tn], in_=ps[g % 4][:, :tn]), reads=[pk], writes=[ok])
                P.dma(zT[g * 128:(g + 1) * 128, t0:t0 + tn], o[:, :tn], reads=[ok], writes=[("zT", g, ti)])
                n_ev += 1
            for sub in range(tn // 128):
                pk = psn % (4 + n_tm % 2)
                p_ = ps[4 + n_tm % 2]
                for kc in range(16):
                    P.op("pe", lambda e, kc=kc, x=x, sub=sub, p_=p_: e.matmul(out=p_[:, :132], lhsT=x[:, kc, sub * 128:(sub + 1) * 128],
                                                                           rhs=wbf[:, kc, 896:1028], start=(kc == 0), stop=(kc == 15)),
                         reads=[("z_wbf", kc), xk], writes=[pk])
                o_v, o_b = ov[n_tm % 2], oba[n_tm % 2]
                P.op("act", lambda e, o_v=o_v, p_=p_: e.copy(out=o_v[:], in_=p_[:, 0:128]), reads=[pk], writes=["z_ov%d" % (n_tm % 2)])
                P.op("dve", lambda e, o_b=o_b, p_=p_: e.tensor_copy(out=o_b[:], in_=p_[:, 128:132]), reads=[pk], writes=["z_oba%d" % (n_tm % 2)])
                r0 = t0 + sub * 128
                P.dma(zv[r0:r0 + 128, :], o_v[:], reads=["z_ov%d" % (n_tm % 2)], writes=[("zv", r0)])
                P.dma(zba[r0:r0 + 128, :], o_b[:], reads=["z_oba%d" % (n_tm % 2)], writes=[("zba", r0)])
                n_tm += 1
        P.barrier()


SEGS = [(0, LC), (LC, NT)]
PIECES = [(0, 256, 0)] + [(256 + 2048 * k, 2048, 1) for k in range(4)]
LN1E4_32 = float(np.log(10000.0) / 32.0)
TWO_PI = float(2 * np.pi)


def conv_piece(P, zrow, xin, xk, outs, ok, taps, t0, n, seg, npart, bias=None):
    s0, s1 = SEGS[seg]
    a, b = max(s0, t0 - 2), min(s1, t0 + n + 1)
    P.op("pool", lambda e: e.memset(xin[:npart, :n + 3], 0.0), writes=[xk])
    P.dma(xin[:npart, a - (t0 - 2):b - (t0 - 2)], zrow[:, a:b], reads=[], writes=[xk])
    for k in range(4):
        if k == 0:
            if bias is None:
                P.op("dve", lambda e: e.tensor_scalar(out=outs[:npart, :n], in0=xin[:npart, 0:n], scalar1=taps[:, 0:1], scalar2=None, op0=ALU.mult),
                     reads=[xk], writes=[ok])
            else:
                P.op("dve", lambda e: e.tensor_scalar(out=outs[:npart, :n], in0=xin[:npart, 0:n], scalar1=taps[:, 0:1], scalar2=bias, op0=ALU.mult, op1=ALU.add),
                     reads=[xk], writes=[ok])
        else:
            P.op("dve", lambda e, k=k: e.scalar_tensor_tensor(out=outs[:npart, :n], in0=xin[:npart, k:k + n], scalar=taps[:, k:k + 1],
                                                              in1=outs[:npart, :n], op0=ALU.mult, op1=ALU.add),
                 reads=[xk, ok], writes=[ok])


def emit_MA(nc, P, C, zT, pa, wri, yaT):
    with ExitStack() as st:
        sb = lambda n, s, d: st.enter_context(nc.sbuf_tensor(_uq(n), s, d))
        PS = C["PS"]
        prm = sb("a_prm", [64, 12], F32)
        wf = sb("a_wf", [64, 4, 64], F32)
        wb_ = sb("a_wb", [64, 4, 64], BF16)
        cneg = sb("a_cneg", [64, 2], F32)
        xin = sb("a_xin", [64, 2051], BF16)
        u = sb("a_u", [64, NT], F32)
        ub = sb("a_ub", [64, NT], BF16)
        ga = sb("a_ga", [64, NT], BF16)
        y = [sb("a_y%d" % d, [64, NT], F32) for d in range(2)]
        yo = sb("a_yo", [64, NT], BF16)
        tmps = [[sb("a_t%d_%d" % (i, j), [64, 512], F32) for i in range(4)] for j in range(2)]
        tr = tmps[0][0]
        P.dma(prm[:], pa[:, :], writes=["a_prm"])
        P.dma(wf[:], wri[:, :, :], writes=["a_wf"])
        P.dma(ga[:], zT[64:128, :], writes=["a_ga"])
        P.op("act", lambda e: e.copy(out=wb_[:], in_=wf[:]), reads=["a_wf"], writes=["a_wb"])
        P.op("act", lambda e: e.activation(out=cneg[:], in_=prm[:, 9:11], func=AF.Exp, scale=-1.0), reads=["a_prm"], writes=["a_cneg"])
        P.op("act", lambda e: e.activation(out=cneg[:], in_=cneg[:], func=AF.Ln, bias=1.0), reads=["a_cneg"], writes=["a_cneg"])
        P.op("dve", lambda e: e.tensor_scalar(out=cneg[:], in0=cneg[:], scalar1=-8.0, scalar2=None, op0=ALU.mult), reads=["a_cneg"], writes=["a_cneg"])
        P.op("act", lambda e: e.activation(out=ga[:], in_=ga[:], func=AF.Silu), reads=["a_ga"], writes=["a_ga"])
        for (t0, n, seg) in PIECES:
            conv_piece(P, zT[0:64, :], xin, "a_xin", u[:, t0:t0 + n], ("a_u", t0), prm, t0, n, seg, 64, bias=prm[:, 4:5])
            P.op("act", lambda e, t0=t0, n=n: e.copy(out=ub[:, t0:t0 + n], in_=u[:, t0:t0 + n]), reads=[("a_u", t0)], writes=[("a_ub", t0)])
        P.op("pool", lambda e: e.tensor_copy(out=tr[:, 0:1], in_=tr[:, 0:1]), reads=[("a_u", p[0]) for p in PIECES] + [("a_ub", p[0]) for p in PIECES], writes=["a_uall", "a_tr0"])
        tiles = [(0, 256)] + [(256 + 512 * k, 512) for k in range(16)]
        nt_ = 0
        for d in range(2):
            order = tiles if d == 0 else [tiles[0]] + tiles[:0:-1]
            prev = None
            for (t0, n) in order:
                pb_ = nt_ % 2
                nt_ += 1
                tr, ti_, ta, tb = tmps[pb_]
                ktr, kti, kta, ktb = ("a_tr%d" % pb_, "a_ti%d" % pb_, "a_ta%d" % pb_, "a_tb%d" % pb_)
                p0, p1 = PS[2 * pb_], PS[2 * pb_ + 1]
                kp0, kp1 = "PS%d" % (2 * pb_), "PS%d" % (2 * pb_ + 1)
                P.op("pe", lambda e, t0=t0, n=n, d=d, p0=p0: e.matmul(out=p0[0:64, :n], lhsT=wb_[:, d, :], rhs=ub[:, t0:t0 + n], start=True, stop=True),
                     reads=["a_wb", "a_uall"], writes=[kp0])
                P.op("pe", lambda e, t0=t0, n=n, d=d, p1=p1: e.matmul(out=p1[0:64, :n], lhsT=wb_[:, 2 + d, :], rhs=ub[:, t0:t0 + n], start=True, stop=True),
                     reads=["a_wb", "a_uall"], writes=[kp1])
                P.op("act", lambda e, n=n, d=d, tr=tr, p0=p0: e.activation(out=tr[:, :n], in_=p0[0:64, :n], func=AF.Sigmoid, bias=prm[:, 5 + d:6 + d]),
                     reads=[kp0, "a_prm"], writes=[ktr])
                P.op("act", lambda e, n=n, d=d, ti_=ti_, p1=p1: e.activation(out=ti_[:, :n], in_=p1[0:64, :n], func=AF.Sigmoid, bias=prm[:, 7 + d:8 + d]),
                     reads=[kp1, "a_prm"], writes=[kti])
                P.op("act", lambda e, n=n, d=d, tr=tr, ta=ta: e.activation(out=ta[:, :n], in_=tr[:, :n], func=AF.Exp, scale=cneg[:, d:d + 1]),
                     reads=[ktr, "a_cneg"], writes=[kta])
                P.op("pool", lambda e, n=n, ta=ta, tb=tb: e.tensor_tensor(out=tb[:, :n], in0=ta[:, :n], in1=ta[:, :n], op=ALU.mult), reads=[kta], writes=[ktb])
                P.op("pool", lambda e, n=n, tb=tb: e.tensor_scalar(out=tb[:, :n], in0=tb[:, :n], scalar1=-1.0, scalar2=1.0, op0=ALU.mult, op1=ALU.add),
                     reads=[ktb], writes=[ktb])
                P.op("act", lambda e, n=n, tb=tb: e.activation(out=tb[:, :n], in_=tb[:, :n], func=AF.Sqrt), reads=[ktb], writes=[ktb])
                P.op("dve", lambda e, n=n, t0=t0, ti_=ti_: e.tensor_tensor(out=ti_[:, :n], in0=ti_[:, :n], in1=u[:, t0:t0 + n], op=ALU.mult),
                     reads=[kti, "a_uall"], writes=[kti])
                P.op("dve", lambda e, n=n, tb=tb, ti_=ti_: e.tensor_tensor(out=tb[:, :n], in0=tb[:, :n], in1=ti_[:, :n], op=ALU.mult), reads=[ktb, kti], writes=[ktb])
                yd = y[d]
                if d == 0:
                    init = 0.0 if prev is None else yd[:, prev[0] + prev[1] - 1:prev[0] + prev[1]]
                    P.op("dve", lambda e, n=n, t0=t0, yd=yd, init=init, ta=ta, tb=tb: e.tensor_tensor_scan(out=yd[:, t0:t0 + n], data0=ta[:, :n], data1=tb[:, :n],
                                                                                           initial=init, op0=ALU.mult, op1=ALU.add),
                         reads=[kta, ktb, ("a_y", d)], writes=[("a_y", d)])
                else:
                    init = 0.0 if prev is None else yd[:, prev[0]:prev[0] + 1]
                    P.op("dve", lambda e, n=n, t0=t0, yd=yd, init=init, ta=ta, tb=tb: e.tensor_tensor_scan(out=yd[:, t0:t0 + n][:, ::-1], data0=ta[:, :n][:, ::-1],
                                                                                           data1=tb[:, :n][:, ::-1], initial=init, op0=ALU.mult, op1=ALU.add),
                         reads=[kta, ktb, ("a_y", d)], writes=[("a_y", d)])
                prev = (t0, n)
        for (t0, n, seg) in PIECES:
            P.op("pool", lambda e, t0=t0, n=n: e.tensor_tensor(out=y[0][:, t0:t0 + n], in0=y[0][:, t0:t0 + n], in1=y[1][:, t0:t0 + n], op=ALU.add),
                 reads=[("a_y", 0), ("a_y", 1)], writes=[("a_y", 0)])
            P.op("dve", lambda e, t0=t0, n=n: e.tensor_tensor(out=yo[:, t0:t0 + n], in0=y[0][:, t0:t0 + n], in1=ga[:, t0:t0 + n], op=ALU.mult),
                 reads=[("a_y", 0), "a_ga"], writes=["a_out"])
        P.dma(yaT[:, :], yo[:, :], reads=["a_out"], writes=["yaT"])
        P.barrier()


def emit_MC(nc, P, C, zT, zv, pc, ycT):
    with ExitStack() as st:
        sb = lambda n, s, d: st.enter_context(nc.sbuf_tensor(_uq(n), s, d))
        PS, identf, onesb = C["PS"], C["identf"], C["onesb"]
        prm = sb("c_prm", [128, 2], F32)
        P.dma(prm[:], pc[:, :], writes=["c_prm"])
        cosT = sb("c_cos", [128, T], BF16)
        sinT = sb("c_sin", [128, T], BF16)
        rm = sb("c_rm", [128, 128], BF16)
        with ExitStack() as st2:
            sb2 = lambda n, s, d: st2.enter_context(nc.sbuf_tensor(_uq(n), s, d))
            CW = 2048
            pos = sb2("c_pos", [128, CW], F32)
            ang = sb2("c_ang", [128, CW], F32)
            ki = sb2("c_ki", [128, CW], I32)
            rr = sb2("c_rr", [128, CW], F32)
            tt = sb2("c_tt", [128, CW], F32)
            fi = sb2("c_fi", [128, 1], F32)
            for q in range(4):
                P.op("pool", lambda e, q=q: e.iota(out=fi[32 * q:32 * q + 32, :], pattern=[[0, 1]], base=0, channel_multiplier=1,
                                                   allow_small_or_imprecise_dtypes=True), writes=[("c_fi", q)])
            P.op("act", lambda e: e.activation(out=fi[:], in_=fi[:], func=AF.Exp, scale=-LN1E4_32), reads=[("c_fi", q) for q in range(4)], writes=["c_fi"])
            PI = float(np.pi)
            for c in range(T // CW):
                for q in range(4):
                    pat = [[1, CW // 64], [0, 64]] if q % 2 == 0 else [[0, CW // 64], [1, 64]]
                    base = (CW // 64) * c if q % 2 == 0 else 0
                    P.op("pool", lambda e, q=q, pat=pat, base=base: e.iota(out=pos[32 * q:32 * q + 32, :], pattern=pat, base=base, channel_multiplier=0,
                                                                           allow_small_or_imprecise_dtypes=True), writes=[("c_pos", q)])
                for which, (dst, shift) in enumerate(((sinT, 0.0), (cosT, PI / 2))):
                    P.op("dve", lambda e, shift=shift: e.tensor_scalar(out=ang[:], in0=pos[:], scalar1=fi[:, 0:1], scalar2=shift, op0=ALU.mult, op1=ALU.add),
                         reads=[("c_pos", q) for q in range(4)] + ["c_fi"], writes=["c_ang"])
                    P.op("dve", lambda e: e.tensor_scalar(out=ki[:], in0=ang[:], scalar1=1.0 / TWO_PI, scalar2=None, op0=ALU.mult), reads=["c_ang"], writes=["c_ki"])
                    P.op("dve", lambda e: e.tensor_copy(out=rr[:], in_=ki[:]), reads=["c_ki"], writes=["c_rr"])
                    P.op("dve", lambda e: e.scalar_tensor_tensor(out=rr[:], in0=rr[:], scalar=-TWO_PI, in1=ang[:], op0=ALU.mult, op1=ALU.add),
                         reads=["c_rr", "c_ang"], writes=["c_rr"])
                    P.op("dve", lambda e: e.tensor_scalar(out=tt[:], in0=rr[:], scalar1=PI, scalar2=TWO_PI, op0=ALU.is_gt, op1=ALU.mult), reads=["c_rr"], writes=["c_tt"])
                    P.op("dve", lambda e: e.tensor_tensor(out=rr[:], in0=rr[:], in1=tt[:], op=ALU.subtract), reads=["c_rr", "c_tt"], writes=["c_rr"])
                    P.op("dve", lambda e: e.tensor_scalar(out=tt[:], in0=rr[:], scalar1=-PI, scalar2=TWO_PI, op0=ALU.is_lt, op1=ALU.mult), reads=["c_rr"], writes=["c_tt"])
                    P.op("dve", lambda e: e.tensor_tensor(out=rr[:], in0=rr[:], in1=tt[:], op=ALU.add), reads=["c_rr", "c_tt"], writes=["c_rr"])
                    P.op("dve", lambda e: e.tensor_scalar(out=rr[:], in0=rr[:], scalar1=PI, scalar2=-PI, op0=ALU.min, op1=ALU.max), reads=["c_rr"], writes=["c_rr"])
                    P.op("act", lambda e, dst=dst, c=c: e.activation(out=dst[:, c * CW:(c + 1) * CW], in_=rr[:], func=AF.Sin), reads=["c_rr"], writes=[("c_tab", which, c)])
            P.barrier()
        P.op("dve", lambda e: e.tensor_scalar(out=rm[:, 0:64], in0=identf[:, 64:128], scalar1=-1.0, scalar2=None, op0=ALU.mult), reads=["ident"], writes=["c_rm"])
        P.op("dve", lambda e: e.tensor_copy(out=rm[:, 64:128], in_=identf[:, 0:64]), reads=["ident"], writes=["c_rm"])
        qT = sb("c_qT", [128, NT], BF16)
        kT = sb("c_kT", [128, NT], BF16)
        sg = sb("c_sg", [128, NT], BF16)
        V = sb("c_V", [128, 66, 128], BF16)
        zin = [sb("c_zin%d" % i, [128, 512], BF16) for i in range(2)]
        sq = sb("c_sq", [128, 512], BF16)
        rs = sb("c_rs", [128, 512], F32)
        xn = sb("c_xn", [128, 512], BF16)
        t1 = sb("c_t1", [128, 512], F32)
        t2 = sb("c_t2", [128, 512], F32)
        P.dma(sg[:], zT[768:896, :], writes=["c_sg"])
        P.op("act", lambda e: e.activation(out=sg[:], in_=sg[:], func=AF.Silu), reads=["c_sg"], writes=["c_sg"])
        for half in range(2):
            P.dma(V[:, 33 * half:33 * half + 33, :], zv[4224 * half:4224 * (half + 1), :].rearrange("(k p) d -> p k d", p=128), writes=[("c_V", half)])
        tiles = [(0, 256)] + [(256 + 512 * k, 512) for k in range(16)]
        n_in = 0
        for which, (row0, dst, pcol) in enumerate(((512, qT, 0), (640, kT, 1))):
            for (t0, n) in tiles:
                zi = zin[n_in % 2]
                zk = "c_zin%d" % (n_in % 2)
                n_in += 1
                P.dma(zi[:, :n], zT[row0:row0 + 128, t0:t0 + n], writes=[zk])
                P.op("pool", lambda e, zi=zi, n=n: e.tensor_tensor(out=sq[:, :n], in0=zi[:, :n], in1=zi[:, :n], op=ALU.mult), reads=[zk], writes=["c_sq"])
                P.op("pe", lambda e, n=n: e.matmul(out=PS[7][:, :n], lhsT=onesb[:], rhs=sq[:, :n], start=True, stop=True), reads=["c_sq", "onesb"], writes=["PS7"])
                P.op("act", lambda e, n=n: e.activation(out=rs[:, :n], in_=PS[7][:, :n], func=AF.Sqrt, scale=1.0 / 128, bias=C["epsb"][:, 0:1]),
                     reads=["PS7", "epsb"], writes=["c_rs"])
                P.op("dve", lambda e, n=n: e.reciprocal(out=rs[:, :n], in_=rs[:, :n]), reads=["c_rs"], writes=["c_rs"])
                if t0 < LC:
                    P.op("dve", lambda e, zi=zi, n=n, t0=t0, dst=dst, pcol=pcol: e.scalar_tensor_tensor(
                        out=dst[:, t0:t0 + n], in0=zi[:, :n], scalar=prm[:, pcol:pcol + 1], in1=rs[:, :n], op0=ALU.mult, op1=ALU.mult),
                        reads=[zk, "c_rs", "c_prm"], writes=[("c_qk", which, t0)])
                    continue
                P.op("dve", lambda e, zi=zi, n=n, pcol=pcol: e.scalar_tensor_tensor(out=xn[:, :n], in0=zi[:, :n], scalar=prm[:, pcol:pcol + 1],
                                                                                  in1=rs[:, :n], op0=ALU.mult, op1=ALU.mult),
                     reads=[zk, "c_rs", "c_prm"], writes=["c_xn"])
                P.op("pe", lambda e, n=n: e.matmul(out=PS[6][:, :n], lhsT=rm[:], rhs=xn[:, :n], start=True, stop=True), reads=["c_xn", "c_rm"], writes=["PS6"])
                P.op("pool", lambda e, n=n, t0=t0: e.tensor_tensor(out=t1[:, :n], in0=xn[:, :n], in1=cosT[:, t0 - LC:t0 - LC + n], op=ALU.mult),
                     reads=["c_xn"], writes=["c_t1"])
                P.op("dve", lambda e, n=n, t0=t0: e.tensor_tensor(out=t2[:, :n], in0=PS[6][:, :n], in1=sinT[:, t0 - LC:t0 - LC + n], op=ALU.mult),
                     reads=["PS6"], writes=["c_t2"])
                P.op("dve", lambda e, n=n, t0=t0, dst=dst: e.tensor_tensor(out=dst[:, t0:t0 + n], in0=t1[:, :n], in1=t2[:, :n], op=ALU.add),
                     reads=["c_t1", "c_t2"], writes=[("c_qk", which, t0)])
        P.op("pool", lambda e: e.tensor_copy(out=t1[:, 0:1], in_=t1[:, 0:1]),
             reads=[("c_qk", w, t[0]) for w in range(2) for t in tiles] + [("c_V", 0), ("c_V", 1)], writes=["c_qkv", "c_t1"])
        pt = [sb("c_pt%d" % i, [128, 512], BF16) for i in range(3)]
        rd = sb("c_rd", [128, 512], F32)
        accd = sb("c_accd", [128, 512], F32)
        accp = sb("c_accp", [128, 512], F32)
        of = sb("c_of", [128, 512], F32)
        yo = [sb("c_yo%d" % i, [128, 512], BF16) for i in range(2)]
        scl = float(128 ** -0.5)
        for qi, (q0, qn_) in enumerate(tiles):
            kts = list(range(2)) if qi == 0 else list(range(66))
            nk = len(kts)
            ob, db = 3 + qi % 2, 5
            def smm(i, kts=kts, q0=q0, qn_=qn_):
                kt = kts[i]
                P.op("pe", lambda e, kt=kt, i=i: e.matmul(out=PS[i % 3][:, :qn_], lhsT=kT[:, kt * 128:(kt + 1) * 128], rhs=qT[:, q0:q0 + qn_], start=True, stop=True),
                     reads=["c_qkv"], writes=["PS%d" % (i % 3)])
            smm(0)
            if nk > 1:
                smm(1)
            for i in range(nk):
                kt = kts[i]
                p_ = pt[i % 3]
                P.op("act", lambda e, i=i, p_=p_, qn_=qn_: e.activation(out=p_[:, :qn_], in_=PS[i % 3][:, :qn_], func=AF.Exp, scale=scl),
                     reads=["PS%d" % (i % 3)], writes=["c_pt%d" % (i % 3)])
                if i + 2 < nk:
                    smm(i + 2)
                P.op("pe", lambda e, kt=kt, i=i, p_=p_, qn_=qn_, ob=ob, nk=nk: e.matmul(out=PS[ob][:, :qn_], lhsT=V[:, kt, :], rhs=p_[:, :qn_], start=(i == 0), stop=(i == nk - 1)),
                     reads=["c_pt%d" % (i % 3), "c_qkv"], writes=["PS%d" % ob])
                if not DEN_ACC:
                    P.op("pe", lambda e, i=i, p_=p_, qn_=qn_, db=db, nk=nk: e.matmul(out=PS[db][:, :qn_], lhsT=onesb[:], rhs=p_[:, :qn_], start=(i == 0), stop=(i == nk - 1)),
                         reads=["c_pt%d" % (i % 3), "onesb"], writes=["PS%d" % db])
                    continue
                on_pool = (i % 3 == 2)
                acc, ak, eng_ = (accp, "c_accp", "pool") if on_pool else (accd, "c_accd", "dve")
                if i == (2 if on_pool else 0):
                    P.op(eng_, lambda e, acc=acc, p_=p_, qn_=qn_: e.tensor_copy(out=acc[:, :qn_], in_=p_[:, :qn_]), reads=["c_pt%d" % (i % 3)], writes=[ak])
                else:
                    P.op(eng_, lambda e, acc=acc, p_=p_, qn_=qn_: e.tensor_tensor(out=acc[:, :qn_], in0=acc[:, :qn_], in1=p_[:, :qn_], op=ALU.add),
                         reads=["c_pt%d" % (i % 3), ak], writes=[ak])
            if DEN_ACC and nk > 2:
                P.op("dve", lambda e, qn_=qn_: e.tensor_tensor(out=accd[:, :qn_], in0=accd[:, :qn_], in1=accp[:, :qn_], op=ALU.add), reads=["c_accd", "c_accp"], writes=["c_accd"])
            if DEN_ACC:
                P.op("pe", lambda e, qn_=qn_, db=db: e.matmul(out=PS[db][:, :qn_], lhsT=C["onesf"][:], rhs=accd[:, :qn_], start=True, stop=True),
                     reads=["c_accd", "onesf"], writes=["PS%d" % db])
            P.op("dve", lambda e, qn_=qn_, db=db: e.reciprocal(out=rd[:, :qn_], in_=PS[db][:, :qn_]), reads=["PS%d" % db], writes=["c_rd"])
            P.op("dve", lambda e, qn_=qn_, ob=ob: e.tensor_tensor(out=of[:, :qn_], in0=PS[ob][:, :qn_], in1=rd[:, :qn_], op=ALU.mult), reads=["PS%d" % ob, "c_rd"], writes=["c_of"])
            y_ = yo[qi % 2]
            P.op("pool", lambda e, y_=y_, qn_=qn_, q0=q0: e.tensor_tensor(out=y_[:, :qn_], in0=of[:, :qn_], in1=sg[:, q0:q0 + qn_], op=ALU.mult),
                 reads=["c_of", "c_sg"], writes=["c_yo%d" % (qi % 2)])
            P.dma(ycT[:, q0:q0 + qn_], y_[:, :qn_], reads=["c_yo%d" % (qi % 2)], writes=[("ycT", qi)])
        P.barrier()


def build_M(parts=("A", "B", "C")):
    nc = bass.Bass("TRN2", target_bir_lowering=False)
    din = lambda n, s, d: nc.dram_tensor(n, s, d, kind="ExternalInput").ap()
    dout = lambda n, s, d: nc.dram_tensor(n, s, d, kind="ExternalOutput").ap()
    zT = din("zT", [896, NT], BF16)
    zv = din("zv", [NT, 128], BF16)
    zba = din("zba", [NT, 4], F32)
    pa = din("pa", [64, 12], F32)
    wri = din("wri", [64, 4, 64], F32)
    pb = din("pb", [128, 16], F32)
    pc = din("pc", [128, 2], F32)
    yaT = dout("yaT", [64, NT], BF16)
    obT = dout("obT", [64, NT], F32)
    gbT = dout("gbT", [64, NT], BF16)
    ycT = dout("ycT", [128, NT], BF16)
    with ExitStack() as st:
        P = Prog(nc, st)
        C = mixer_consts(nc, st, P)
        if "A" in parts:
            emit_MA(nc, P, C, zT, pa, wri, yaT)
        if "B" in parts:
            emit_MB(nc, P, C, zT, zba, pb, obT, gbT)
        if "C" in parts:
            emit_MC(nc, P, C, zT, zv, pc, ycT)
        P.finish()
        P.emit()
    return nc


def mixer_consts(nc, st, P):
    sb = lambda n, s, d: st.enter_context(nc.sbuf_tensor(_uq(n), s, d))
    C = {}
    C["PS"] = [st.enter_context(nc.psum_tensor("PS%d" % i, [128, 512], F32)) for i in range(8)]
    C["identf"] = sb("identf", [128, 128], F32)
    C["identb"] = sb("identb", [128, 128], BF16)
    C["onesf"] = sb("onesf", [128, 128], F32)
    C["onesb"] = sb("onesb", [128, 128], BF16)
    C["epsb"] = sb("epsb", [128, 1], F32)
    _ident(P, C["identf"], C["identb"])
    P.op("pool", lambda e: e.memset(C["onesf"][:], 1.0), writes=["onesf"])
    P.op("pool", lambda e: e.memset(C["onesb"][:], 1.0), writes=["onesb"])
    P.op("pool", lambda e: e.memset(C["epsb"][:], EPS), writes=["epsb"])
    return C


def m_params(d, l, j):
    hb, hf = j // 2, j % 2
    sl = slice(64 * j, 64 * j + 64)
    pa = np.zeros((64, 12), np.float32)
    pa[:, 0:4] = d["conv_a_w"][l][:, sl].T
    pa[:, 4] = d["conv_a_b"][l][sl]
    pa[:, 5:7] = d["b_ra"][l][:, sl].T
    pa[:, 7:9] = d["b_ia"][l][:, sl].T
    pa[:, 9:11] = d["lam_a"][l][:, sl].T
    wri = np.stack([d["w_ra"][l][0, j], d["w_ra"][l][1, j], d["w_ia"][l][0, j], d["w_ia"][l][1, j]], axis=1)
    cw = d["conv_b_w"][l]
    pb = np.zeros((128, 16), np.float32)
    pb[:, 0:4] = cw[:, 128 * hb:128 * hb + 128].T
    pb[:, 4:8] = cw[:, 512 + 128 * hb:512 + 128 * hb + 128].T
    pb[0:64, 8:12] = cw[:, 1024 + 128 * hb + 64 * hf:1024 + 128 * hb + 64 * hf + 64].T
    pb[:, 12:14] = d["a_log_b"][l][:, hb][None, :]
    pb[:, 14:16] = d["dt_bias_b"][l][:, hb][None, :]
    pc = np.stack([d["qn_c"][l], d["kn_c"][l]], axis=1)
    return {"pa": pa, "wri": np.ascontiguousarray(wri.astype(np.float32)), "pb": pb, "pc": np.ascontiguousarray(pc.astype(np.float32))}


def emit_MB(nc, P, C, zT, zba, pb, obT, gbT, ssb=None):
    NCH = NT // 64
    WINS = [(0, 4)] + [(4 + 8 * k, 8) for k in range(16)]
    with ExitStack() as st:
        sb = lambda n, s, d: st.enter_context(nc.sbuf_tensor(_uq(n), s, d))
        PS, identf, identb, onesf, onesb, epsb = C["PS"], C["identf"], C["identb"], C["onesf"], C["onesb"], C["epsb"]
        prm = sb("b_prm", [128, 16], F32)
        P.dma(prm[:], pb[:, :], writes=["b_prm"])
        qT = sb("b_qT", [128, NT], BF16)
        kT = sb("b_kT", [128, NT], BF16)
        vT = sb("b_vT", [64, NT], BF16)
        oT = sb("b_oT", [64, NT], F32)
        with ExitStack() as st2:
            sb2 = lambda n, s, d: st2.enter_context(nc.sbuf_tensor(_uq(n), s, d))
            gbt = sb2("b_gb", [64, NT], BF16)
            P.dma(gbt[:], zT[448:512, :], writes=["b_gb"])
            P.op("act", lambda e: e.activation(out=gbt[:], in_=gbt[:], func=AF.Silu), reads=["b_gb"], writes=["b_gb"])
            P.dma(gbT[:, :], gbt[:], reads=["b_gb"], writes=["gbT"])
            xins = [sb2("b_xin%d" % i, [128, 2051], BF16) for i in range(2)]
            cvs = [sb2("b_cv%d" % i, [128, 2048], F32) for i in range(2)]
            sls = [sb2("b_sl%d" % i, [128, 2048], F32) for i in range(2)]
            sqs = [sb2("b_sq%d" % i, [128, 512], BF16) for i in range(2)]
            rss = [sb2("b_rs%d" % i, [128, 512], F32) for i in range(2)]
            npc = 0
            nsq = 0
            for which, (row0, npart, tc, dst) in enumerate(((128, 128, 0, qT), (256, 128, 4, kT), (384, 64, 8, vT))):
                for (t0, n, seg) in PIECES:
                    pp = npc % 2
                    npc += 1
                    xin, cv, sl = xins[pp], cvs[pp], sls[pp]
                    kcv, ksl = "b_cv%d" % pp, "b_sl%d" % pp
                    conv_piece(P, zT[row0:row0 + npart, :], xin, "b_xin%d" % pp, cv, kcv, prm[:npart, tc:tc + 4], t0, n, seg, npart)
                    if which == 2:
                        P.op("act", lambda e, t0=t0, n=n, cv=cv: e.activation(out=vT[:, t0:t0 + n], in_=cv[0:64, :n], func=AF.Silu),
                             reads=[kcv], writes=[("b_qkv", which, t0)])
                        continue
                    P.op("act", lambda e, n=n, cv=cv, sl=sl: e.activation(out=sl[:, :n], in_=cv[:, :n], func=AF.Silu), reads=[kcv], writes=[ksl])
                    for s0 in range(0, n, 512):
                        sn = min(512, n - s0)
                        qq = nsq % 2
                        nsq += 1
                        sq, rs, pbk = sqs[qq], rss[qq], PS[6 + qq]
                        ksq, krs, kpb = "b_sq%d" % qq, "b_rs%d" % qq, "PS%d" % (6 + qq)
                        P.op("pool", lambda e, s0=s0, sn=sn, sq=sq, sl=sl: e.tensor_tensor(out=sq[:, :sn], in0=sl[:, s0:s0 + sn], in1=sl[:, s0:s0 + sn], op=ALU.mult),
                             reads=[ksl], writes=[ksq])
                        P.op("pe", lambda e, sn=sn, sq=sq, pbk=pbk: e.matmul(out=pbk[:, :sn], lhsT=onesb[:], rhs=sq[:, :sn], start=True, stop=True),
                             reads=[ksq, "onesb"], writes=[kpb])
                        P.op("act", lambda e, sn=sn, rs=rs, pbk=pbk: e.activation(out=rs[:, :sn], in_=pbk[:, :sn], func=AF.Sqrt, bias=epsb[:, 0:1]),
                             reads=[kpb, "epsb"], writes=[krs])
                        P.op("dve", lambda e, sn=sn, rs=rs: e.reciprocal(out=rs[:, :sn], in_=rs[:, :sn]), reads=[krs], writes=[krs])
                        cst = float(128 ** -0.5) if which == 0 else 1.0
                        P.op("dve", lambda e, s0=s0, sn=sn, t0=t0, dst=dst, cst=cst, sl=sl, rs=rs: e.scalar_tensor_tensor(
                            out=dst[:, t0 + s0:t0 + s0 + sn], in0=sl[:, s0:s0 + sn], scalar=cst, in1=rs[:, :sn], op0=ALU.mult, op1=ALU.mult),
                            reads=[ksl, krs], writes=[("b_qkv", which, t0)])
            P.barrier()
        gin = sb("b_gin", [64, NCH, 4], F32)
        for q in range(4):
            P.dma(gin[:, 33 * q:33 * q + 33, :], zba[2112 * q:2112 * (q + 1), :].rearrange("(c p) f -> p c f", p=64), writes=[("b_gin", q)])
        P.op("pool", lambda e: e.tensor_copy(out=gin[:, 0, 0:1], in_=gin[:, 0, 0:1]), reads=[("b_gin", q) for q in range(4)], writes=["b_gin"])
        na = sb("b_na", [128, 2], F32)
        P.op("act", lambda e: e.activation(out=na[:], in_=prm[:, 12:14], func=AF.Exp), reads=["b_prm"], writes=["b_na"])
        P.op("dve", lambda e: e.tensor_scalar(out=na[:], in0=na[:], scalar1=-1.0, scalar2=None, op0=ALU.mult), reads=["b_na"], writes=["b_na"])
        beta = sb("b_beta", [64, 2, NCH], F32)
        nbeta = sb("b_nbeta", [64, 2, NCH], F32)
        gg = sb("b_gg", [64, 2, NCH], F32)
        G = sb("b_G", [64, 2, NCH], F32)
        nG = sb("b_nG", [64, 2, NCH], F32)
        bexpG = sb("b_bexpG", [64, 2, NCH], F32)
        etail = sb("b_etail", [64, 2, NCH], F32)
        EGT = sb("b_EGT", [128, 2, NCH], F32)
        mc = sb("b_mc", [64, 2, 64], F32)
        nms = sb("b_nms", [64, 2, 64], F32)
        nmt = sb("b_nmt", [64, 2, 64], F32)
        for d in range(2):
            sgn = 1 if d == 0 else -1
            P.op("pool", lambda e, d=d: e.memset(mc[:, d, :], 1.0), writes=[("b_mc", d)])
            P.op("pool", lambda e, d=d, sgn=sgn: e.affine_select(out=mc[:, d, :], in_=mc[:, d, :], pattern=[[sgn, 64]], compare_op=ALU.is_ge, fill=0.0,
                                                               base=0, channel_multiplier=-sgn), reads=[("b_mc", d)], writes=[("b_mc", d)])
            P.op("pool", lambda e, d=d: e.memset(nms[:, d, :], 0.0), writes=[("b_nms", d)])
            P.op("pool", lambda e, d=d, sgn=sgn: e.affine_select(out=nms[:, d, :], in_=nms[:, d, :], pattern=[[-sgn, 64]], compare_op=ALU.is_gt, fill=-1.0e5,
                                                               base=0, channel_multiplier=sgn), reads=[("b_nms", d)], writes=[("b_nms", d)])
            P.op("pool", lambda e, d=d: e.memset(nmt[:, d, :], 0.0), writes=[("b_nmt", d)])
            P.op("pool", lambda e, d=d, sgn=sgn: e.affine_select(out=nmt[:, d, :], in_=nmt[:, d, :], pattern=[[sgn, 64]], compare_op=ALU.is_ge, fill=-1.0e5,
                                                               base=0, channel_multiplier=-sgn), reads=[("b_nmt", d)], writes=[("b_nmt", d)])
            P.op("act", lambda e, d=d: e.activation(out=beta[:, d, :], in_=gin[:, :, d], func=AF.Sigmoid), reads=["b_gin"], writes=[("b_beta", d)])
            P.op("dve", lambda e, d=d: e.tensor_scalar(out=nbeta[:, d, :], in0=beta[:, d, :], scalar1=-1.0, scalar2=None, op0=ALU.mult),
                 reads=[("b_beta", d)], writes=[("b_nbeta", d)])
            P.op("act", lambda e, d=d: e.activation(out=gg[:, d, :], in_=gin[:, :, 2 + d], func=AF.Exp, bias=prm[0:64, 14 + d:15 + d]),
                 reads=["b_gin", "b_prm"], writes=[("b_gg", d)])
            P.op("act", lambda e, d=d: e.activation(out=gg[:, d, :], in_=gg[:, d, :], func=AF.Ln, bias=1.0), reads=[("b_gg", d)], writes=[("b_gg", d)])
            P.op("dve", lambda e, d=d: e.tensor_scalar(out=gg[:, d, :], in0=gg[:, d, :], scalar1=na[0:64, d:d + 1], scalar2=None, op0=ALU.mult),
                 reads=[("b_gg", d), "b_na"], writes=[("b_gg", d)])
            P.op("pe", lambda e, d=d: e.matmul(out=PS[7][0:64, 0:NCH], lhsT=mc[:, d, :], rhs=gg[:, d, :], start=True, stop=True),
                 reads=[("b_mc", d), ("b_gg", d)], writes=["PS7"])
            P.op("pe", lambda e, d=d: e.matmul(out=PS[6][:, 0:NCH], lhsT=onesf[0:64, :], rhs=gg[:, d, :], start=True, stop=True),
                 reads=["onesf", ("b_gg", d)], writes=["PS6"])
            P.op("act", lambda e, d=d: e.copy(out=G[:, d, :], in_=PS[7][0:64, 0:NCH]), reads=["PS7"], writes=[("b_G", d)])
            P.op("dve", lambda e, d=d: e.tensor_scalar(out=nG[:, d, :], in0=G[:, d, :], scalar1=-1.0, scalar2=None, op0=ALU.mult),
                 reads=[("b_G", d)], writes=[("b_nG", d)])
            P.op("act", lambda e, d=d: e.activation(out=EGT[:, d, :], in_=PS[6][:, 0:NCH], func=AF.Exp), reads=["PS6"], writes=[("b_EGT", d)])
            P.op("dve", lambda e, d=d: e.tensor_tensor(out=etail[:, d, :], in0=PS[6][0:64, 0:NCH], in1=G[:, d, :], op=ALU.subtract),
                 reads=["PS6", ("b_G", d)], writes=[("b_etail", d)])
            P.op("act", lambda e, d=d: e.activation(out=etail[:, d, :], in_=etail[:, d, :], func=AF.Exp), reads=[("b_etail", d)], writes=[("b_etail", d)])
            P.op("act", lambda e, d=d: e.activation(out=bexpG[:, d, :], in_=G[:, d, :], func=AF.Exp), reads=[("b_G", d)], writes=[("b_bexpG", d)])
            P.op("dve", lambda e, d=d: e.tensor_tensor(out=bexpG[:, d, :], in0=bexpG[:, d, :], in1=beta[:, d, :], op=ALU.mult),
                 reads=[("b_bexpG", d), ("b_beta", d)], writes=[("b_bexpG", d)])
        id64 = identf[0:64, 0:64]
        bcl = lambda ap, n, w: ap.unsqueeze(2).to_broadcast([64, n, w])
        bcm = lambda ap, n: ap.unsqueeze(1).to_broadcast([64, n, 64])
        fl = lambda t, n: t[:, :n, :].rearrange("p c w -> p (c w)")
        rr = lambda ap: ap.bitcast(F32R)
        dmy = sb("b_dmy", [64, 2], F32)
        P.op("pool", lambda e: e.memset(dmy[:], 0.0),
             reads=[("b_qkv", w, p[0]) for w in range(3) for p in PIECES], writes=["b_qkvall"])
        for (c0, n) in WINS:
            P.op("pool", lambda e, c0=c0, n=n: e.memset(oT[:, c0 * 64:(c0 + n) * 64], 0.0), writes=[("b_oT", c0)])

        def chain(d):
            f3 = lambda nm, p, w: sb("%s_%d" % (nm, d), [p, 8, w], F32)
            K_ = lambda nm: "%s_%d" % (nm, d)
            dG, X, XT, Nm = f3("b_dG", 64, 64), f3("b_X", 64, 64), f3("b_XT", 64, 64), f3("b_Nm", 64, 64)
            EGB = f3("b_EGB", 128, 64)
            Aa = [f3("b_A%d" % i, 64, 64) for i in range(2)]
            At = [f3("b_At%d" % i, 64, 64) for i in range(2)]
            Rr = [f3("b_R%d" % i, 64, 64) for i in range(2)]
            ktok, kbg, ktail = f3("b_ktok", 64, 128), f3("b_kbg", 64, 128), f3("b_ktail", 64, 128)
            vb, u_, wT, qkT, qhT = f3("b_vb", 64, 64), f3("b_u", 64, 64), f3("b_wT", 128, 64), f3("b_qkT", 64, 64), f3("b_qhT", 128, 64)
            S = sb("b_S_%d" % d, [128, 64], F32)
            S_r = sb("b_Sr_%d" % d, [128, 64], F32)
            vnew = sb("b_vnew_%d" % d, [64, 64], F32)
            i0, i1, i2, io = 4 * d, 4 * d + 1, 4 * d + 2, 4 * d + 3
            B0, B1, B2, BO = PS[i0], PS[i1], PS[i2], PS[io]
            k0, k1, k2, ko = "PS%d" % i0, "PS%d" % i1, "PS%d" % i2, "PS%d" % io
            P.op("pool", lambda e: e.memset(S[:], 0.0), writes=[K_("b_S")])
            P.op("pool", lambda e: e.tensor_copy(out=rr(S_r[:]), in_=S[:]), reads=[K_("b_S")], writes=[K_("b_Sr")])
            wins = WINS if d == 0 else [WINS[0]] + WINS[:0:-1]
            for (c0, n) in wins:
                cs = slice(c0, c0 + n)
                W = n * 64
                tk = lambda c, c0=c0: slice((c0 + c) * 64, (c0 + c + 1) * 64)
                v3 = lambda bank, W=W: bank[0:64, 0:W].rearrange("p (c w) -> p c w", w=64)
                P.op("pool", lambda e, n=n, cs=cs: e.tensor_tensor(out=dG[:, :n, :], in0=bcm(id64, n), in1=bcl(G[:, d, cs], n, 64), op=ALU.mult),
                     reads=["ident", ("b_G", d)], writes=[K_("b_dG")])
                for c in range(n):
                    P.op("pe", lambda e, c=c: e.matmul(out=B0[:, c * 64:(c + 1) * 64], lhsT=onesf[0:64, :], rhs=dG[:, c, :], start=True, stop=True),
                         reads=["onesf", K_("b_dG")], writes=[k0])
                yield
                P.op("dve", lambda e, n=n, v3=v3: e.scalar_tensor_tensor(out=X[:, :n, :], in0=v3(B0), scalar=-1.0, in1=bcm(nms[:, d, :], n), op0=ALU.mult, op1=ALU.add),
                     reads=[k0, ("b_nms", d)], writes=[K_("b_X")])
                P.op("dve", lambda e, n=n, v3=v3: e.tensor_tensor(out=XT[:, :n, :], in0=v3(B0), in1=bcm(nmt[:, d, :], n), op=ALU.add),
                     reads=[k0, ("b_nmt", d)], writes=[K_("b_XT")])
                P.op("act", lambda e, n=n, W=W: e.activation(out=fl(EGB, n), in_=B0[:, 0:W], func=AF.Exp), reads=[k0], writes=[K_("b_EGB")])
                yield
                P.op("pool", lambda e, n=n, cs=cs: e.tensor_tensor(out=X[:, :n, :], in0=X[:, :n, :], in1=bcl(G[:, d, cs], n, 64), op=ALU.add),
                     reads=[K_("b_X"), ("b_G", d)], writes=[K_("b_X")])
                P.op("pool", lambda e, n=n, cs=cs: e.tensor_tensor(out=XT[:, :n, :], in0=XT[:, :n, :], in1=bcl(nG[:, d, cs], n, 64), op=ALU.add),
                     reads=[K_("b_XT"), ("b_nG", d)], writes=[K_("b_XT")])
                for c in range(n):
                    P.op("pe", lambda e, c=c, tk=tk: e.matmul(out=B1[0:64, c * 64:(c + 1) * 64], lhsT=kT[:, tk(c)], rhs=kT[:, tk(c)], start=True, stop=True),
                         reads=["b_qkvall"], writes=[k1])
                yield
                P.op("act", lambda e, n=n: e.activation(out=X[:, :n, :], in_=X[:, :n, :], func=AF.Exp), reads=[K_("b_X")], writes=[K_("b_X")])
                P.op("act", lambda e, n=n: e.activation(out=XT[:, :n, :], in_=XT[:, :n, :], func=AF.Exp), reads=[K_("b_XT")], writes=[K_("b_XT")])
                P.op("dve", lambda e, n=n, cs=cs, v3=v3: e.tensor_tensor(out=Nm[:, :n, :], in0=v3(B1), in1=bcl(nbeta[:, d, cs], n, 64), op=ALU.mult),
                     reads=[k1, ("b_nbeta", d)], writes=[K_("b_Nm")])
                yield
                P.op("pool", lambda e, n=n: e.tensor_tensor(out=rr(Aa[0][:, :n, :]), in0=Nm[:, :n, :], in1=X[:, :n, :], op=ALU.mult),
                     reads=[K_("b_Nm"), K_("b_X")], writes=[K_("b_A0")])
                for c in range(n):
                    P.op("pe", lambda e, c=c: e.transpose(out=B0[0:64, c * 64:(c + 1) * 64], in_=Aa[0][:, c, :], identity=id64),
                         reads=[K_("b_A0"), "ident"], writes=[k0])
                yield
                P.op("act", lambda e, n=n, v3=v3: e.copy(out=rr(At[0][:, :n, :]), in_=v3(B0)), reads=[k0], writes=[K_("b_At0")])
                P.op("dve", lambda e, n=n, v3=v3: e.tensor_tensor(out=rr(Rr[0][:, :n, :]), in0=v3(B0), in1=bcm(id64, n), op=ALU.add), reads=[k0, "ident"], writes=[K_("b_R0")])
                yield
                cur = 0
                for k in range(1, 6):
                    nx = 1 - cur
                    for c in range(n):
                        P.op("pe", lambda e, c=c, cur=cur: e.matmul(out=B1[0:64, c * 64:(c + 1) * 64], lhsT=rr(At[cur][:, c, :]), rhs=rr(Aa[cur][:, c, :]), start=True, stop=True),
                             reads=[K_("b_A%d" % cur), K_("b_At%d" % cur)], writes=[k1])
                    if k < 5:
                        for c in range(n):
                            P.op("pe", lambda e, c=c, cur=cur: e.matmul(out=B0[0:64, c * 64:(c + 1) * 64], lhsT=rr(Aa[cur][:, c, :]), rhs=rr(At[cur][:, c, :]), start=True, stop=True),
                                 reads=[K_("b_A%d" % cur), K_("b_At%d" % cur)], writes=[k0])
                    yield
                    P.op("act", lambda e, nx=nx, n=n, W=W: e.copy(out=rr(fl(Aa[nx], n)), in_=B1[0:64, 0:W]), reads=[k1], writes=[K_("b_A%d" % nx)])
                    if k < 5:
                        P.op("dve", lambda e, nx=nx, n=n, W=W: e.tensor_copy(out=rr(fl(At[nx], n)), in_=B0[0:64, 0:W]), reads=[k0], writes=[K_("b_At%d" % nx)])
                    for c in range(n):
                        P.op("pe", lambda e, c=c, cur=cur, nx=nx: e.matmul(out=B2[0:64, c * 64:(c + 1) * 64], lhsT=rr(Aa[nx][:, c, :]), rhs=rr(Rr[cur][:, c, :]), start=True, stop=True),
                             reads=[K_("b_A%d" % nx), K_("b_R%d" % cur)], writes=[k2])
                    yield
                    P.op("dve", lambda e, cur=cur, nx=nx, n=n, W=W: e.tensor_tensor(out=rr(fl(Rr[nx], n)), in0=B2[0:64, 0:W], in1=fl(Rr[cur], n), op=ALU.add),
                         reads=[k2, K_("b_R%d" % cur)], writes=[K_("b_R%d" % nx)])
                    cur = nx
                R = Rr[cur]
                rk = K_("b_R%d" % cur)
                for hf in range((n + 3) // 4):
                    bank, bk = (B0, k0) if hf == 0 else (B1, k1)
                    for c in range(4 * hf, min(n, 4 * hf + 4)):
                        P.op("pe", lambda e, c=c, hf=hf, bank=bank, tk=tk: e.matmul(out=bank[0:64, (c - 4 * hf) * 128:(c - 4 * hf + 1) * 128], lhsT=kT[:, tk(c)], rhs=identb[:],
                                                                                  start=True, stop=True), reads=["b_qkvall", "identb"], writes=[bk])
                    m = min(n, 4 * hf + 4) - 4 * hf
                    P.op("act", lambda e, hf=hf, m=m, bank=bank: e.copy(out=ktok[:, 4 * hf:4 * hf + m, :].rearrange("p c w -> p (c w)"), in_=bank[0:64, 0:m * 128]),
                         reads=[bk], writes=[(K_("b_ktok"), hf)])
                yield
                P.op("pool", lambda e, n=n, cs=cs: e.tensor_tensor(out=rr(kbg[:, :n, :]), in0=ktok[:, :n, :], in1=bcl(bexpG[:, d, cs], n, 128), op=ALU.mult),
                     reads=[(K_("b_ktok"), 0), (K_("b_ktok"), 1), ("b_bexpG", d)], writes=[K_("b_kbg")])
                P.op("pool", lambda e, n=n, cs=cs: e.tensor_tensor(out=rr(ktail[:, :n, :]), in0=ktok[:, :n, :], in1=bcl(etail[:, d, cs], n, 128), op=ALU.mult),
                     reads=[(K_("b_ktok"), 0), (K_("b_ktok"), 1), ("b_etail", d)], writes=[K_("b_ktail")])
                for c in range(n):
                    P.op("pe", lambda e, c=c, tk=tk: e.matmul(out=B2[0:64, c * 64:(c + 1) * 64], lhsT=vT[:, tk(c)], rhs=identb[0:64, 0:64], start=True, stop=True),
                         reads=["b_qkvall", "identb"], writes=[k2])
                yield
                P.op("dve", lambda e, n=n, cs=cs, v3=v3: e.tensor_tensor(out=rr(vb[:, :n, :]), in0=v3(B2), in1=bcl(beta[:, d, cs], n, 64), op=ALU.mult),
                     reads=[k2, ("b_beta", d)], writes=[K_("b_vb")])
                for c in range(n):
                    P.op("pe", lambda e, c=c, R=R: e.matmul(out=B0[0:64, c * 64:(c + 1) * 64], lhsT=rr(R[:, c, :]), rhs=rr(vb[:, c, :]), start=True, stop=True),
                         reads=[rk, K_("b_vb")], writes=[k0])
                for c in range(n):
                    P.op("pe", lambda e, c=c, R=R: e.matmul(out=B1[:, c * 64:(c + 1) * 64], lhsT=rr(kbg[:, c, :]), rhs=rr(R[:, c, :]), start=True, stop=True),
                         reads=[rk, K_("b_kbg")], writes=[k1])
                yield
                P.op("act", lambda e, n=n, W=W: e.copy(out=fl(u_, n), in_=B0[0:64, 0:W]), reads=[k0], writes=[K_("b_u")])
                P.op("dve", lambda e, n=n, W=W: e.tensor_copy(out=rr(fl(wT, n)), in_=B1[:, 0:W]), reads=[k1], writes=[K_("b_wT")])
                for c in range(n):
                    P.op("pe", lambda e, c=c, tk=tk: e.matmul(out=B2[0:64, c * 64:(c + 1) * 64], lhsT=kT[:, tk(c)], rhs=qT[:, tk(c)], start=True, stop=True),
                         reads=["b_qkvall"], writes=[k2])
                yield
                P.op("dve", lambda e, n=n, W=W: e.tensor_tensor(out=rr(fl(qkT, n)), in0=B2[0:64, 0:W], in1=fl(XT, n), op=ALU.mult), reads=[k2, K_("b_XT")], writes=[K_("b_qkT")])
                P.op("pool", lambda e, n=n, W=W, c0=c0: e.tensor_tensor(out=rr(fl(qhT, n)), in0=qT[:, c0 * 64:c0 * 64 + W], in1=fl(EGB, n), op=ALU.mult),
                     reads=["b_qkvall", K_("b_EGB")], writes=[K_("b_qhT")])
                yield
                order = list(range(n)) if d == 0 else list(range(n - 1, -1, -1))
                for ci in order:
                    c = c0 + ci
                    P.op("pe", lambda e, ci=ci: e.matmul(out=B2[0:64, 0:64], lhsT=rr(wT[:, ci, :]), rhs=rr(S_r[:]), start=True, stop=True),
                         reads=[K_("b_wT"), K_("b_Sr")], writes=[k2])
                    P.op("pe", lambda e, ci=ci: e.matmul(out=BO[0:64, ci * 64:(ci + 1) * 64], lhsT=rr(S_r[:]), rhs=rr(qhT[:, ci, :]), start=True, stop=False),
                         reads=[K_("b_Sr"), K_("b_qhT")], writes=[ko])
                    yield
                    P.op("dve", lambda e, ci=ci: e.tensor_tensor(out=rr(vnew[:]), in0=u_[:, ci, :], in1=B2[0:64, 0:64], op=ALU.subtract),
                         reads=[K_("b_u"), k2], writes=[K_("b_vnew")])
                    P.op("pe", lambda e, ci=ci: e.matmul(out=BO[0:64, ci * 64:(ci + 1) * 64], lhsT=rr(vnew[:]), rhs=rr(qkT[:, ci, :]), start=False, stop=True),
                         reads=[K_("b_vnew"), K_("b_qkT")], writes=[ko])
                    P.op("pe", lambda e, ci=ci: e.matmul(out=B2[:, 64:128], lhsT=rr(ktail[:, ci, :]), rhs=rr(vnew[:]), start=True, stop=True),
                         reads=[K_("b_vnew"), K_("b_ktail")], writes=[k2])
                    yield
                    P.op("dve", lambda e, c=c: e.scalar_tensor_tensor(out=S[:], in0=S[:], scalar=EGT[:, d, c:c + 1], in1=B2[:, 64:128], op0=ALU.mult, op1=ALU.add),
                         reads=[K_("b_S"), ("b_EGT", d), k2], writes=[K_("b_S")])
                    P.op("act", lambda e: e.copy(out=rr(S_r[:]), in_=S[:]), reads=[K_("b_S")], writes=[K_("b_Sr")])
                    yield
                osl = oT[:, c0 * 64:c0 * 64 + W]
                P.op("dve", lambda e, osl=osl, W=W: e.tensor_tensor(out=osl, in0=osl, in1=BO[0:64, 0:W], op=ALU.add), reads=[ko, ("b_oT", c0)], writes=[("b_oT", c0)])
                yield

        gens = [chain(0), chain(1)]
        while gens:
            for g in list(gens):
                try:
                    next(g)
                except StopIteration:
                    gens.remove(g)
        P.dma(obT[:, :], oT[:, :], reads=[("b_oT", w[0]) for w in WINS], writes=["obT"])
        if ssb is not None:
            ssr = [sb("b_ssr%d" % i, [1, 512], F32) for i in range(2)]
            sq2 = [sb("b_sq2%d" % i, [64, 512], F32) for i in range(2)]
            for ti, (t0, n) in enumerate(Z_TILES):
                q_ = sq2[ti % 2]
                r_ = ssr[ti % 2]
                P.op("pool", lambda e, q_=q_, t0=t0, n=n: e.tensor_tensor(out=q_[:, :n], in0=oT[:, t0:t0 + n], in1=oT[:, t0:t0 + n], op=ALU.mult),
                     reads=[("b_oT", w[0]) for w in WINS], writes=["b_sq2%d" % (ti % 2)])
                P.op("pe", lambda e, q_=q_, n=n: e.matmul(out=PS[7][0:1, :n], lhsT=onesf[0:64, 0:1], rhs=q_[:, :n], start=True, stop=True),
                     reads=["b_sq2%d" % (ti % 2), "onesf"], writes=["PS7"])
                P.op("act", lambda e, r_=r_, n=n: e.copy(out=r_[0:1, :n], in_=PS[7][0:1, :n]), reads=["PS7"], writes=["b_ssr%d" % (ti % 2)])
                P.dma(ssb[0:1, t0:t0 + n], r_[0:1, :n], reads=["b_ssr%d" % (ti % 2)], writes=[("ssb", ti)])
        P.barrier()


_PROGS = {}


def _prog(name, fn):
    if name not in _PROGS:
        _PROGS[name] = fn()
    return _PROGS[name]


def _run(nc, maps):
    return run_bass_kernel_spmd(nc, maps, core_ids=list(range(NCORE))).results


def _tok_slice(full, i):
    return np.ascontiguousarray(np.concatenate([full[:, 32 * i:32 * i + 32], full[:, LC + 1024 * i:LC + 1024 * (i + 1)]], axis=1))


HALF = LC + 8 * 512


def colsel(pair, t0, n):
    return pair[0][:, t0:t0 + n] if t0 < HALF else pair[1][:, t0 - HALF:t0 - HALF + n]


def mix_row_feature(j, r):
    if r < 64:
        return 64 * j + r
    if r < 128:
        return D_A + 64 * j + (r - 64)
    return D_A + D_B + 128 * j + (r - 128)


MIX_PERM = [mix_row_feature(j, r) for j in range(NCORE) for r in range(256)]


def emit_mod(nc, P, C, cc, wmT, bmT, modv):
    with ExitStack() as st:
        sb = lambda n, s, d: st.enter_context(nc.sbuf_tensor(_uq(n), s, d))
        PS = C["PS"]
        craw = sb("m_craw", [128, 16, 2], F32)
        sc = sb("m_sc", [128, 16, 2], F32)
        wb = [sb("m_wb%d" % i, [128, 768], F32) for i in range(4)]
        bt = sb("m_bt", [128, DEPTH, 6], F32)
        for r in range(2):
            P.dma(craw[:, :, r], cc[r, :].rearrange("(k p) -> p k", p=128), writes=["m_craw%d" % r], allow_slow_non_contiguous=True)
        P.dma(bt[:], bmT[:, :, :], writes=["m_bt"])
        P.op("act", lambda e: e.activation(out=sc[:], in_=craw[:], func=AF.Silu), reads=["m_craw0", "m_craw1"], writes=["m_sc"])
        n = 0
        for l in range(DEPTH):
            for kc in range(16):
                w = wb[n % 4]
                P.dma(w[:], wmT[l, kc * 128:(kc + 1) * 128, :], writes=["m_wb%d" % (n % 4)])
                for q in range(6):
                    P.op("pe", lambda e, w=w, q=q, kc=kc: e.matmul(out=PS[q][:, 0:2], lhsT=w[:, q * 128:(q + 1) * 128], rhs=sc[:, kc, :],
                                                                 start=(kc == 0), stop=(kc == 15)),
                         reads=["m_sc", "m_wb%d" % (n % 4)], writes=["PS%d" % q])
                n += 1
            for q in range(6):
                P.op("dve", lambda e, l=l, q=q: e.tensor_scalar(out=modv[:, l, q, :], in0=PS[q][:, 0:2], scalar1=bt[:, l, q:q + 1], scalar2=None, op0=ALU.add),
                     reads=["PS%d" % q, "m_bt"], writes=["modv"])
        P.barrier()


def emit_Ap(nc, P, C, l, hT, mix_all, wo, modv, gvv, ss_loc, ss_all, xn_loc, xn_all, yT):
    PS, onesb, onesf, epsb = C["PS"], C["onesb"], C["onesf"], C["epsb"]
    first, last = (l == 0), (l == DEPTH)
    with ExitStack() as st:
        sb = lambda n, s, d: st.enter_context(nc.sbuf_tensor(_uq(n), s, d))
        if not first:
            wst = sb("ap_wst", [128, 16, 256], F32)
            wbf = sb("ap_wbf", [128, 16, 256], BF16)
            P.dma(wst[:], wo.rearrange("(k p) c -> p k c", p=128), writes=["ap_wst"])
            P.op("act", lambda e: e.copy(out=wbf[:], in_=wst[:]), reads=["ap_wst"], writes=["ap_wbf"])
            mt = [sb("ap_mt%d" % i, [128, 16, 512], BF16) for i in range(2)]
        ht = [sb("ap_ht%d" % i, [128, 512], F32) for i in range(4)]
        sq = [sb("ap_sq%d" % i, [128, 512], BF16) for i in range(2)]
        ssr = sb("ap_ssr", [1, NT], F32)
        for ti, (t0, n) in enumerate(Z_TILES):
            r = 1 if t0 < LC else 0
            if not first:
                m_ = mt[ti % 2]
                mk = "ap_mt%d" % (ti % 2)
                for (src, k0_, nk_) in ((mix_all[0], 0, 4), (mix_all[1], 4, 4), (mix_all[2], 8, 8)):
                    P.dma(m_[:, k0_:k0_ + nk_, :n], src[:, t0:t0 + n].rearrange("(k p) t -> p k t", p=128), writes=[mk])
            for fc in range(2):
                hi = (2 * ti + fc) % 4
                h, hk = ht[hi], "ap_ht%d" % hi
                P.dma(h[:, :n], hT[fc, :, t0:t0 + n], reads=[("hT", fc, ti)], writes=[hk])
                if not first:
                    for kc in range(16):
                        P.op("pe", lambda e, kc=kc, fc=fc, m_=m_, n=n: e.matmul(out=PS[fc][:, :n], lhsT=wbf[:, kc, fc * 128:(fc + 1) * 128], rhs=m_[:, kc, :n],
                                                                             start=(kc == 0), stop=(kc == 15)),
                             reads=["ap_wbf", mk], writes=["PS%d" % fc])
                    P.op("dve", lambda e, fc=fc, h=h, n=n, r=r: e.scalar_tensor_tensor(out=h[:, :n], in0=PS[fc][:, :n], scalar=modv[:, l - 1, 4 + fc, r:r + 1],
                                                                                     in1=h[:, :n], op0=ALU.mult, op1=ALU.add),
                         reads=["PS%d" % fc, hk, "modv"], writes=[hk])
                    P.dma(hT[fc, :, t0:t0 + n], h[:, :n], reads=[hk], writes=[("hT", fc, ti)])
                s_ = sq[fc]
                P.op("act", lambda e, h=h, s_=s_, n=n: e.activation(out=s_[:, :n], in_=h[:, :n], func=AF.Square), reads=[hk], writes=["ap_sq%d" % fc])
                P.op("pe", lambda e, s_=s_, n=n, fc=fc: e.matmul(out=PS[2][0:1, :n], lhsT=onesb[:, 0:1], rhs=s_[:, :n], start=(fc == 0), stop=(fc == 1)),
                     reads=["ap_sq%d" % fc, "onesb"], writes=["PS2"])
            P.op("act", lambda e, t0=t0, n=n: e.copy(out=ssr[0:1, t0:t0 + n], in_=PS[2][0:1, :n]), reads=["PS2"], writes=["ap_ssr"])
        P.dma(ss_loc[0:1, :], ssr[:], reads=["ap_ssr"], writes=["ss_loc"])
        P.allgather(ss_all[:, :], ss_loc[:, :], reads=["ss_loc"], writes=["ss_all"])
        P.barrier()
    with ExitStack() as st:
        sb = lambda n, s, d: st.enter_context(nc.sbuf_tensor(_uq(n), s, d))
        sst = [sb("ap_sst%d" % i, [8, 512], F32) for i in range(2)]
        rstd = [sb("ap_rstd%d" % i, [128, 512], F32) for i in range(2)]
        ht = [sb("ap_h2%d" % i, [128, 512], F32) for i in range(4)]
        tf = [sb("ap_tf%d" % i, [128, 512], F32) for i in range(2)]
        xo = [sb("ap_xo%d" % i, [128, 512], BF16) for i in range(2)]
        for ti, (t0, n) in enumerate(Z_TILES):
            if last and t0 < LC:
                continue
            r = 1 if t0 < LC else 0
            s_, sk = sst[ti % 2], "ap_sst%d" % (ti % 2)
            rs, rk = rstd[ti % 2], "ap_rstd%d" % (ti % 2)
            P.dma(s_[:, :n], ss_all[:, t0:t0 + n], writes=[sk])
            P.op("pe", lambda e, s_=s_, n=n: e.matmul(out=PS[3][:, :n], lhsT=onesf[0:8, :], rhs=s_[:, :n], start=True, stop=True), reads=[sk, "onesf"], writes=["PS3"])
            P.op("act", lambda e, rs=rs, n=n: e.activation(out=rs[:, :n], in_=PS[3][:, :n], func=AF.Sqrt, scale=1.0 / D, bias=epsb[:, 0:1]),
                 reads=["PS3", "epsb"], writes=[rk])
            P.op("dve", lambda e, rs=rs, n=n: e.reciprocal(out=rs[:, :n], in_=rs[:, :n]), reads=[rk], writes=[rk])
            for fc in range(2):
                hi = (2 * ti + fc) % 4
                h, hk = ht[hi], "ap_h2%d" % hi
                P.dma(h[:, :n], hT[fc, :, t0:t0 + n], writes=[hk])
                t_, tk_ = tf[fc], "ap_tf%d" % fc
                gsc = gvv[:, l, fc, r:r + 1]
                P.op("dve", lambda e, h=h, t_=t_, rs=rs, n=n, gsc=gsc: e.scalar_tensor_tensor(out=t_[:, :n], in0=h[:, :n], scalar=gsc, in1=rs[:, :n],
                                                                                           op0=ALU.mult, op1=ALU.mult),
                     reads=[hk, rk, "gvv"], writes=[tk_])
                if last:
                    P.dma(yT[fc * 128:(fc + 1) * 128, t0 - LC:t0 - LC + n], t_[:, :n], reads=[tk_], writes=[("yT", fc, ti)])
                    continue
                x_, xk = xo[fc], "ap_xo%d" % fc
                P.op("act", lambda e, t_=t_, x_=x_, n=n, fc=fc, r=r: e.activation(out=x_[:, :n], in_=t_[:, :n], func=AF.Identity, bias=modv[:, l, fc, r:r + 1]),
                     reads=[tk_, "modv"], writes=[xk])
                P.dma(colsel(xn_loc, t0, n)[fc * 128:(fc + 1) * 128, :], x_[:, :n], reads=[xk], writes=[("xn_loc", fc, ti)])
            if not last and ti == 8:
                P.allgather(xn_all[0][:, :], xn_loc[0][:, :], reads=[("xn_loc", fc, tj) for fc in range(2) for tj in range(9)], writes=["xn0_all"])
        if not last:
            P.allgather(xn_all[1][:, :], xn_loc[1][:, :], reads=[("xn_loc", fc, tj) for fc in range(2) for tj in range(9, len(Z_TILES))], writes=["xn1_all"])
        P.barrier()


def emit_MBfin(nc, P, C, ob_scr, gb_scr, ssb_all, selB, onB, l, yb_out):
    PS, onesf, epsb = C["PS"], C["onesf"], C["epsb"]
    with ExitStack() as st:
        sb = lambda n, s, d: st.enter_context(nc.sbuf_tensor(_uq(n), s, d))
        sel = sb("f_sel", [8, 64], F32)
        on = sb("f_on", [64, DEPTH], F32)
        P.dma(sel[:], selB[:, :], writes=["f_sel"])
        P.dma(on[:], onB[:, :], writes=["f_on"])
        sst = [sb("f_sst%d" % i, [8, 512], F32) for i in range(2)]
        ot = [sb("f_ot%d" % i, [64, 512], F32) for i in range(2)]
        gt = [sb("f_gt%d" % i, [64, 512], BF16) for i in range(2)]
        rs = [sb("f_rs%d" % i, [64, 512], F32) for i in range(2)]
        yo = [sb("f_yo%d" % i, [64, 512], BF16) for i in range(2)]
        for ti, (t0, n) in enumerate(Z_TILES):
            b = ti % 2
            P.dma(sst[b][:, :n], ssb_all[:, t0:t0 + n], writes=["f_sst%d" % b])
            P.dma(ot[b][:, :n], ob_scr[:, t0:t0 + n], writes=["f_ot%d" % b])
            P.dma(gt[b][:, :n], gb_scr[:, t0:t0 + n], writes=["f_gt%d" % b])
            P.op("pe", lambda e, b=b, n=n: e.matmul(out=PS[4][0:64, :n], lhsT=sel[:], rhs=sst[b][:, :n], start=True, stop=True), reads=["f_sst%d" % b, "f_sel"], writes=["PS4"])
            P.op("act", lambda e, b=b, n=n: e.activation(out=rs[b][:, :n], in_=PS[4][0:64, :n], func=AF.Sqrt, scale=1.0 / 128, bias=epsb[0:64, 0:1]),
                 reads=["PS4", "epsb"], writes=["f_rs%d" % b])
            P.op("dve", lambda e, b=b, n=n: e.reciprocal(out=rs[b][:, :n], in_=rs[b][:, :n]), reads=["f_rs%d" % b], writes=["f_rs%d" % b])
            P.op("dve", lambda e, b=b, n=n: e.scalar_tensor_tensor(out=ot[b][:, :n], in0=ot[b][:, :n], scalar=on[:, l:l + 1], in1=rs[b][:, :n], op0=ALU.mult, op1=ALU.mult),
                 reads=["f_ot%d" % b, "f_rs%d" % b, "f_on"], writes=["f_ot%d" % b])
            P.op("pool", lambda e, b=b, n=n: e.tensor_tensor(out=yo[b][:, :n], in0=ot[b][:, :n], in1=gt[b][:, :n], op=ALU.mult),
                 reads=["f_ot%d" % b, "f_gt%d" % b], writes=["f_yo%d" % b])
            P.dma(yb_out[:, t0:t0 + n], yo[b][:, :n], reads=["f_yo%d" % b], writes=[("yb", ti)])
        P.barrier()


def build_fused(upto=99, nlayers=DEPTH):
    nc = bass.Bass("TRN2", target_bir_lowering=False)
    din = lambda n, s, d: nc.dram_tensor(n, s, d, kind="ExternalInput").ap()
    scr = lambda n, s, d: nc.dram_tensor(n, s, d).ap()
    hT0 = din("hT0", [2, 128, NT], F32)
    cc = din("cc", [2, D], F32)
    wmT = din("wmT", [DEPTH, D, 768], F32)
    bmT = din("bmT", [128, DEPTH, 6], F32)
    ngT = din("ngT", [128, DEPTH + 1, 2], F32)
    selB = din("selB", [8, 64], F32)
    onB = din("onB", [64, DEPTH], F32)
    wz = [din("wz%d" % l, [D, NZ], F32) for l in range(DEPTH)]
    wo = [din("wo%d" % l, [D, 256], F32) for l in range(DEPTH)]
    pa = [din("pa%d" % l, [64, 12], F32) for l in range(DEPTH)]
    wri = [din("wri%d" % l, [64, 4, 64], F32) for l in range(DEPTH)]
    pb = [din("pb%d" % l, [128, 16], F32) for l in range(DEPTH)]
    pc = [din("pc%d" % l, [128, 2], F32) for l in range(DEPTH)]
    yT = nc.dram_tensor("yT", [256, T], F32, kind="ExternalOutput").ap()
    hT = scr("hT", [2, 128, NT], F32)
    ss_loc, ss_all = scr("ss_loc", [1, NT], F32), scr("ss_all", [8, NT], F32)
    ssb_loc, ssb_all = scr("ssb_loc", [1, NT], F32), scr("ssb_all", [8, NT], F32)
    xn_loc = (scr("xn0_loc", [256, HALF], BF16), scr("xn1_loc", [256, NT - HALF], BF16))
    xn_all = (scr("xn0_all", [D, HALF], BF16), scr("xn1_all", [D, NT - HALF], BF16))
    mixA_loc, mixA_all = scr("mixA_loc", [64, NT], BF16), scr("mixA_all", [512, NT], BF16)
    mixB_loc, mixB_all = scr("mixB_loc", [64, NT], BF16), scr("mixB_all", [512, NT], BF16)
    mixC_loc, mixC_all = scr("mixC_loc", [128, NT], BF16), scr("mixC_all", [1024, NT], BF16)
    mix_all = (mixA_all, mixB_all, mixC_all)
    zT, zv, zba = scr("zT", [896, NT], BF16), scr("zv", [NT, 128], BF16), scr("zba", [NT, 4], F32)
    ob_scr, gb_scr = scr("ob_scr", [64, NT], F32), scr("gb_scr", [64, NT], BF16)
    dbg = upto < 99 or nlayers < DEPTH
    if dbg:
        d_xn = nc.dram_tensor("d_xn", [D, NT], BF16, kind="ExternalOutput").ap()
        d_mix = nc.dram_tensor("d_mix", [D, NT], BF16, kind="ExternalOutput").ap()
        d_h = nc.dram_tensor("d_h", [2, 128, NT], F32, kind="ExternalOutput").ap()
    with ExitStack() as st:
        P = Prog(nc, st)
        C = mixer_consts(nc, st, P)
        sb = lambda n, s, d: st.enter_context(nc.sbuf_tensor(_uq(n), s, d))
        modv = sb("modv", [128, DEPTH, 6, 2], F32)
        gvv = sb("gvv", [128, DEPTH + 1, 2, 2], F32)
        ngs = sb("ngs", [128, DEPTH + 1, 2], F32)
        P.dma(ngs[:], ngT[:, :, :], writes=["ngs"])
        for fc in range(2):
            P.dma(hT[fc, :, :], hT0[fc, :, :], writes=[("hT", fc, ti) for ti in range(len(Z_TILES))])
        emit_mod(nc, P, C, cc, wmT, bmT, modv)
        for l in range(DEPTH + 1):
            for fc in range(2):
                for r in range(2):
                    if l < DEPTH:
                        P.op("dve", lambda e, l=l, fc=fc, r=r: e.tensor_scalar(out=gvv[:, l, fc, r:r + 1], in0=modv[:, l, 2 + fc, r:r + 1], scalar1=1.0,
                                                                             scalar2=ngs[:, l, fc:fc + 1], op0=ALU.add, op1=ALU.mult),
                             reads=["modv", "ngs"], writes=["gvv"])
                    else:
                        P.op("dve", lambda e, l=l, fc=fc, r=r: e.tensor_copy(out=gvv[:, l, fc, r:r + 1], in_=ngs[:, l, fc:fc + 1]), reads=["ngs"], writes=["gvv"])
        P.barrier()
        for l in range(DEPTH + 1):
            if upto < 1:
                break
            emit_Ap(nc, P, C, l, hT, mix_all, wo[l - 1] if l > 0 else None, modv, gvv, ss_loc, ss_all, xn_loc, xn_all, yT)
            if l == DEPTH or l >= nlayers or upto < 2:
                break
            emit_Z(nc, st, P, xn_all, wz[l], zT, zv, zba, PSB=C["PS"])
            if upto < 3:
                break
            emit_MB(nc, P, C, zT, zba, pb[l], ob_scr, gb_scr, ssb=ssb_loc)
            P.allgather(ssb_all[:, :], ssb_loc[:, :], writes=["ssb_all"])
            if upto < 4:
                break
            emit_MA(nc, P, C, zT, pa[l], wri[l], mixA_loc[:, :])
            P.allgather(mixA_all[:, :], mixA_loc[:, :], writes=["mixA_all"])
            if upto < 5:
                break
            emit_MC(nc, P, C, zT, zv, pc[l], mixC_loc[:, :])
            P.allgather(mixC_all[:, :], mixC_loc[:, :], writes=["mixC_all"])
            if upto < 6:
                break
            emit_MBfin(nc, P, C, ob_scr, gb_scr, ssb_all, selB, onB, l, mixB_loc[:, :])
            P.allgather(mixB_all[:, :], mixB_loc[:, :], writes=["mixB_all"])
            P.barrier()
        if dbg:
            P.barrier()
            P.dma(d_xn[:, 0:HALF], xn_all[0][:, :])
            P.dma(d_xn[:, HALF:NT], xn_all[1][:, :])
            P.dma(d_mix[0:512, :], mixA_all[:, :])
            P.dma(d_mix[512:1024, :], mixB_all[:, :])
            P.dma(d_mix[1024:2048, :], mixC_all[:, :])
            P.dma(d_h[:, :, :], hT[:, :, :])
        P.finish()
        P.emit()
    return nc


def fused_inputs(i, x, c, ctx, c_ctx, norm_g, w_mod, b_mod, w_in, d, onorm_b, w_out, final_g):
    fs = slice(256 * i, 256 * i + 256)
    hT0 = np.concatenate([ctx[0][:, fs].T, x[0][:, fs].T], axis=1).reshape(2, 128, NT)
    mcols = [p * 2048 + 256 * i + fc * 128 + f for p in range(3) for fc in range(2) for f in range(128)]
    m = {
        "hT0": np.ascontiguousarray(hT0),
        "cc": np.ascontiguousarray(np.stack([c[0], c_ctx])),
        "wmT": np.ascontiguousarray(w_mod[:, :, mcols]),
        "bmT": np.ascontiguousarray(b_mod[:, mcols].reshape(DEPTH, 6, 128).transpose(2, 0, 1)),
        "ngT": np.ascontiguousarray(np.concatenate([norm_g[:, fs], final_g[None, fs]], axis=0).reshape(DEPTH + 1, 2, 128).transpose(2, 0, 1)),
        "selB": np.ascontiguousarray(np.tile((np.arange(8) // 2 == i // 2).astype(np.float32)[:, None], (1, 64))),
        "onB": np.ascontiguousarray(onorm_b[:, 64 * (i % 2):64 * (i % 2) + 64].T),
    }
    for l in range(DEPTH):
        m["wz%d" % l] = np.ascontiguousarray(w_in[l][:, zcols(i)])
        m["wo%d" % l] = np.ascontiguousarray(w_out[l][:, fs])
        for k, v in m_params(d, l, i).items():
            m["%s%d" % (k, l)] = v
    return m


def kernel(x, c, ctx, c_ctx, norm_g, w_mod, b_mod, w_in, conv_a_w, conv_a_b, w_ra, b_ra, w_ia, b_ia,
           lam_a, conv_b_w, a_log_b, dt_bias_b, onorm_b, qn_c, kn_c, w_out, final_g):
    f32 = lambda a: np.ascontiguousarray(np.asarray(a, dtype=np.float32))
    d = {k: f32(v) for k, v in dict(conv_a_w=conv_a_w, conv_a_b=conv_a_b, w_ra=w_ra, b_ra=b_ra, w_ia=w_ia, b_ia=b_ia, lam_a=lam_a,
                                    conv_b_w=conv_b_w, a_log_b=a_log_b, dt_bias_b=dt_bias_b, qn_c=qn_c, kn_c=kn_c).items()}
    x, c, ctx, c_ctx, norm_g, w_mod, b_mod, w_in, w_out, onorm_b, final_g = map(f32, (x, c, ctx, c_ctx, norm_g, w_mod, b_mod, w_in, w_out, onorm_b, final_g))
    maps = [fused_inputs(i, x, c, ctx, c_ctx, norm_g, w_mod, b_mod, w_in, d, onorm_b, w_out, final_g) for i in range(NCORE)]
    res = _run(_prog("fused", build_fused), maps)
    yT = np.concatenate([np.asarray(r["yT"]) for r in res], axis=0)
    return np.ascontiguousarray(yT.T)[None].astype(np.float32)
```
